# Optimizing a Trainium2 kernel written in Bass

```python
import math
import jax, jax.numpy as jnp
from jax import lax
import numpy as np

D_MODEL = 2048
BATCH = 4
SEQ = 2048
DEPTH = 1
DEC_BATCH = 128
DEC_SEQ = 8
PAST_LEN = 8192
PAGE_SIZE = 128

D_CONV = D_MODEL // 2
CONV_WIDTH = 3
HEAD_DIM = 64
N_HEADS = (D_MODEL // 2) // HEAD_DIM
N_KV_HEADS = N_HEADS // 4
GROUP = N_HEADS // N_KV_HEADS
D_ATTN = N_HEADS * HEAD_DIM
D_KV = N_KV_HEADS * HEAD_DIM
WINDOW = 128
NUM_BUCKETS = 32
MAX_DISTANCE = 128
D_FF = -(-8 * D_MODEL // (3 * 256)) * 256
EPS = 1e-6
SPLIT_SIZES = (D_CONV, D_CONV, D_CONV, D_ATTN, D_KV, D_KV, D_MODEL, D_MODEL)
D_IN_PROJ = 3 * D_CONV + D_ATTN + 2 * D_KV + 2 * D_MODEL

kernel_name = 'hybrid_shortconv_swa_sink_decoder_step'


def rms_norm(x, g):
    xf = x.astype(jnp.float32)
    y = xf * lax.rsqrt(jnp.mean(xf * xf, axis=-1, keepdims=True) + EPS)
    return (y * g.astype(jnp.float32)).astype(x.dtype)


def t5_bucket(dist):
    dist = jnp.maximum(dist, 0)
    max_exact = NUM_BUCKETS // 2
    ratio = jnp.log(jnp.maximum(dist, 1).astype(jnp.float32) / max_exact) / math.log(MAX_DISTANCE / max_exact)
    large = max_exact + (ratio * (NUM_BUCKETS - max_exact)).astype(jnp.int32)
    large = jnp.minimum(large, NUM_BUCKETS - 1)
    return jnp.where(dist < max_exact, dist, large)


def window_attend(q, k, v, dist, valid, rel_bias, sinks):
    s = jnp.einsum('...qhgd,...shd->...hgqs', q, k).astype(jnp.float32) * (HEAD_DIM ** -0.5)
    bias = rel_bias.astype(jnp.float32)[t5_bucket(dist)]
    bias = jnp.moveaxis(bias, -1, 0).reshape(N_KV_HEADS, GROUP, dist.shape[0], dist.shape[1])
    s = jnp.where(valid, s + bias, -jnp.inf)
    sink = sinks.astype(jnp.float32).reshape(N_KV_HEADS, GROUP, 1, 1)
    m = jnp.maximum(jnp.max(s, axis=-1, keepdims=True), sink)
    p = jnp.exp(s - m)
    p = p / (jnp.sum(p, axis=-1, keepdims=True) + jnp.exp(sink - m))
    return jnp.einsum('...hgqs,...shd->...qhgd', p.astype(v.dtype), v)


def prompt_attention(q, k, v, rel_bias, sinks):
    b, t = q.shape[0], q.shape[1]
    nb = t // WINDOW
    qb = q.reshape(b, nb, WINDOW, N_KV_HEADS, GROUP, HEAD_DIM)
    kc = k.reshape(b, nb, WINDOW, N_KV_HEADS, HEAD_DIM)
    vc = v.reshape(b, nb, WINDOW, N_KV_HEADS, HEAD_DIM)
    kb = jnp.concatenate([jnp.concatenate([jnp.zeros_like(kc[:, :1]), kc[:, :-1]], axis=1), kc], axis=2)
    vb = jnp.concatenate([jnp.concatenate([jnp.zeros_like(vc[:, :1]), vc[:, :-1]], axis=1), vc], axis=2)
    qi = jnp.arange(WINDOW)[:, None]
    kj = jnp.arange(2 * WINDOW)[None, :]
    dist = qi + WINDOW - kj
    in_win = (dist >= 0) & (dist < WINDOW)
    key_pos = jnp.arange(nb)[:, None, None] * WINDOW - WINDOW + kj[None]
    valid = (in_win[None] & (key_pos >= 0))[:, None, None]
    o = window_attend(qb, kb, vb, dist, valid, rel_bias, sinks)
    return o.reshape(b, t, D_ATTN)


def sample_attention(q, k, v, k_win, v_win, rel_bias, sinks):
    b, t = q.shape[0], q.shape[1]
    wc = k_win.shape[1]
    kf = jnp.concatenate([k_win.astype(k.dtype), k], axis=1)
    vf = jnp.concatenate([v_win.astype(v.dtype), v], axis=1)
    dist = jnp.arange(t)[:, None] + wc - jnp.arange(wc + t)[None, :]
    valid = (dist >= 0) & (dist < WINDOW)
    o = window_attend(q.reshape(b, t, N_KV_HEADS, GROUP, HEAD_DIM), kf, vf, dist, valid, rel_bias, sinks)
    return o.reshape(b, t, D_ATTN), kf[:, -wc:], vf[:, -wc:]


def short_conv(u, prev, w):
    t = u.shape[1]
    full = jnp.concatenate([prev.astype(u.dtype), u], axis=1)
    out = w[0] * full[:, 0:t]
    for j in range(1, CONV_WIDTH):
        out = out + w[j] * full[:, j:j + t]
    return out, full[:, -(CONV_WIDTH - 1):]


def decoder_layer(x, conv_prev, k_win, v_win, rel_bias, w_in, w_conv, w_conv_out, sinks,
                  w_attn_out, w_o, g_mix, g_ffn, w_gate, w_up, w_down):
    b, t = x.shape[0], x.shape[1]
    xn = rms_norm(x, g_mix)
    z = xn @ w_in
    parts = []
    start = 0
    for size in SPLIT_SIZES:
        parts.append(z[..., start:start + size])
        start += size
    b_gate, c_gate, h_conv, q, k, v, gate_c, gate_a = parts
    u = c_gate * h_conv
    if conv_prev is None:
        conv_prev = jnp.zeros((b, CONV_WIDTH - 1, D_CONV), x.dtype)
    conv_out, conv_state = short_conv(u, conv_prev, w_conv)
    y_conv = (b_gate * conv_out) @ w_conv_out
    k = k.reshape(b, t, N_KV_HEADS, HEAD_DIM)
    v = v.reshape(b, t, N_KV_HEADS, HEAD_DIM)
    if k_win is None:
        attn = prompt_attention(q, k, v, rel_bias, sinks)
        wc = min(WINDOW, t)
        k_state, v_state = k[:, t - wc:], v[:, t - wc:]
    else:
        attn, k_state, v_state = sample_attention(q, k, v, k_win, v_win, rel_bias, sinks)
    y_attn = attn @ w_attn_out
    merged = jax.nn.sigmoid(gate_c) * y_conv + jax.nn.sigmoid(gate_a) * y_attn
    h = x + merged @ w_o
    hn = rms_norm(h, g_ffn)
    h = h + (jax.nn.silu(hn @ w_gate) * (hn @ w_up)) @ w_down
    return h, conv_state, k_state, v_state


def setup_inputs(seed: int = 0) -> dict:
    key = jax.random.key(seed)
    ks = jax.random.split(key, 20)
    f32 = jnp.float32
    wc = min(WINDOW, PAST_LEN)

    def nrm(k, shape, scale):
        return jax.random.normal(k, shape, f32) * scale

    return {
        'x_prompt': nrm(ks[0], (BATCH, SEQ, D_MODEL), 1.0),
        'x_sample': nrm(ks[1], (DEC_BATCH, DEC_SEQ, D_MODEL), 1.0),
        'cache_k': nrm(ks[2], (DEPTH, DEC_BATCH, wc, N_KV_HEADS, HEAD_DIM), 1.0),
        'cache_v': nrm(ks[3], (DEPTH, DEC_BATCH, wc, N_KV_HEADS, HEAD_DIM), 1.0),
        'state_conv': nrm(ks[4], (DEPTH, DEC_BATCH, CONV_WIDTH - 1, D_CONV), 1.0),
        'rel_bias': nrm(ks[5], (NUM_BUCKETS, N_HEADS), 0.5),
        'w_in': nrm(ks[6], (DEPTH, D_MODEL, D_IN_PROJ), D_MODEL ** -0.5),
        'w_conv': nrm(ks[7], (DEPTH, CONV_WIDTH, D_CONV), CONV_WIDTH ** -0.5),
        'w_conv_out': nrm(ks[8], (DEPTH, D_CONV, D_MODEL), D_CONV ** -0.5),
        'sinks': nrm(ks[9], (DEPTH, N_HEADS), 0.5),
        'w_attn_out': nrm(ks[10], (DEPTH, D_ATTN, D_MODEL), D_ATTN ** -0.5),
        'w_o': nrm(ks[11], (DEPTH, D_MODEL, D_MODEL), D_MODEL ** -0.5),
        'g_mix': 1.0 + nrm(ks[12], (DEPTH, D_MODEL), 0.02),
        'g_ffn': 1.0 + nrm(ks[13], (DEPTH, D_MODEL), 0.02),
        'w_gate': nrm(ks[14], (DEPTH, D_MODEL, D_FF), D_MODEL ** -0.5),
        'w_up': nrm(ks[15], (DEPTH, D_MODEL, D_FF), D_MODEL ** -0.5),
        'w_down': nrm(ks[16], (DEPTH, D_FF, D_MODEL), D_FF ** -0.5),
        'g_final': 1.0 + nrm(ks[17], (D_MODEL,), 0.02),
    }


def reference(x_prompt, x_sample, cache_k, cache_v, state_conv, rel_bias, w_in, w_conv, w_conv_out,
              sinks, w_attn_out, w_o, g_mix, g_ffn, w_gate, w_up, w_down, g_final):
    hp, hs = x_prompt, x_sample
    kp_l, vp_l, cp_l, ks_l, vs_l, cs_l = [], [], [], [], [], []
    for l in range(DEPTH):
        lw = (w_in[l], w_conv[l], w_conv_out[l], sinks[l], w_attn_out[l], w_o[l],
              g_mix[l], g_ffn[l], w_gate[l], w_up[l], w_down[l])
        hp, cp, kp, vp = decoder_layer(hp, None, None, None, rel_bias, *lw)
        hs, cs, ksm, vsm = decoder_layer(hs, state_conv[l], cache_k[l], cache_v[l], rel_bias, *lw)
        kp_l.append(kp); vp_l.append(vp); cp_l.append(cp)
        ks_l.append(ksm); vs_l.append(vsm); cs_l.append(cs)
    y_prompt = rms_norm(hp, g_final)
    y_sample = rms_norm(hs, g_final)
    return (y_prompt, y_sample, jnp.stack(kp_l), jnp.stack(vp_l), jnp.stack(cp_l),
            jnp.stack(ks_l), jnp.stack(vs_l), jnp.stack(cs_l))
```

```python
import math
from contextlib import ExitStack

import numpy as np
import concourse.bass as bass
import concourse.mybir as mybir
from concourse.bass_utils import run_bass_kernel_spmd

F32 = mybir.dt.float32
BF16 = mybir.dt.bfloat16
AF = mybir.ActivationFunctionType
ALU = mybir.AluOpType
AX = mybir.AxisListType

D = 2048
DFF = 5632
DIN = 8704
NCORES = 8
TP = 1024
TS = 128
TR = TP + TS
EXT = TR + 128
NSEQ = 16
EPS = 1e-6
NEG = -30000.0
KC = D // 128


class Reg:
    __slots__ = ("w", "rs", "excl")

    def __init__(self):
        self.w = None
        self.rs = []
        self.excl = False


class Tile:
    def __init__(self, ap, nreg, arena=None, off=0, nbytes=0):
        self.ap = ap
        self.regs = [Reg() for _ in range(nreg)]
        self.arena = arena
        self.off = off
        self.nbytes = nbytes

    def r(self, *idx):
        if not idx:
            return list(self.regs)
        return [self.regs[i] for i in idx]


class BankTile(Tile):
    def __init__(self, ap):
        Tile.__init__(self, ap, 1)
        self.regs[0].excl = True

    def r(self, *idx):
        return [self.regs[0]]


class Op:
    __slots__ = ("eng", "fn", "kind", "deps", "sig", "seq", "sem", "val", "prev", "waits", "clock", "idx")

    def __init__(self, eng, fn, kind):
        self.eng = eng
        self.fn = fn
        self.kind = kind
        self.deps = set()
        self.sig = False
        self.seq = 0
        self.sem = None
        self.val = 0
        self.prev = None
        self.waits = []
        self.clock = None
        self.idx = 0


class Arena:
    def __init__(self, size):
        self.size = size
        self.free = [(0, size)]
        self.pending = []
        self.peak = 0

    def alloc(self, n, top=False, at=None, lo=None):
        n = (n + 63) // 64 * 64
        order = list(enumerate(self.free))
        if top:
            order = order[::-1]
        for i, (s, e) in order:
            if at is not None:
                if not (s <= at and at + n <= e):
                    continue
                a = at
            elif lo is not None:
                a = max(s, lo)
                if a + n > e:
                    continue
            elif top:
                a = e - n
                if a < s:
                    continue
            else:
                a = s
                if a + n > e:
                    continue
            self.free.pop(i)
            if a > s:
                self.free.append((s, a))
            if a + n < e:
                self.free.append((a + n, e))
            self.free.sort()
            pend = []
            for (ps, pe, ops) in self.pending:
                if ps < a + n and pe > a:
                    pend.extend(ops)
            self.peak = max(self.peak, a + n)
            return a, n, pend
        raise RuntimeError(f"arena OOM: need {n} at={at} lo={lo}, free={self.free}")

    def release(self, off, n, ops):
        self.free.append((off, off + n))
        self.free.sort()
        merged = []
        for s, e in self.free:
            if merged and merged[-1][1] == s:
                merged[-1] = (merged[-1][0], e)
            else:
                merged.append((s, e))
        self.free = merged
        self.pending.append((off, off + n, ops))


class Prog:
    ENGS = ("pe", "act", "dve", "pool", "sp")
    NDSEM = {"sp": 16, "pool": 8}

    def __init__(self, nc, sb_arena_ap, sb_bytes):
        self.nc = nc
        self.ops = []
        self.out_ops = []
        self.arena = Arena(sb_bytes)
        self.sb = sb_arena_ap

    def tile(self, shape, dt, nreg=1, parts=128, top=False, at=None, lo=None):
        esz = 4 if dt == F32 else 2
        n = esz
        for s in shape:
            n *= s
        off, nb, pend = self.arena.alloc(n, top, at, lo)
        ap = self.sb[0:parts, off // 2:(off + n) // 2]
        if dt == F32:
            ap = ap.bitcast(F32)
        if len(shape) == 2:
            ap = ap.rearrange("p (a b) -> p a b", a=shape[0])
        elif len(shape) == 3:
            ap = ap.rearrange("p (a b c) -> p a b c", a=shape[0], b=shape[1])
        elif len(shape) == 4:
            ap = ap.rearrange("p (a b c d) -> p a b c d", a=shape[0], b=shape[1], c=shape[2])
        t = Tile(ap, nreg, self.arena, off, nb)
        if pend:
            for r in t.regs:
                r.rs = list(pend)
        return t

    def free(self, t):
        ops = []
        for r in t.regs:
            if r.w is not None:
                ops.append(r.w)
            ops.extend(r.rs)
        ops = list({id(o): o for o in ops}.values())
        self.arena.release(t.off, t.nbytes, ops)

    def add(self, eng, fn, reads=(), writes=(), kind="c", out=False):
        op = Op(eng, fn, kind)
        op.idx = len(self.ops)
        xr = [r for r in reads if r.excl]
        if xr:
            reads = [r for r in reads if not r.excl]
            writes = list(writes) + [r for r in xr if r not in writes]
        for r in reads:
            if r.w is not None:
                op.deps.add(r.w)
        for r in writes:
            if r.w is not None:
                op.deps.add(r.w)
            for o in r.rs:
                op.deps.add(o)
        for r in reads:
            r.rs.append(op)
        for r in writes:
            r.w = op
            r.rs = []
        op.deps.discard(op)
        self.ops.append(op)
        if out:
            self.out_ops.append(op)
        return op

    def dma(self, q, out, in_, reads=(), writes=(), out_final=False):
        return self.add(q, lambda e: e.dma_start(out=out, in_=in_), reads, writes, kind="dma", out=out_final)

    def finalize(self, stack):
        nc = self.nc
        fin = Op("sp", None, "c")
        fin.idx = len(self.ops)
        fin.deps = set(self.out_ops)
        self.ops.append(fin)
        for op in self.ops:
            for d in op.deps:
                if d.kind == "dma":
                    continue
                if d.eng == "pe" and op.eng == "pe":
                    continue
                d.sig = True
        dsem = {}
        for q in ("pool", "sp"):
            dsem[q] = [stack.enter_context(nc.semaphore(f"d_{q}{i}")) for i in range(self.NDSEM[q])]
        engsem = {e: stack.enter_context(nc.semaphore("s_" + e)) for e in ("pe", "act", "dve", "pool")}
        print("semaphores:", [x.num if hasattr(x, "num") else x for x in dsem["pool"][:2] + dsem["sp"][-2:] + list(engsem.values())])
        cnt = {e: 0 for e in self.ENGS}
        dcnt = {q: 0 for q in self.NDSEM}
        for op in self.ops:
            if op.kind == "dma":
                q = op.eng
                j = dcnt[q]
                K = self.NDSEM[q]
                op.sem = (q, j % K)
                op.val = 16 * (j // K + 1)
                if j >= K:
                    op.prev = (("d", q, j % K), 16 * (j // K))
                dcnt[q] += 1
            elif op.sig:
                cnt[op.eng] += 1
                op.seq = cnt[op.eng]
        known = {e: {} for e in self.ENGS}
        for op in self.ops:
            kn = known[op.eng]
            waits = []
            for d in sorted(op.deps, key=lambda o: -o.idx):
                if d.kind == "dma":
                    key = ("d",) + d.sem
                    val = d.val
                else:
                    if d.eng == "pe" and op.eng == "pe":
                        continue
                    key = d.eng
                    val = d.seq
                if kn.get(key, 0) >= val:
                    continue
                waits.append((key, val))
                if d.clock is not None:
                    for k, v in d.clock.items():
                        if kn.get(k, 0) < v:
                            kn[k] = v
                kn[key] = max(kn.get(key, 0), val)
            if op.prev is not None:
                key, val = op.prev
                if kn.get(key, 0) < val:
                    waits.append((key, val))
                    kn[key] = val
            op.waits = waits
            if op.kind == "dma" or op.sig:
                op.clock = dict(kn)
                if op.kind != "dma":
                    op.clock[op.eng] = op.seq

        def semof(key):
            if isinstance(key, tuple):
                return dsem[key[1]][key[2]]
            return engsem[key]

        per = {e: [o for o in self.ops if o.eng == e] for e in self.ENGS}

        def emit(ename, eng):
            for op in per[ename]:
                for key, val in op.waits:
                    eng.wait_ge(semof(key), val)
                if op.fn is None:
                    continue
                ins = op.fn(eng)
                if op.kind == "dma":
                    ins.then_inc(dsem[op.sem[0]][op.sem[1]], 16)
                elif op.sig:
                    ins.then_inc(engsem[ename], 1)

        block = stack.enter_context(nc.Block())

        @block.tensor
        def _(e):
            emit("pe", e)

        @block.scalar
        def _(e):
            emit("act", e)

        @block.vector
        def _(e):
            emit("dve", e)

        @block.gpsimd
        def _(e):
            emit("pool", e)

        @block.sync
        def _(e):
            emit("sp", e)


def _bucket_tables():
    dist = np.arange(0, 128)
    max_exact = 16
    ratio = np.log(np.maximum(dist, 1).astype(np.float32) / np.float32(max_exact)) / np.float32(math.log(128 / max_exact))
    large = max_exact + (ratio * np.float32(32 - max_exact)).astype(np.int32)
    large = np.minimum(large, 31)
    bucket = np.where(dist < max_exact, dist, large)
    T = np.zeros((33, 383), np.float32)
    for c in range(383):
        d = c - 127
        if 0 <= d < 128:
            T[bucket[d], c] = 1.0
        else:
            T[32, c] = 1.0
    Trev = np.ascontiguousarray(T[:, ::-1])
    return T, Trev


def I(method, *a, **kw):
    return lambda e: getattr(e, method)(*a, **kw)


def build(debug=(), stop_after=None):
    nc = bass.Bass("TRN2", target_bir_lowering=False)

    def din(name, shape):
        return nc.dram_tensor(name, list(shape), F32, kind="ExternalInput").ap()

    def dout(name, shape):
        return nc.dram_tensor(name, list(shape), F32, kind="ExternalOutput").ap()

    x_ext = din("x_ext", [EXT, D])
    ck = din("ck", [NSEQ, 128, 256])
    cv = din("cv", [NSEQ, 128, 256])
    sc = din("sc", [NSEQ * 2, 1024])
    w_in = din("w_in", [D, DIN])
    w_conv = din("w_conv", [3, 1024])
    w_co = din("w_co", [1024, D])
    w_ao = din("w_ao", [1024, D])
    w_o = din("w_o", [D, D])
    w_g = din("w_g", [D, DFF])
    w_u = din("w_u", [D, DFF])
    w_d = din("w_d", [DFF, D])
    g_mix = din("g_mix", [1, D])
    g_ffn = din("g_ffn", [1, D])
    g_fin = din("g_fin", [1, D])
    rel_ext = din("rel_ext", [33, 16])
    sinks = din("sinks", [1, 16])
    ttab = din("ttab", [33, 383])
    trev = din("trev", [33, 383])
    hsel_d = din("hsel", [128, 16])
    ident_d = din("ident", [128, 128])
    hmask_d = din("hmask", [128, 1])

    y_out = dout("y", [TR, D])
    knp_out = dout("knp", [128, 256])
    vnp_out = dout("vnp", [128, 256])
    cnp_out = dout("cnp", [2, 1024])
    ks_out = dout("ks", [NSEQ, 128, 256])
    vs_out = dout("vs", [NSEQ, 128, 256])
    cns_out = dout("cns", [NSEQ * 2, 1024])
    gscr = nc.dram_tensor("gscr", [16, 383], F32, kind="Internal").ap()

    SB_BYTES = 206 * 1024
    stack = ExitStack()
    with stack:
        sb_t = stack.enter_context(nc.sbuf_tensor("arena", [128, SB_BYTES // 2], BF16))
        ps_t = stack.enter_context(nc.psum_tensor("psum", [128, 8, 512], F32))
        P = Prog(nc, sb_t[:, :], SB_BYTES)
        banks = [BankTile(ps_t[:, b, :]) for b in range(8)]
        A = P.add

        def bcast_rows(ap2d, n):
            return bass.AP(ap2d.tensor, ap2d.offset, [[0, n]] + [list(x) for x in ap2d.ap[1:]])

        def dump(name, t, shape, dt=F32):
            if name not in debug:
                return
            o = nc.dram_tensor("dbg_" + name, list(shape), dt, kind="ExternalOutput").ap()
            P.dma("sp", o, t.ap, reads=t.r(), out_final=True)

        class Stop(Exception):
            pass

        def phase_end(name):
            if stop_after == name:
                raise Stop()

        def body():
            xt = [P.tile([D], F32, top=True) for _ in range(3)]
            for i in range(2):
                P.dma("sp", xt[i].ap, x_ext[i * 128:(i + 1) * 128, :], writes=xt[i].r())
            g_bc = P.tile([D], F32)
            P.dma("sp", g_bc.ap, bcast_rows(g_mix, 128), writes=g_bc.r())
            ident_f = P.tile([128], F32)
            ident_b = P.tile([128], BF16)
            P.dma("sp", ident_f.ap, ident_d, writes=ident_f.r())
            A("dve", I("tensor_copy", out=ident_b.ap, in_=ident_f.ap), ident_f.r(), ident_b.r())
            tt = P.tile([383], F32)
            tr = P.tile([383], F32)
            rel = P.tile([16], F32)
            P.dma("sp", tt.ap[0:33], ttab, writes=tt.r())
            P.dma("sp", tr.ap[0:33], trev, writes=tr.r())
            P.dma("sp", rel.ap[0:33], rel_ext, writes=rel.r())
            sink_bc = P.tile([16], F32)
            P.dma("sp", sink_bc.ap, bcast_rows(sinks, 128), writes=sink_bc.r())
            hsel = P.tile([16], F32)
            P.dma("sp", hsel.ap, hsel_d, writes=hsel.r())
            hmask = P.tile([1], F32)
            P.dma("sp", hmask.ap, hmask_d, writes=hmask.r())
            wcv = P.tile([3, 8], F32, nreg=3)
            stat = P.tile([64], F32, nreg=64)
            sink_col = P.tile([1], F32)
            tmp16 = P.tile([16], F32)
            A("dve", I("tensor_tensor", out=tmp16.ap, in0=sink_bc.ap, in1=hsel.ap, op=ALU.mult),
              sink_bc.r() + hsel.r(), tmp16.r())
            A("dve", I("tensor_reduce", out=sink_col.ap, in_=tmp16.ap, axis=AX.X, op=ALU.add), tmp16.r(), sink_col.r())
            NW = 6
            wslots = [P.tile([4096], BF16, nreg=4) for _ in range(NW)]
            wctr = [0]

            wgate = []

            def wslot():
                t = wslots[wctr[0] % NW]
                wctr[0] += 1
                return t

            def wdma(dst, src, writes):
                op = P.dma("pool", dst, src, writes=writes)
                if wgate and wctr[0] <= NW:
                    op.deps.add(wgate[0])
                return op

            def wload(src, r0, nk, c0, ncols):
                assert nk * ncols <= 4096
                t = wslot()
                dst = t.ap[:, 0:nk * ncols].rearrange("p (k n) -> p k n", k=nk)
                s = src[r0:r0 + nk * 128, c0:c0 + ncols].rearrange("(k p) n -> p k n", p=128)
                P.dma("pool", dst, s, writes=t.r())
                return [(dst[:, k, :], t.r()) for k in range(nk)]

            bias_s = P.tile([137], F32)
            R0 = (P.arena.free[0][0] + 1023) // 1024 * 1024
            assert R0 + 144 * 1024 <= SB_BYTES, R0

            def KB(x):
                return R0 + int(x * 1024)

            xnT = P.tile([KC, EXT], BF16, nreg=10, at=KB(0))

            bias_p = P.tile([16, 257], F32, nreg=128, at=KB(91))
            LB = KB(40)
            tt_b = P.tile([383], BF16, lo=LB)
            tr_b = P.tile([383], BF16, lo=LB)
            rel_h = P.tile([16], BF16, lo=LB)
            rel_l = P.tile([16], BF16, lo=LB)
            rel_t = P.tile([16], F32, lo=LB)
            A("dve", I("tensor_copy", out=tt_b.ap[0:33], in_=tt.ap[0:33]), tt.r(), tt_b.r())
            A("dve", I("tensor_copy", out=tr_b.ap[0:33], in_=tr.ap[0:33]), tr.r(), tr_b.r())
            A("dve", I("tensor_copy", out=rel_h.ap[0:33], in_=rel.ap[0:33]), rel.r(), rel_h.r())
            A("dve", I("tensor_copy", out=rel_t.ap[0:33], in_=rel_h.ap[0:33]), rel_h.r(), rel_t.r())
            A("dve", I("tensor_tensor", out=rel_t.ap[0:33], in0=rel.ap[0:33], in1=rel_t.ap[0:33], op=ALU.subtract),
              rel.r() + rel_t.r(), rel_t.r())
            A("dve", I("tensor_copy", out=rel_l.ap[0:33], in_=rel_t.ap[0:33]), rel_t.r(), rel_l.r())
            lts = [P.tile([8, 128], BF16, lo=LB) for _ in range(2)]
            for lt_, rl_ in zip(lts, (rel_h, rel_l)):
                A("dve", I("memset", lt_.ap[0:33], 0.0), (), lt_.r())
                for t_ in range(8):
                    A("dve", I("tensor_copy", out=lt_.ap[0:33, t_, :].rearrange("p (h t) -> p h t", t=8)[:, :, t_],
                               in_=rl_.ap[0:33, :]), rl_.r() + lt_.r(), lt_.r())

            def bias_chunk(r):
                def f():
                    bk = banks[4 + r % 4]
                    for sl in range(32):
                        s = r * 32 + sl
                        for j, rl_ in enumerate((rel_h, rel_l)):
                            A("pe", I("matmul", bk.ap[:, sl * 16:(sl + 1) * 16], lhsT=tt_b.ap[0:33, 255 - s:255 - s + 128],
                                      rhs=rl_.ap[0:33, :], start=(j == 0), stop=(j == 1)), tt_b.r() + rl_.r(), bk.r())
                    A("dve", I("tensor_copy", out=bias_p.ap[:, :, r * 32:r * 32 + 32],
                               in_=bk.ap.rearrange("p (s h) -> p h s", h=16)), bk.r(), bias_p.r())
                return f

            def bias_sample():
                bk = banks[4]
                for (c0, n, t0) in ((0, 128, 127), (128, 8, 255)):
                    for t_ in range(8):
                        for j, lt_ in enumerate(lts):
                            A("pe", I("matmul", bk.ap[:, c0:c0 + n], lhsT=lt_.ap[0:33, t_, :], rhs=tr_b.ap[0:33, t0 - t_:t0 - t_ + n],
                                      start=(t_ == 0 and j == 0), stop=(t_ == 7 and j == 1)), lt_.r() + tr_b.r(), bk.r())
                A("dve", I("tensor_copy", out=bias_s.ap[:, 0:128], in_=bk.ap[:, 0:128]), bk.r(), bias_s.r())
                A("dve", I("tensor_copy", out=bias_s.ap[:, 129:137], in_=bk.ap[:, 128:136]), bk.r() + bias_s.r(), bias_s.r())
                A("dve", I("tensor_copy", out=bias_s.ap[:, 128:129], in_=sink_col.ap), sink_col.r() + bias_s.r(), bias_s.r())

            gsb = P.tile([383], F32, lo=LB)
            bkg = banks[7]
            for j, rl_ in enumerate((rel_h, rel_l)):
                A("pe", I("matmul", bkg.ap[0:16, 0:383], lhsT=rl_.ap[0:33, :], rhs=tr_b.ap[0:33, 0:383], start=(j == 0), stop=(j == 1)),
                  rl_.r() + tr_b.r(), bkg.r())
            A("dve", I("tensor_copy", out=gsb.ap[0:16], in_=bkg.ap[0:16, 0:383]), bkg.r(), gsb.r())
            gst = P.dma("sp", gscr, gsb.ap[0:16], reads=gsb.r())

            def bias_rows():
                for q in range(128):
                    src = bass.AP(gscr.tensor, 127 - q, [[383 * 16, 1], [383, 16], [1, 256]])
                    op = P.dma("sp", bias_p.ap[q:q + 1, :, 0:256], src, writes=bias_p.r(q))
                    op.deps.add(gst)

            bias_work = [bias_sample]

            def norm_transpose(src_fn, ntiles, dstT, stat0, lo, extra=()):
                xn_b = [P.tile([D], BF16, lo=lo) for _ in range(2)]
                junk = P.tile([D], BF16, lo=lo)
                srcs = {}

                def N0(i):
                    srcs[i] = src_fn(i)

                def N1a(i):
                    xap, xregs = srcs[i]
                    ss = stat.ap[:, stat0 + i:stat0 + i + 1]
                    ssr = stat.r(stat0 + i)
                    A("act", I("activation", out=junk.ap, in_=xap, func=AF.Square, accum_out=ss), xregs, junk.r() + ssr)
                    A("act", I("activation", out=ss, in_=ss, func=AF.Sqrt, scale=1.0 / D, bias=EPS), ssr, ssr)

                def N1b(i):
                    ss = stat.ap[:, stat0 + i:stat0 + i + 1]
                    ssr = stat.r(stat0 + i)
                    A("dve", I("reciprocal", out=ss, in_=ss), ssr, ssr)

                def N2(i):
                    xap, xregs = srcs[i]
                    ss = stat.ap[:, stat0 + i:stat0 + i + 1]
                    ssr = stat.r(stat0 + i)
                    xb = xn_b[i % 2]
                    A("dve", I("scalar_tensor_tensor", out=xb.ap, in0=xap, scalar=ss, in1=g_bc.ap, op0=ALU.mult, op1=ALU.mult),
                      xregs + ssr + g_bc.r(), xb.r())
                    pb = (banks[0], banks[1]) if i % 2 == 0 else (banks[2], banks[3])
                    for c in range(KC):
                        bk_ = pb[c // 8]
                        o = bk_.ap.bitcast(BF16)[:, (c % 8) * 128:(c % 8 + 1) * 128]
                        A("pe", I("transpose", out=o, in_=xb.ap[:, c * 128:(c + 1) * 128], identity=ident_b.ap),
                          xb.r() + ident_b.r(), bk_.r())

                def N3(i):
                    pb = (banks[0], banks[1]) if i % 2 == 0 else (banks[2], banks[3])
                    col = i * 128
                    for hf in range(2):
                        bk_ = pb[hf]
                        src = bk_.ap.bitcast(BF16).rearrange("p (c t) -> p c t", c=8)
                        dst = dstT.ap[:, hf * 8:(hf + 1) * 8, col:col + 128]
                        if hf == 0:
                            A("act", I("activation", out=dst, in_=src, func=AF.Identity), bk_.r(), dstT.r(i))
                        else:
                            A("dve", I("tensor_copy", out=dst, in_=src), bk_.r(), dstT.r(i))

                N0(0)
                if ntiles > 1:
                    N0(1)
                N1a(0)
                N1b(0)
                for i in range(ntiles + 1):
                    if i + 2 < ntiles:
                        N0(i + 2)
                    if i + 1 < ntiles:
                        N1a(i + 1)
                    if i < ntiles:
                        N2(i)
                    if i + 1 < ntiles:
                        N1b(i + 1)
                    if i < len(extra):
                        extra[i]()
                    if i >= 1:
                        N3(i - 1)
                for j in range(ntiles + 1, len(extra)):
                    extra[j]()
                P.free(junk)
                for t_ in xn_b:
                    P.free(t_)

            xload_ops = []

            def src1(i):
                t = xt[i % 3]
                if i >= 2:
                    xload_ops.append(P.dma("sp", t.ap, x_ext[i * 128:(i + 1) * 128, :], writes=t.r()))
                return t.ap, t.r()

            norm_transpose(src1, 10, xnT, 0, KB(40), extra=bias_work)
            for t_ in [tt_b, tr_b, rel_h, rel_l, rel_t, gsb] + lts:
                P.free(t_)
            dump("bias_p", bias_p, [128, 16, 257])
            dump("bias_s", bias_s, [128, 137])
            for t_ in xt:
                P.free(t_)
            dump("xnT", xnT, [128, KC, EXT], BF16)
            phase_end("p1")

            GA = (128, 512, [1, 2, 3, 4])
            GB = (640, 512, [5, 6, 7, 8])
            GS = (1152, 128, [9])
            GROUPS = [GA, GB, GS]
            RG = [(0, 512, [0, 1, 2, 3]), (512, 512, [4, 5, 6, 7]), (1024, 128, [8])]
            bank_rr = [0]

            def next_bank():
                b = banks[bank_rr[0] % 8]
                bank_rr[0] += 1
                return b

            def fm_mm(bk, col0, wk, m0, mw, acts, n):
                regs = bk.r(*[q for q in range(4) if q * 128 < col0 + n and (q + 1) * 128 > col0])
                nk = len(wk)
                for k in range(nk):
                    A("pe", I("matmul", bk.ap[0:mw, col0:col0 + n], lhsT=wk[k][0][:, m0:m0 + mw], rhs=acts[k][0],
                              start=(k == 0), stop=(k == nk - 1)), wk[k][1] + acts[k][1], regs)
                return regs

            def xn_acts(e0, n, xr):
                return [(xnT.ap[:, k, e0:e0 + n], xnT.r(*xr)) for k in range(KC)]

            for j in range(3):
                A("sp", I("dma_start", out=wcv.ap[:, j, :], in_=w_conv[j:j + 1, :].rearrange("o (c p) -> p (o c)", p=128),
                          allow_slow_non_contiguous=True), (), wcv.r(j), kind="dma")
            aT = P.tile([8, TR], BF16, nreg=24, at=KB(40))
            qT = P.tile([8, TR], BF16, nreg=24, at=KB(58))
            kT = P.tile([4, EXT], BF16, nreg=12, at=KB(76))
            v_tm = P.tile([10, 256], BF16, nreg=10, at=KB(86))
            L2 = KB(108)
            ubuf = P.tile([TP + 2], F32, nreg=3, lo=L2)
            us = P.tile([NSEQ, 10], F32, lo=L2)
            csb = [P.tile([512], F32, lo=L2) for _ in range(2)]
            t1b = [P.tile([512], F32, lo=L2) for _ in range(2)]
            t2b = [P.tile([512], F32, lo=L2) for _ in range(2)]
            uo_p = P.tile([8, 2], F32, lo=L2)
            uo_s = P.tile([8, 32], F32, lo=L2)
            scT = P.tile([8, 32], F32, lo=L2)
            sct = P.tile([1024], F32, lo=L2)
            P.dma("sp", sct.ap[0:32], sc, writes=sct.r())
            bias_rows()
            bk = next_bank()
            for c in range(8):
                A("pe", I("transpose", out=bk.ap[:, c * 32:(c + 1) * 32], in_=sct.ap[0:32, c * 128:(c + 1) * 128],
                          identity=ident_f.ap[0:32, 0:32]), sct.r() + ident_f.r(), bk.r())
            A("dve", I("tensor_copy", out=scT.ap, in_=bk.ap[:, 0:256].rearrange("p (c j) -> p c j", c=8)), bk.r(), scT.r())
            phase_end("p2a_sc")

            def v3(ap):
                return ap.rearrange("p (s t) -> p s t", t=8)

            it = 0
            for c in range(8):
                wk = []
                for kh in range(2):
                    t = wslot()
                    dst4 = t.ap[:, 0:8 * 384].rearrange("p (k j n) -> p k j n", k=8, j=3)
                    for j in range(3):
                        src = w_in[kh * 1024:(kh + 1) * 1024, j * 1024 + c * 128:j * 1024 + (c + 1) * 128].rearrange("(k p) n -> p k n", p=128)
                        wop = P.dma("pool", dst4[:, :, j, :], src, writes=t.r(j) if j < 2 else t.r(2, 3))
                        if c == 0 and kh == 0 and j == 0:
                            wop.deps.add(xload_ops[3])
                    dst3 = t.ap[:, 0:8 * 384].rearrange("p (k n) -> p k n", k=8)
                    wk += [(dst3[:, k, :], t.r()) for k in range(8)]
                if c == 0 and stop_after == "p2a_w0x":
                    A("dve", I("tensor_copy", out=t1b[0].ap[:, 0:128], in_=wk[0][0][:, 0:128]), wk[0][1] + wk[8][1], t1b[0].r())
                    phase_end("p2a_w0x")
                if c == 0:
                    phase_end("p2a_w0")
                bh = next_bank()
                ha = xn_acts(0, 128, [0])
                bh2 = next_bank()
                fm_mm(bh, 0, wk, 128, 128, ha, 128)
                fm_mm(bh2, 128, wk, 256, 128, ha, 128)
                if c == 0 and stop_after == "p2a_mm0w":
                    A("sp", None, bh.r(), ())
                    phase_end("p2a_mm0w")
                if c == 0:
                    phase_end("p2a_mm0")
                A("dve", I("tensor_copy", out=csb[0].ap[:, 0:128], in_=bh.ap[:, 0:128]), bh.r(0), csb[0].r())
                if c == 0:
                    phase_end("p2a_act0")
                A("dve", I("tensor_tensor", out=ubuf.ap[:, 0:2], in0=csb[0].ap[:, 126:128], in1=bh2.ap[:, 254:256], op=ALU.mult),
                  csb[0].r() + bh2.r(1), ubuf.r(0))
                if c == 0:
                    phase_end("p2a_c0h")
                for gi, (e0, n, xr) in enumerate((GA, GB)):
                    bB, bC, bH = next_bank(), next_bank(), next_bank()
                    acts = xn_acts(e0, n, xr)
                    fm_mm(bB, 0, wk, 0, 128, acts, n)
                    fm_mm(bC, 0, wk, 128, 128, acts, n)
                    fm_mm(bH, 0, wk, 256, 128, acts, n)
                    cs_, t1_, t2_ = csb[it % 2], t1b[it % 2], t2b[it % 2]
                    it += 1
                    off = gi * 512
                    A("act", I("activation", out=cs_.ap, in_=bC.ap, func=AF.Identity), bC.r(), cs_.r())
                    A("dve", I("tensor_tensor", out=ubuf.ap[:, 2 + off:2 + off + 512], in0=cs_.ap, in1=bH.ap, op=ALU.mult),
                      cs_.r() + bH.r(), ubuf.r(1 + gi))
                    ur = ubuf.r(0, 1) if gi == 0 else ubuf.r(1, 2)
                    A("act", I("activation", out=t1_.ap, in_=ubuf.ap[:, off:off + 512], func=AF.Identity, scale=wcv.ap[:, 0, c:c + 1]),
                      ur + wcv.r(), t1_.r())
                    A("dve", I("scalar_tensor_tensor", out=t2_.ap, in0=ubuf.ap[:, off + 1:off + 513], scalar=wcv.ap[:, 1, c:c + 1],
                               in1=t1_.ap, op0=ALU.mult, op1=ALU.add), ur + wcv.r() + t1_.r(), t2_.r())
                    A("dve", I("scalar_tensor_tensor", out=t1_.ap, in0=ubuf.ap[:, off + 2:off + 514], scalar=wcv.ap[:, 2, c:c + 1],
                               in1=t2_.ap, op0=ALU.mult, op1=ALU.add), ur + wcv.r() + t2_.r(), t1_.r())
                    A("dve", I("tensor_tensor", out=aT.ap[:, c, off:off + 512], in0=t1_.ap, in1=bB.ap, op=ALU.mult),
                      t1_.r() + bB.r(), aT.r(c * 3 + gi))
                A("dve", I("tensor_copy", out=uo_p.ap[:, c, :], in_=ubuf.ap[:, TP:TP + 2]), ubuf.r(2), uo_p.r())
                if c == 0:
                    phase_end("p2a_c0g")
                e0, n, xr = GS
                bS = next_bank()
                acts = xn_acts(e0, n, xr)
                fm_mm(bS, 0, wk, 0, 128, acts, n)
                fm_mm(bS, 128, wk, 128, 128, acts, n)
                fm_mm(bS, 256, wk, 256, 128, acts, n)
                cs_, t1_, t2_ = csb[it % 2], t1b[it % 2], t2b[it % 2]
                it += 1
                A("act", I("activation", out=cs_.ap[:, 0:128], in_=bS.ap[:, 128:256], func=AF.Identity), bS.r(1), cs_.r())
                A("dve", I("tensor_copy", out=us.ap[:, :, 0:2], in_=scT.ap[:, c, :].rearrange("p (s j) -> p s j", j=2)),
                  scT.r(), us.r())
                A("dve", I("tensor_tensor", out=us.ap[:, :, 2:10], in0=v3(cs_.ap[:, 0:128]), in1=v3(bS.ap[:, 256:384]), op=ALU.mult),
                  cs_.r() + bS.r(2), us.r())
                A("act", I("activation", out=v3(t1_.ap[:, 0:128]), in_=us.ap[:, :, 0:8], func=AF.Identity, scale=wcv.ap[:, 0, c:c + 1]),
                  us.r() + wcv.r(), t1_.r())
                A("dve", I("scalar_tensor_tensor", out=v3(t2_.ap[:, 0:128]), in0=us.ap[:, :, 1:9], scalar=wcv.ap[:, 1, c:c + 1],
                           in1=v3(t1_.ap[:, 0:128]), op0=ALU.mult, op1=ALU.add), us.r() + wcv.r() + t1_.r(), t2_.r())
                A("dve", I("scalar_tensor_tensor", out=v3(t1_.ap[:, 0:128]), in0=us.ap[:, :, 2:10], scalar=wcv.ap[:, 2, c:c + 1],
                           in1=v3(t2_.ap[:, 0:128]), op0=ALU.mult, op1=ALU.add), us.r() + wcv.r() + t2_.r(), t1_.r())
                A("dve", I("tensor_tensor", out=aT.ap[:, c, TP:TP + 128], in0=t1_.ap[:, 0:128], in1=bS.ap[:, 0:128], op=ALU.mult),
                  t1_.r() + bS.r(0), aT.r(c * 3 + 2))
                A("dve", I("tensor_copy", out=uo_s.ap[:, c, :].rearrange("p (s j) -> p s j", j=2), in_=us.ap[:, :, 8:10]),
                  us.r(), uo_s.r())
                if c == 0:
                    phase_end("p2a_c0")
            phase_end("p2a_conv")
            for (uo, npart, dst_out) in ((uo_p, 2, cnp_out), (uo_s, 32, cns_out)):
                cst = P.tile([1024], F32, lo=L2)
                bka, bkb = next_bank(), next_bank()
                for c in range(8):
                    bk_ = bka if c < 4 else bkb
                    A("pe", I("transpose", out=bk_.ap[0:npart, (c % 4) * 128:(c % 4 + 1) * 128], in_=uo.ap[:, c, :],
                              identity=ident_f.ap), uo.r() + ident_f.r(), bk_.r())
                A("dve", I("tensor_copy", out=cst.ap[0:npart, 0:512], in_=bka.ap[0:npart, :]), bka.r(), cst.r())
                A("dve", I("tensor_copy", out=cst.ap[0:npart, 512:1024], in_=bkb.ap[0:npart, :]), bkb.r(), cst.r())
                P.dma("sp", dst_out, cst.ap[0:npart, :], reads=cst.r(), out_final=True)
                P.free(cst)
            for t_ in csb + t1b + t2b + [ubuf, us, uo_p, uo_s, scT, sct]:
                P.free(t_)
            dump("aT", aT, [128, 8, TR], BF16)
            phase_end("p2a")

            for qp in range(4):
                wk = wload(w_in, 0, KC, 3072 + qp * 256, 256)
                for ml in range(2):
                    m = qp * 2 + ml
                    for gi, (e0, n, xr) in enumerate(GROUPS):
                        bk = next_bank()
                        regs = fm_mm(bk, 0, wk, ml * 128, 128, xn_acts(e0, n, xr), n)
                        A("act", I("activation", out=qT.ap[:, m, e0 - 128:e0 - 128 + n], in_=bk.ap[:, 0:n], func=AF.Identity, scale=0.125),
                          regs, qT.r(m * 3 + gi))
            EG = [(0, 512, [0, 1, 2, 3]), (512, 512, [4, 5, 6, 7]), (1024, 256, [8, 9])]
            for kp in range(2):
                t = wslot()
                w5 = t.ap.rearrange("p (k h u d) -> p k h u d", k=KC, h=2, u=2)
                for u in range(2):
                    for hh in range(2):
                        src = w_in[:, 4096 + kp * 128 + hh * 64:4096 + kp * 128 + (hh + 1) * 64].rearrange("(k p) n -> p k n", p=128)
                        P.dma("pool", w5[:, :, hh, u, :], src, writes=t.r(u * 2 + hh))
                w3 = t.ap.rearrange("p (k n) -> p k n", k=KC)
                wk = [(w3[:, k, :], t.r()) for k in range(KC)]
                for hl in range(2):
                    kvh = kp * 2 + hl
                    for gi, (e0, n, xr) in enumerate(EG):
                        bk = next_bank()
                        regs = fm_mm(bk, 0, wk, hl * 128, 128, xn_acts(e0, n, xr), n)
                        A("dve", I("tensor_copy", out=kT.ap[:, kvh, e0:e0 + n], in_=bk.ap[:, 0:n]), regs, kT.r(kvh * 3 + gi))
            phase_end("p2b")
            wk = wload(w_in, 0, 8, 4096, 512) + wload(w_in, 1024, 8, 4096, 512)
            kvo = [P.tile([512], F32, lo=L2) for _ in range(2)]
            for i in range(10):
                bk = next_bank()
                for kc in range(KC):
                    A("pe", I("matmul", bk.ap, lhsT=xnT.ap[:, kc, i * 128:(i + 1) * 128], rhs=wk[kc][0],
                              start=(kc == 0), stop=(kc == KC - 1)), wk[kc][1] + xnT.r(i), bk.r())
                if i == 0 and stop_after == "p2c_mm0":
                    A("sp", None, bk.r(), ())
                    phase_end("p2c_mm0")
                A("act", I("activation", out=v_tm.ap[:, i, :], in_=bk.ap[:, 256:512], func=AF.Identity), bk.r(2, 3), v_tm.r(i))
                if i == 0:
                    phase_end("p2c_i0")
                if i == 7:
                    phase_end("p2c_i7")
                if i == 8:
                    ko = kvo[0]
                    A("dve", I("tensor_copy", out=ko.ap, in_=bk.ap), bk.r(), ko.r())
                    P.dma("sp", knp_out, ko.ap[:, 0:256], reads=ko.r(), out_final=True)
                    P.dma("sp", vnp_out, ko.ap[:, 256:512], reads=ko.r(), out_final=True)
                if i == 9:
                    ko = kvo[1]
                    A("dve", I("tensor_copy", out=ko.ap, in_=bk.ap), bk.r(), ko.r())
                    import os
                    for s in range(NSEQ if not os.environ.get("SKIP_SMALL") else 0):
                        P.dma("sp", ks_out[s, 120:128, :], ko.ap[s * 8:(s + 1) * 8, 0:256], reads=ko.r(), out_final=True)
                        P.dma("sp", vs_out[s, 120:128, :], ko.ap[s * 8:(s + 1) * 8, 256:512], reads=ko.r(), out_final=True)
            phase_end("p2c0")
            P.dma("sp", ks_out[:, 0:120, :], ck[:, 8:128, :], out_final=True)
            P.dma("sp", vs_out[:, 0:120, :], cv[:, 8:128, :], out_final=True)
            for t_ in kvo:
                P.free(t_)
            dump("qT", qT, [128, 8, TR], BF16)
            dump("kT", kT, [128, 4, EXT], BF16)
            dump("v_tm", v_tm, [128, 10, 256], BF16)
            phase_end("p2")

            def load_pair(pair):
                wgc = wload(w_in, 0, KC, 4608 + pair * 256, 256)
                wga = wload(w_in, 0, KC, 6656 + pair * 256, 256)
                t = wslot()
                vy = t.ap.rearrange("p (k n) -> p k n", k=KC)
                P.dma("pool", vy[:, 0:8, :], w_co[:, pair * 256:(pair + 1) * 256].rearrange("(k p) n -> p k n", p=128), writes=t.r(0, 2))
                P.dma("pool", vy[:, 8:16, :], w_ao[:, pair * 256:(pair + 1) * 256].rearrange("(k p) n -> p k n", p=128), writes=t.r(1, 3))
                wyc = [(vy[:, k, :], t.r(0, 2)) for k in range(8)]
                wya = [(vy[:, 8 + k, :], t.r(1, 3)) for k in range(8)]
                return wgc, wga, wyc, wya

            pair_w = {0: load_pair(0), 1: load_pair(1)}

            attnT = P.tile([8, TR], BF16, nreg=72, at=KB(108))
            NSB = 4
            L3 = KB(126)
            sbs = [P.tile([257], F32, lo=L3) for _ in range(NSB)]
            pbs = [P.tile([257], BF16, lo=L3) for _ in range(NSB)]
            pts = [P.tile([256], BF16, lo=L3) for _ in range(NSB)]
            atm = [P.tile([8, 128], BF16, nreg=8, lo=L3) for _ in range(2)]
            tiles = [(2 * m + p, blk) for m in range(8) for p in range(2) for blk in range(1, 9)]
            NT = len(tiles)

            def tp_(k):
                h, blk = tiles[k]
                return h, blk, h // 4, h // 2, h % 2, k % NSB, k % 2

            def stA(k):
                h, blk, kvh, m, p, sl, pslot = tp_(k)
                sbank = banks[pslot]
                ps = slice(p * 64, (p + 1) * 64)
                kc0 = (blk - 1) * 128
                g0 = kc0 // 512 if kc0 < 1024 else 2
                g1 = (kc0 + 255) // 512 if kc0 + 255 < 1024 else 2
                kregs = kT.r(*sorted({kvh * 3 + g0, kvh * 3 + g1}))
                gq = 0 if blk <= 4 else 1
                A("pe", I("matmul", sbank.ap[:, 0:256], lhsT=qT.ap[ps, m, (blk - 1) * 128:blk * 128],
                          rhs=kT.ap[ps, kvh, kc0:kc0 + 256], start=True, stop=True), qT.r(m * 3 + gq) + kregs, sbank.r())

            def stB1(k):
                h, blk, kvh, m, p, sl, pslot = tp_(k)
                sbank = banks[pslot]
                sb_ = sbs[sl]
                A("dve", I("tensor_tensor", out=sb_.ap[:, 0:256], in0=sbank.ap[:, 0:256], in1=bias_p.ap[:, h, 0:256], op=ALU.add),
                  sbank.r() + bias_p.r(), sb_.r())
                if blk == 1:
                    A("dve", I("tensor_scalar", out=sb_.ap[:, 0:128], in0=sb_.ap[:, 0:128], scalar1=hmask.ap[:, 0:1], scalar2=None,
                               op0=ALU.add), sb_.r() + hmask.r(), sb_.r())
                if k % 8 < NSB:
                    A("dve", I("tensor_copy", out=sb_.ap[:, 256:257], in_=sink_bc.ap[:, h:h + 1]), sink_bc.r() + sb_.r(), sb_.r())

            def stB2(k):
                h, blk, kvh, m, p, sl, pslot = tp_(k)
                sb_, pb_ = sbs[sl], pbs[sl]
                mx, mxr = stat.ap[:, 32 + sl:33 + sl], stat.r(32 + sl)
                rs, rsr = stat.ap[:, 40 + sl:41 + sl], stat.r(40 + sl)
                A("dve", I("tensor_reduce", out=mx, in_=sb_.ap, axis=AX.X, op=ALU.max, negate=True), sb_.r(), mxr)
                A("act", I("activation", out=pb_.ap, in_=sb_.ap, func=AF.Exp, bias=mx, accum_out=rs), sb_.r() + mxr, pb_.r() + rsr)

            def stC(k):
                h, blk, kvh, m, p, sl, pslot = tp_(k)
                pb_ = pbs[sl]
                tb = banks[2 + pslot]
                tq = tb.ap.bitcast(BF16)[:, 0:256]
                for j in range(2):
                    A("pe", I("transpose", out=tq[:, j * 128:(j + 1) * 128], in_=pb_.ap[:, j * 128:(j + 1) * 128], identity=ident_b.ap),
                      pb_.r() + ident_b.r(), tb.r())

            def stD(k):
                h, blk, kvh, m, p, sl, pslot = tp_(k)
                tb = banks[2 + pslot]
                A("act", I("activation", out=pts[sl].ap, in_=tb.ap.bitcast(BF16)[:, 0:256], func=AF.Identity), tb.r(), pts[sl].r())

            def stE(k):
                h, blk, kvh, m, p, sl, pslot = tp_(k)
                pt_ = pts[sl]
                ob = banks[4 + pslot]
                oq = ob.ap[:, 0:64]
                A("pe", I("matmul", oq, lhsT=pt_.ap[:, 0:128], rhs=v_tm.ap[:, blk - 1, kvh * 64:(kvh + 1) * 64], start=True, stop=False),
                  pt_.r() + v_tm.r(blk - 1), ob.r())
                A("pe", I("matmul", oq, lhsT=pt_.ap[:, 128:256], rhs=v_tm.ap[:, blk, kvh * 64:(kvh + 1) * 64], start=False, stop=True),
                  pt_.r() + v_tm.r(blk), ob.r())

            def stF(k):
                h, blk, kvh, m, p, sl, pslot = tp_(k)
                rs, rsr = stat.ap[:, 40 + sl:41 + sl], stat.r(40 + sl)
                ob = banks[4 + pslot]
                at_ = atm[m % 2]
                A("dve", I("reciprocal", out=rs, in_=rs), rsr, rsr)
                A("act", I("activation", out=at_.ap[:, blk - 1, p * 64:(p + 1) * 64], in_=ob.ap[:, 0:64], func=AF.Identity, scale=rs),
                  ob.r() + rsr, at_.r(blk - 1))
                if k % 16 == 15:
                    tb = banks[6 + m % 2]
                    for b_ in range(8):
                        A("pe", I("transpose", out=tb.ap.bitcast(BF16)[:, b_ * 128:(b_ + 1) * 128], in_=at_.ap[:, b_, :],
                                  identity=ident_b.ap), at_.r(b_) + ident_b.r(), tb.r())
                    A("act", I("activation", out=attnT.ap[:, m, 0:TP], in_=tb.ap.bitcast(BF16), func=AF.Identity),
                      tb.r(), attnT.r(*[m * 9 + i for i in range(8)]))

            for i in range(NT + 4):
                if i < NT:
                    stA(i)
                if 0 <= i - 1 < NT:
                    stB1(i - 1)
                if 0 <= i - 2 < NT:
                    stB2(i - 2)
                if 0 <= i - 3 < NT:
                    stC(i - 3)
                    stD(i - 3)
                if 0 <= i - 4 < NT:
                    stE(i - 4)
                    stF(i - 4)
            for t_ in sbs + pbs + pts + atm + [bias_p]:
                P.free(t_)
            dump("attnT_p", attnT, [128, 8, TR], BF16)
            phase_end("p3a")

            L3b = KB(91)
            NB4 = 4
            ckb = [P.tile([4, 128], BF16, nreg=2, lo=L3b) for _ in range(NB4)]
            vsb = [P.tile([256], BF16, lo=L3b) for _ in range(NB4)]
            kts = [P.tile([4, 137], BF16, nreg=2, lo=L3b) for _ in range(NB4)]
            st1 = [P.tile([128], F32, nreg=2, lo=L3b) for _ in range(NB4)]
            st2 = [P.tile([128], F32, nreg=2, lo=L3b) for _ in range(NB4)]
            sbq = [P.tile([137], F32, lo=L3b) for _ in range(NB4)]
            pq = [P.tile([129], BF16, lo=L3b) for _ in range(NB4)]
            pn = P.tile([NSEQ, 128], BF16, nreg=NSEQ, lo=L3b)
            ptq = [P.tile([256], BF16, lo=L3b) for _ in range(NB4)]
            asd = [P.tile([128], BF16, nreg=4, lo=L3b) for _ in range(NB4)]
            osb = [P.tile([256], F32, lo=L3b) for _ in range(NB4)]
            sstat = P.tile([NB4, 4], F32, nreg=NB4 * 4, lo=L3b)
            A("dve", I("memset", pn.ap, 0.0), (), pn.r())
            tb = banks[2]
            b5, b6, b7 = banks[5], banks[6], banks[7]
            A("dve", I("memset", b7.ap[:, 128:129], 0.0), (), b7.r())

            def hv(ap):
                return ap.rearrange("p (m q t) -> p m q t", m=8, q=2)

            def sst(b4, j):
                return sstat.ap[:, b4, j:j + 1], sstat.r(b4 * 4 + j)

            def sX1(s):
                b4 = s % NB4
                ckv = ckb[b4].ap.rearrange("p h (u d) -> p h u d", u=2)
                for u in range(2):
                    P.dma("pool", ckv[:, :, u, :], ck[s].rearrange("p (h d) -> p h d", h=4), writes=ckb[b4].r(u))
                P.dma("pool", vsb[b4].ap, cv[s], writes=vsb[b4].r())
                for kvh in range(4):
                    A("pe", I("transpose", out=tb.ap.bitcast(BF16)[:, kvh * 128:(kvh + 1) * 128], in_=ckb[b4].ap[:, kvh, :],
                              identity=ident_b.ap), ckb[b4].r() + ident_b.r(), tb.r())
                A("act", I("activation", out=kts[b4].ap[:, :, 0:128],
                           in_=tb.ap.bitcast(BF16)[:, 0:512].rearrange("p (h k) -> p h k", h=4), func=AF.Identity), tb.r(), kts[b4].r(0))
                A("dve", I("tensor_copy", out=kts[b4].ap[:, :, 129:137], in_=kT.ap[:, :, 1152 + s * 8:1152 + s * 8 + 8]),
                  kT.r(2, 5, 8, 11), kts[b4].r(1))

            def sX2(s):
                b4 = s % NB4
                def hv5(ap):
                    return ap.rearrange("p (k g q t) -> p k g q t", k=4, g=2, q=2)

                for c0, kc in ((0, slice(0, 128)), (128, slice(129, 137))):
                    npart = 128 if c0 == 0 else 8
                    for kvh in range(4):
                        for p in range(2):
                            ps = slice(p * 64, (p + 1) * 64)
                            bq = b6 if p == 0 else b5
                            A("pe", I("matmul", hv5(bq.ap[0:npart, c0:c0 + 128])[:, kvh, :, p, :], lhsT=kts[b4].ap[ps, kvh, kc],
                                      rhs=qT.ap[ps, 2 * kvh:2 * kvh + 2, TP + s * 8:TP + s * 8 + 8], start=True, stop=True),
                              kts[b4].r() + qT.r((2 * kvh) * 3 + 2, (2 * kvh + 1) * 3 + 2), bq.r())
                A("act", I("activation", out=hv(st1[b4].ap)[:, :, 0, :], in_=hv(b6.ap[:, 0:128])[:, :, 0, :], func=AF.Identity),
                  b6.r(), st1[b4].r(0))
                A("dve", I("tensor_copy", out=hv(st1[b4].ap)[:, :, 1, :], in_=hv(b5.ap[:, 0:128])[:, :, 1, :]),
                  b5.r(), st1[b4].r(1))
                A("act", I("activation", out=hv(st2[b4].ap[0:8])[:, :, 0, :], in_=hv(b6.ap[0:8, 128:256])[:, :, 0, :], func=AF.Identity),
                  b6.r(), st2[b4].r(0))
                A("dve", I("tensor_copy", out=hv(st2[b4].ap[0:8])[:, :, 1, :], in_=hv(b5.ap[0:8, 128:256])[:, :, 1, :]),
                  b5.r(), st2[b4].r(1))

            def sX3(s):
                b4 = s % NB4
                A("pe", I("transpose", out=b7.ap[:, 0:128], in_=st1[b4].ap, identity=ident_f.ap), st1[b4].r() + ident_f.r(), b7.r())
                A("pe", I("transpose", out=b7.ap[:, 129:137], in_=st2[b4].ap[0:8, :], identity=ident_f.ap[0:8, 0:8]),
                  st2[b4].r() + ident_f.r(), b7.r())
                (mx, mxr), (r1, r1r), (r2, r2r) = sst(b4, 0), sst(b4, 1), sst(b4, 2)
                A("dve", I("tensor_tensor", out=sbq[b4].ap, in0=b7.ap[:, 0:137], in1=bias_s.ap, op=ALU.add), b7.r() + bias_s.r(), sbq[b4].r())
                A("dve", I("tensor_reduce", out=mx, in_=sbq[b4].ap, axis=AX.X, op=ALU.max, negate=True), sbq[b4].r(), mxr)
                A("act", I("activation", out=pq[b4].ap, in_=sbq[b4].ap[:, 0:129], func=AF.Exp, bias=mx, accum_out=r1),
                  sbq[b4].r() + mxr, pq[b4].r() + r1r)
                A("act", I("activation", out=pn.ap[:, s, s * 8:s * 8 + 8], in_=sbq[b4].ap[:, 129:137], func=AF.Exp, bias=mx, accum_out=r2),
                  sbq[b4].r() + mxr, pn.r(s) + r2r)

            def sY1(s):
                b4 = s % NB4
                tb3, ob = banks[3], banks[4]
                tq = tb3.ap.bitcast(BF16)[:, 0:256]
                A("pe", I("transpose", out=tq[:, 0:128], in_=pq[b4].ap[:, 0:128], identity=ident_b.ap), pq[b4].r() + ident_b.r(), tb3.r())
                A("pe", I("transpose", out=tq[:, 128:256], in_=pn.ap[:, s, :], identity=ident_b.ap), pn.r(s) + ident_b.r(), tb3.r())
                A("act", I("activation", out=ptq[b4].ap, in_=tq, func=AF.Identity), tb3.r(), ptq[b4].r())
                A("pe", I("matmul", ob.ap[:, 0:256], lhsT=ptq[b4].ap[:, 0:128], rhs=vsb[b4].ap, start=True, stop=False),
                  ptq[b4].r() + vsb[b4].r(), ob.r())
                A("pe", I("matmul", ob.ap[:, 0:256], lhsT=ptq[b4].ap[:, 128:256], rhs=v_tm.ap[:, 9, :], start=False, stop=True),
                  ptq[b4].r() + v_tm.r(9), ob.r())

            def sY2(s):
                b4 = s % NB4
                (r1, r1r), (r2, r2r) = sst(b4, 1), sst(b4, 2)
                ob, tb1 = banks[4], banks[1]
                A("dve", I("tensor_tensor", out=r1, in0=r1, in1=r2, op=ALU.add), r1r + r2r, r1r)
                A("dve", I("reciprocal", out=r1, in_=r1), r1r, r1r)
                A("act", I("activation", out=osb[b4].ap, in_=ob.ap[:, 0:256], func=AF.Identity), ob.r(), osb[b4].r())
                asv = asd[b4].ap.rearrange("p (u d) -> p u d", u=2)
                for kvh in range(4):
                    pr = slice(kvh * 32, (kvh + 1) * 32)
                    src = osb[b4].ap[pr, kvh * 64:(kvh + 1) * 64]
                    srcb = bass.AP(src.tensor, src.offset, [list(src.ap[0]), [0, 2], list(src.ap[1])])
                    if kvh % 2 == 0:
                        A("dve", I("tensor_scalar", out=asv[pr, :, :], in0=srcb, scalar1=r1[pr, :], scalar2=None, op0=ALU.mult),
                          osb[b4].r() + r1r, asd[b4].r(kvh))
                    else:
                        A("act", I("activation", out=asv[pr, :, :], in_=srcb, func=AF.Identity, scale=r1[pr, :]),
                          osb[b4].r() + r1r, asd[b4].r(kvh))
                tq2 = tb1.ap.bitcast(BF16)[:, 0:128]
                A("pe", I("transpose", out=tq2, in_=asd[b4].ap, identity=ident_b.ap), asd[b4].r() + ident_b.r(), tb1.r())
                for p in range(2):
                    pr = slice(p * 64, (p + 1) * 64)
                    src = tq2.rearrange("p (m q t) -> p m q t", m=8, q=2)[pr, :, p, :]
                    A("dve", I("tensor_copy", out=attnT.ap[pr, :, TP + s * 8:TP + s * 8 + 8], in_=src),
                      tb1.r() + attnT.r(*[m * 9 + 8 for m in range(8)]), attnT.r(*[m * 9 + 8 for m in range(8)]))

            stages = [sX1, sX2, sX3, sY1, sY2]
            for i in range(NSEQ + len(stages) - 1):
                for d_ in reversed(range(len(stages))):
                    if 0 <= i - d_ < NSEQ:
                        stages[d_](i - d_)
            for t_ in ckb + vsb + kts + st1 + st2 + sbq + pq + [pn, sstat] + ptq + asd + osb:
                P.free(t_)
            P.free(qT)
            P.free(kT)
            P.free(v_tm)
            dump("attnT", attnT, [128, 8, TR], BF16)
            phase_end("p3")

            mergedT = P.tile([KC, TR], BF16, nreg=48, at=KB(72))
            sg = [P.tile([512], F32, lo=KB(126)) for _ in range(4)]
            mm = [P.tile([512], F32, lo=KB(126)) for _ in range(4)]
            it = 0
            for pair in range(8):
                wgc, wga, wyc, wya = pair_w[pair] if pair in pair_w else load_pair(pair)
                for ml in range(2):
                    c = pair * 2 + ml
                    for gi, (r0, n, tr_) in enumerate(RG):
                        e0 = r0 + 128
                        xr = [x + 1 for x in tr_]
                        bs = [banks[(it % 2) * 4 + j] for j in range(4)]
                        rg0 = fm_mm(bs[0], 0, wgc, ml * 128, 128, xn_acts(e0, n, xr), n)
                        rg1 = fm_mm(bs[1], 0, wga, ml * 128, 128, xn_acts(e0, n, xr), n)
                        rg2 = fm_mm(bs[2], 0, wyc, ml * 128, 128, [(aT.ap[:, k, r0:r0 + n], aT.r(k * 3 + gi)) for k in range(8)], n)
                        rg3 = fm_mm(bs[3], 0, wya, ml * 128, 128,
                                    [(attnT.ap[:, k, r0:r0 + n], attnT.r(*[k * 9 + x for x in tr_])) for k in range(8)], n)
                        s0, s1, m0, m1 = sg[(it % 2) * 2], sg[(it % 2) * 2 + 1], mm[(it % 2) * 2], mm[(it % 2) * 2 + 1]
                        it += 1
                        A("act", I("activation", out=s0.ap[:, 0:n], in_=bs[0].ap[:, 0:n], func=AF.Sigmoid), rg0, s0.r())
                        A("act", I("activation", out=s1.ap[:, 0:n], in_=bs[1].ap[:, 0:n], func=AF.Sigmoid), rg1, s1.r())
                        A("dve", I("tensor_tensor", out=m0.ap[:, 0:n], in0=s0.ap[:, 0:n], in1=bs[2].ap[:, 0:n], op=ALU.mult), s0.r() + rg2, m0.r())
                        A("dve", I("tensor_tensor", out=m1.ap[:, 0:n], in0=s1.ap[:, 0:n], in1=bs[3].ap[:, 0:n], op=ALU.mult), s1.r() + rg3, m1.r())
                        A("dve", I("tensor_tensor", out=mergedT.ap[:, c, r0:r0 + n], in0=m0.ap[:, 0:n], in1=m1.ap[:, 0:n], op=ALU.add),
                          m0.r() + m1.r(), mergedT.r(c * 3 + gi))
            for t_ in sg + mm:
                P.free(t_)
            P.free(aT)
            P.free(attnT)
            P.free(xnT)
            dump("mergedT", mergedT, [128, KC, TR], BF16)
            phase_end("p4a")

            h_acc = P.tile([9, D], F32, nreg=36, at=KB(0))
            for i in range(9):
                P.dma("sp", h_acc.ap[:, i, :], x_ext[128 + i * 128:128 + (i + 1) * 128, :], writes=h_acc.r(*[i * 4 + n for n in range(4)]))
            for nb in range(4):
                wk = wload(w_o, 0, 8, nb * 512, 512) + wload(w_o, 1024, 8, nb * 512, 512)
                for i in range(9):
                    bk = next_bank()
                    gi = 0 if i < 4 else (1 if i < 8 else 2)
                    for kc in range(KC):
                        A("pe", I("matmul", bk.ap, lhsT=mergedT.ap[:, kc, i * 128:(i + 1) * 128], rhs=wk[kc][0],
                                  start=(kc == 0), stop=(kc == KC - 1)), wk[kc][1] + mergedT.r(kc * 3 + gi), bk.r())
                    hs = h_acc.ap[:, i, nb * 512:(nb + 1) * 512]
                    A("dve", I("tensor_tensor", out=hs, in0=hs, in1=bk.ap, op=ALU.add), bk.r() + h_acc.r(i * 4 + nb), h_acc.r(i * 4 + nb))
            P.free(mergedT)
            dump("h1", h_acc, [128, 9, D])
            phase_end("p4b")

            P.dma("sp", g_bc.ap, bcast_rows(g_ffn, 128), writes=g_bc.r())
            hnT = P.tile([KC, TR], BF16, nreg=9, at=KB(72))

            def src2(i):
                return h_acc.ap[:, i, :], h_acc.r(*[i * 4 + n for n in range(4)])

            norm_transpose(src2, 9, hnT, 16, KB(108))
            dump("hnT", hnT, [128, KC, TR], BF16)
            guT = [P.tile([4, TR], BF16, nreg=12, lo=KB(108)) for _ in range(2)]
            sgl = [P.tile([512], F32, lo=KB(108)) for _ in range(2)]
            it = 0
            for j in range(DFF // 512):
                gu = guT[j % 2]
                for hp in range(2):
                    wg = wload(w_g, 0, KC, j * 512 + hp * 256, 256)
                    wu = wload(w_u, 0, KC, j * 512 + hp * 256, 256)
                    for ml in range(2):
                        mi = hp * 2 + ml
                        for gi, (r0, n, tr_) in enumerate(RG):
                            bg, bu = banks[(it % 3) * 2], banks[(it % 3) * 2 + 1]
                            acts = [(hnT.ap[:, k, r0:r0 + n], hnT.r(*tr_)) for k in range(KC)]
                            rg = fm_mm(bg, 0, wg, ml * 128, 128, acts, n)
                            ru = fm_mm(bu, 0, wu, ml * 128, 128, acts, n)
                            s_ = sgl[it % 2]
                            it += 1
                            A("act", I("activation", out=s_.ap[:, 0:n], in_=bg.ap[:, 0:n], func=AF.Silu), rg, s_.r())
                            A("dve", I("tensor_tensor", out=gu.ap[:, mi, r0:r0 + n], in0=s_.ap[:, 0:n], in1=bu.ap[:, 0:n], op=ALU.mult),
                              s_.r() + ru, gu.r(mi * 3 + gi))
                for nb in range(4):
                    wd = wload(w_d, j * 512, 4, nb * 512, 512)
                    for i in range(9):
                        bk = banks[6 + (i + nb) % 2]
                        gi = 0 if i < 4 else (1 if i < 8 else 2)
                        for kc in range(4):
                            A("pe", I("matmul", bk.ap, lhsT=gu.ap[:, kc, i * 128:(i + 1) * 128], rhs=wd[kc][0],
                                      start=(kc == 0), stop=(kc == 3)), wd[kc][1] + gu.r(kc * 3 + gi), bk.r())
                        hs = h_acc.ap[:, i, nb * 512:(nb + 1) * 512]
                        A("dve", I("tensor_tensor", out=hs, in0=hs, in1=bk.ap, op=ALU.add), bk.r() + h_acc.r(i * 4 + nb), h_acc.r(i * 4 + nb))
            for t_ in guT + sgl:
                P.free(t_)
            P.free(hnT)
            dump("h2", h_acc, [128, 9, D])
            phase_end("p5")

            P.dma("sp", g_bc.ap, bcast_rows(g_fin, 128), writes=g_bc.r())
            yt = [P.tile([D], F32, lo=KB(72)) for _ in range(2)]
            junk = P.tile([D], BF16, lo=KB(72))

            def F1(i):
                hap = h_acc.ap[:, i, :]
                hr = h_acc.r(*[i * 4 + n for n in range(4)])
                ss = stat.ap[:, 54 + i:55 + i]
                ssr = stat.r(54 + i)
                A("act", I("activation", out=junk.ap, in_=hap, func=AF.Square, accum_out=ss), hr, junk.r() + ssr)
                A("act", I("activation", out=ss, in_=ss, func=AF.Sqrt, scale=1.0 / D, bias=EPS), ssr, ssr)
                A("dve", I("reciprocal", out=ss, in_=ss), ssr, ssr)

            def F2(i):
                hap = h_acc.ap[:, i, :]
                hr = h_acc.r(*[i * 4 + n for n in range(4)])
                ss = stat.ap[:, 54 + i:55 + i]
                ssr = stat.r(54 + i)
                y_ = yt[i % 2]
                A("dve", I("scalar_tensor_tensor", out=y_.ap, in0=hap, scalar=ss, in1=g_bc.ap, op0=ALU.mult, op1=ALU.mult),
                  hr + ssr + g_bc.r(), y_.r())
                P.dma("sp", y_out[i * 128:(i + 1) * 128, :], y_.ap, reads=y_.r(), out_final=True)

            F1(0)
            for i in range(9):
                if i + 1 < 9:
                    F1(i + 1)
                F2(i)

        try:
            body()
        except Stop:
            pass
        P.finalize(stack)
        print("sbuf peak bytes/partition:", P.arena.peak, "ops:", len(P.ops), flush=True)
    return nc


_NC_CACHE = {}


def host_inputs(x_prompt, x_sample, cache_k, cache_v, state_conv, rel_bias, w_in, w_conv, w_conv_out,
                sinks, w_attn_out, w_o, g_mix, g_ffn, w_gate, w_up, w_down, g_final, cores=range(NCORES)):
    f = lambda a: np.ascontiguousarray(np.asarray(a, dtype=np.float32))
    T, Trev = _bucket_tables()
    rel_ext = np.concatenate([f(rel_bias), np.full((1, 16), NEG, np.float32)], axis=0)
    hsel = np.zeros((128, 16), np.float32)
    hsel[np.arange(128), np.arange(128) // 8] = 1.0
    shared = {
        "w_in": f(w_in[0]), "w_conv": f(w_conv[0]), "w_co": f(w_conv_out[0]), "w_ao": f(w_attn_out[0]), "w_o": f(w_o[0]),
        "w_g": f(w_gate[0]), "w_u": f(w_up[0]), "w_d": f(w_down[0]),
        "g_mix": f(g_mix[0]).reshape(1, D), "g_ffn": f(g_ffn[0]).reshape(1, D), "g_fin": f(g_final).reshape(1, D),
        "rel_ext": rel_ext, "sinks": f(sinks[0]).reshape(1, 16), "ttab": T, "trev": Trev, "hsel": hsel,
        "ident": np.eye(128, dtype=np.float32),
    }
    xp = np.asarray(x_prompt, dtype=np.float32)
    xs = np.asarray(x_sample, dtype=np.float32)
    maps = []
    for c in cores:
        b, half = c // 2, c % 2
        x_ext = np.zeros((EXT, D), np.float32)
        if half == 1:
            x_ext[0:128] = xp[b, TP - 128:TP]
        x_ext[128:128 + TP] = xp[b, half * TP:(half + 1) * TP]
        x_ext[128 + TP:] = xs[c * NSEQ:(c + 1) * NSEQ].reshape(TS, D)
        m = dict(shared)
        m["x_ext"] = x_ext
        m["ck"] = f(cache_k[0, c * NSEQ:(c + 1) * NSEQ]).reshape(NSEQ, 128, 256)
        m["cv"] = f(cache_v[0, c * NSEQ:(c + 1) * NSEQ]).reshape(NSEQ, 128, 256)
        m["sc"] = f(state_conv[0, c * NSEQ:(c + 1) * NSEQ]).reshape(NSEQ * 2, 1024)
        m["hmask"] = np.full((128, 1), 0.0 if half == 1 else NEG, np.float32)
        maps.append(m)
    return maps


def kernel(x_prompt, x_sample, cache_k, cache_v, state_conv, rel_bias, w_in, w_conv, w_conv_out,
           sinks, w_attn_out, w_o, g_mix, g_ffn, w_gate, w_up, w_down, g_final):
    if "nc" not in _NC_CACHE:
        _NC_CACHE["nc"] = build()
    nc = _NC_CACHE["nc"]
    maps = host_inputs(x_prompt, x_sample, cache_k, cache_v, state_conv, rel_bias, w_in, w_conv, w_conv_out,
                       sinks, w_attn_out, w_o, g_mix, g_ffn, w_gate, w_up, w_down, g_final)
    res = run_bass_kernel_spmd(nc, maps, core_ids=list(range(NCORES)))
    R = res.results
    B = 4
    y_prompt = np.zeros((B, 2048, D), np.float32)
    y_sample = np.zeros((128, 8, D), np.float32)
    nkp = np.zeros((1, B, 128, 4, 64), np.float32)
    nvp = np.zeros((1, B, 128, 4, 64), np.float32)
    ncp = np.zeros((1, B, 2, 1024), np.float32)
    nks = np.zeros((1, 128, 128, 4, 64), np.float32)
    nvs = np.zeros((1, 128, 128, 4, 64), np.float32)
    ncs = np.zeros((1, 128, 2, 1024), np.float32)
    for c in range(NCORES):
        b, half = c // 2, c % 2
        r = R[c]
        y = np.asarray(r["y"])
        y_prompt[b, half * TP:(half + 1) * TP] = y[0:TP]
        y_sample[c * NSEQ:(c + 1) * NSEQ] = y[TP:].reshape(NSEQ, 8, D)
        if half == 1:
            nkp[0, b] = np.asarray(r["knp"]).reshape(128, 4, 64)
            nvp[0, b] = np.asarray(r["vnp"]).reshape(128, 4, 64)
            ncp[0, b] = np.asarray(r["cnp"])
        nks[0, c * NSEQ:(c + 1) * NSEQ] = np.asarray(r["ks"]).reshape(NSEQ, 128, 4, 64)
        nvs[0, c * NSEQ:(c + 1) * NSEQ] = np.asarray(r["vs"]).reshape(NSEQ, 128, 4, 64)
        ncs[0, c * NSEQ:(c + 1) * NSEQ] = np.asarray(r["cns"]).reshape(NSEQ, 2, 1024)
    return (y_prompt, y_sample, nkp, nvp, ncp, nks, nvs, ncs)
```

```python
import math
from contextlib import ExitStack

import numpy as np
import concourse.bass as bass
import concourse.mybir as mybir
from concourse.bass_utils import run_bass_kernel_spmd

F32 = mybir.dt.float32
BF16 = mybir.dt.bfloat16
AF = mybir.ActivationFunctionType
ALU = mybir.AluOpType
AX = mybir.AxisListType

D = 2048
DFF = 5632
DIN = 8704
NCORES = 8
TP = 1024
TS = 128
TR = TP + TS
EXT = TR + 128
NSEQ = 16
EPS = 1e-6
NEG = -30000.0
KC = D // 128


class Reg:
    __slots__ = ("w", "rs", "excl")

    def __init__(self):
        self.w = None
        self.rs = []
        self.excl = False


class Tile:
    def __init__(self, ap, nreg, arena=None, off=0, nbytes=0):
        self.ap = ap
        self.regs = [Reg() for _ in range(nreg)]
        self.arena = arena
        self.off = off
        self.nbytes = nbytes

    def r(self, *idx):
        if not idx:
            return list(self.regs)
        return [self.regs[i] for i in idx]


class BankTile(Tile):
    def __init__(self, ap):
        Tile.__init__(self, ap, 1)
        self.regs[0].excl = True

    def r(self, *idx):
        return [self.regs[0]]


class Op:
    __slots__ = ("eng", "fn", "kind", "deps", "sig", "seq", "sem", "val", "prev", "waits", "clock", "idx")

    def __init__(self, eng, fn, kind):
        self.eng = eng
        self.fn = fn
        self.kind = kind
        self.deps = set()
        self.sig = False
        self.seq = 0
        self.sem = None
        self.val = 0
        self.prev = None
        self.waits = []
        self.clock = None
        self.idx = 0


class Arena:
    def __init__(self, size):
        self.size = size
        self.free = [(0, size)]
        self.pending = []
        self.peak = 0

    def alloc(self, n, top=False, at=None, lo=None):
        n = (n + 63) // 64 * 64
        order = list(enumerate(self.free))
        if top:
            order = order[::-1]
        for i, (s, e) in order:
            if at is not None:
                if not (s <= at and at + n <= e):
                    continue
                a = at
            elif lo is not None:
                a = max(s, lo)
                if a + n > e:
                    continue
            elif top:
                a = e - n
                if a < s:
                    continue
            else:
                a = s
                if a + n > e:
                    continue
            self.free.pop(i)
            if a > s:
                self.free.append((s, a))
            if a + n < e:
                self.free.append((a + n, e))
            self.free.sort()
            pend = []
            for (ps, pe, ops) in self.pending:
                if ps < a + n and pe > a:
                    pend.extend(ops)
            self.peak = max(self.peak, a + n)
            return a, n, pend
        raise RuntimeError(f"arena OOM: need {n} at={at} lo={lo}, free={self.free}")

    def release(self, off, n, ops):
        self.free.append((off, off + n))
        self.free.sort()
        merged = []
        for s, e in self.free:
            if merged and merged[-1][1] == s:
                merged[-1] = (merged[-1][0], e)
            else:
                merged.append((s, e))
        self.free = merged
        self.pending.append((off, off + n, ops))


class Prog:
    ENGS = ("pe", "act", "dve", "pool", "sp")
    NDSEM = {"sp": 16, "pool": 8}

    def __init__(self, nc, sb_arena_ap, sb_bytes):
        self.nc = nc
        self.ops = []
        self.out_ops = []
        self.arena = Arena(sb_bytes)
        self.sb = sb_arena_ap

    def tile(self, shape, dt, nreg=1, parts=128, top=False, at=None, lo=None):
        esz = 4 if dt == F32 else 2
        n = esz
        for s in shape:
            n *= s
        off, nb, pend = self.arena.alloc(n, top, at, lo)
        ap = self.sb[0:parts, off // 2:(off + n) // 2]
        if dt == F32:
            ap = ap.bitcast(F32)
        if len(shape) == 2:
            ap = ap.rearrange("p (a b) -> p a b", a=shape[0])
        elif len(shape) == 3:
            ap = ap.rearrange("p (a b c) -> p a b c", a=shape[0], b=shape[1])
        elif len(shape) == 4:
            ap = ap.rearrange("p (a b c d) -> p a b c d", a=shape[0], b=shape[1], c=shape[2])
        t = Tile(ap, nreg, self.arena, off, nb)
        if pend:
            for r in t.regs:
                r.rs = list(pend)
        return t

    def free(self, t):
        ops = []
        for r in t.regs:
            if r.w is not None:
                ops.append(r.w)
            ops.extend(r.rs)
        ops = list({id(o): o for o in ops}.values())
        self.arena.release(t.off, t.nbytes, ops)

    def add(self, eng, fn, reads=(), writes=(), kind="c", out=False):
        op = Op(eng, fn, kind)
        op.idx = len(self.ops)
        xr = [r for r in reads if r.excl]
        if xr:
            reads = [r for r in reads if not r.excl]
            writes = list(writes) + [r for r in xr if r not in writes]
        for r in reads:
            if r.w is not None:
                op.deps.add(r.w)
        for r in writes:
            if r.w is not None:
                op.deps.add(r.w)
            for o in r.rs:
                op.deps.add(o)
        for r in reads:
            r.rs.append(op)
        for r in writes:
            r.w = op
            r.rs = []
        op.deps.discard(op)
        self.ops.append(op)
        if out:
            self.out_ops.append(op)
        return op

    def dma(self, q, out, in_, reads=(), writes=(), out_final=False):
        return self.add(q, lambda e: e.dma_start(out=out, in_=in_), reads, writes, kind="dma", out=out_final)

    def finalize(self, stack):
        nc = self.nc
        fin = Op("sp", None, "c")
        fin.idx = len(self.ops)
        fin.deps = set(self.out_ops)
        self.ops.append(fin)
        for op in self.ops:
            for d in op.deps:
                if d.kind == "dma":
                    continue
                if d.eng == "pe" and op.eng == "pe":
                    continue
                d.sig = True
        dsem = {}
        for q in ("pool", "sp"):
            dsem[q] = [stack.enter_context(nc.semaphore(f"d_{q}{i}")) for i in range(self.NDSEM[q])]
        engsem = {e: stack.enter_context(nc.semaphore("s_" + e)) for e in ("pe", "act", "dve", "pool")}
        print("semaphores:", [x.num if hasattr(x, "num") else x for x in dsem["pool"][:2] + dsem["sp"][-2:] + list(engsem.values())])
        cnt = {e: 0 for e in self.ENGS}
        dcnt = {q: 0 for q in self.NDSEM}
        for op in self.ops:
            if op.kind == "dma":
                q = op.eng
                j = dcnt[q]
                K = self.NDSEM[q]
                op.sem = (q, j % K)
                op.val = 16 * (j // K + 1)
                if j >= K:
                    op.prev = (("d", q, j % K), 16 * (j // K))
                dcnt[q] += 1
            elif op.sig:
                cnt[op.eng] += 1
                op.seq = cnt[op.eng]
        known = {e: {} for e in self.ENGS}
        for op in self.ops:
            kn = known[op.eng]
            waits = []
            for d in sorted(op.deps, key=lambda o: -o.idx):
                if d.kind == "dma":
                    key = ("d",) + d.sem
                    val = d.val
                else:
                    if d.eng == "pe" and op.eng == "pe":
                        continue
                    key = d.eng
                    val = d.seq
                if kn.get(key, 0) >= val:
                    continue
                waits.append((key, val))
                if d.clock is not None:
                    for k, v in d.clock.items():
                        if kn.get(k, 0) < v:
                            kn[k] = v
                kn[key] = max(kn.get(key, 0), val)
            if op.prev is not None:
                key, val = op.prev
                if kn.get(key, 0) < val:
                    waits.append((key, val))
                    kn[key] = val
            op.waits = waits
            if op.kind == "dma" or op.sig:
                op.clock = dict(kn)
                if op.kind != "dma":
                    op.clock[op.eng] = op.seq

        def semof(key):
            if isinstance(key, tuple):
                return dsem[key[1]][key[2]]
            return engsem[key]

        per = {e: [o for o in self.ops if o.eng == e] for e in self.ENGS}

        def emit(ename, eng):
            for op in per[ename]:
                for key, val in op.waits:
                    eng.wait_ge(semof(key), val)
                if op.fn is None:
                    continue
                ins = op.fn(eng)
                if op.kind == "dma":
                    ins.then_inc(dsem[op.sem[0]][op.sem[1]], 16)
                elif op.sig:
                    ins.then_inc(engsem[ename], 1)

        block = stack.enter_context(nc.Block())

        @block.tensor
        def _(e):
            emit("pe", e)

        @block.scalar
        def _(e):
            emit("act", e)

        @block.vector
        def _(e):
            emit("dve", e)

        @block.gpsimd
        def _(e):
            emit("pool", e)

        @block.sync
        def _(e):
            emit("sp", e)


def _bucket_tables():
    dist = np.arange(0, 128)
    max_exact = 16
    ratio = np.log(np.maximum(dist, 1).astype(np.float32) / np.float32(max_exact)) / np.float32(math.log(128 / max_exact))
    large = max_exact + (ratio * np.float32(32 - max_exact)).astype(np.int32)
    large = np.minimum(large, 31)
    bucket = np.where(dist < max_exact, dist, large)
    T = np.zeros((33, 383), np.float32)
    for c in range(383):
        d = c - 127
        if 0 <= d < 128:
            T[bucket[d], c] = 1.0
        else:
            T[32, c] = 1.0
    Trev = np.ascontiguousarray(T[:, ::-1])
    return T, Trev


def I(method, *a, **kw):
    return lambda e: getattr(e, method)(*a, **kw)


def build(debug=(), stop_after=None):
    nc = bass.Bass("TRN2", target_bir_lowering=False)

    def din(name, shape):
        return nc.dram_tensor(name, list(shape), F32, kind="ExternalInput").ap()

    def dout(name, shape):
        return nc.dram_tensor(name, list(shape), F32, kind="ExternalOutput").ap()

    x_ext = din("x_ext", [EXT, D])
    ck = din("ck", [NSEQ, 128, 256])
    cv = din("cv", [NSEQ, 128, 256])
    sc = din("sc", [NSEQ * 2, 1024])
    w_in = din("w_in", [D, DIN])
    w_conv = din("w_conv", [3, 1024])
    w_co = din("w_co", [1024, D])
    w_ao = din("w_ao", [1024, D])
    w_o = din("w_o", [D, D])
    w_g = din("w_g", [D, DFF])
    w_u = din("w_u", [D, DFF])
    w_d = din("w_d", [DFF, D])
    g_mix = din("g_mix", [1, D])
    g_ffn = din("g_ffn", [1, D])
    g_fin = din("g_fin", [1, D])
    rel_ext = din("rel_ext", [33, 16])
    sinks = din("sinks", [1, 16])
    ttab = din("ttab", [33, 383])
    trev = din("trev", [33, 383])
    hsel_d = din("hsel", [128, 16])
    ident_d = din("ident", [128, 128])
    hmask_d = din("hmask", [128, 1])

    y_out = dout("y", [TR, D])
    knp_out = dout("knp", [128, 256])
    vnp_out = dout("vnp", [128, 256])
    cnp_out = dout("cnp", [2, 1024])
    ks_out = dout("ks", [NSEQ, 128, 256])
    vs_out = dout("vs", [NSEQ, 128, 256])
    cns_out = dout("cns", [NSEQ * 2, 1024])
    gscr = nc.dram_tensor("gscr", [16, 383], F32, kind="Internal").ap()

    SB_BYTES = 206 * 1024
    stack = ExitStack()
    with stack:
        sb_t = stack.enter_context(nc.sbuf_tensor("arena", [128, SB_BYTES // 2], BF16))
        ps_t = stack.enter_context(nc.psum_tensor("psum", [128, 8, 512], F32))
        P = Prog(nc, sb_t[:, :], SB_BYTES)
        banks = [BankTile(ps_t[:, b, :]) for b in range(8)]
        A = P.add

        def bcast_rows(ap2d, n):
            return bass.AP(ap2d.tensor, ap2d.offset, [[0, n]] + [list(x) for x in ap2d.ap[1:]])

        def dump(name, t, shape, dt=F32):
            if name not in debug:
                return
            o = nc.dram_tensor("dbg_" + name, list(shape), dt, kind="ExternalOutput").ap()
            P.dma("sp", o, t.ap, reads=t.r(), out_final=True)

        class Stop(Exception):
            pass

        def phase_end(name):
            if stop_after == name:
                raise Stop()

        def body():
            xt = [P.tile([D], F32, top=True) for _ in range(3)]
            for i in range(2):
                P.dma("sp", xt[i].ap, x_ext[i * 128:(i + 1) * 128, :], writes=xt[i].r())
            g_bc = P.tile([D], F32)
            P.dma("sp", g_bc.ap, bcast_rows(g_mix, 128), writes=g_bc.r())
            ident_f = P.tile([128], F32)
            ident_b = P.tile([128], BF16)
            P.dma("sp", ident_f.ap, ident_d, writes=ident_f.r())
            A("dve", I("tensor_copy", out=ident_b.ap, in_=ident_f.ap), ident_f.r(), ident_b.r())
            tt = P.tile([383], F32)
            tr = P.tile([383], F32)
            rel = P.tile([16], F32)
            P.dma("sp", tt.ap[0:33], ttab, writes=tt.r())
            P.dma("sp", tr.ap[0:33], trev, writes=tr.r())
            P.dma("sp", rel.ap[0:33], rel_ext, writes=rel.r())
            sink_bc = P.tile([16], F32)
            P.dma("sp", sink_bc.ap, bcast_rows(sinks, 128), writes=sink_bc.r())
            hsel = P.tile([16], F32)
            P.dma("sp", hsel.ap, hsel_d, writes=hsel.r())
            hmask = P.tile([1], F32)
            P.dma("sp", hmask.ap, hmask_d, writes=hmask.r())
            wcv = P.tile([3, 8], F32, nreg=3)
            stat = P.tile([64], F32, nreg=64)
            sink_col = P.tile([1], F32)
            tmp16 = P.tile([16], F32)
            A("dve", I("tensor_tensor", out=tmp16.ap, in0=sink_bc.ap, in1=hsel.ap, op=ALU.mult),
              sink_bc.r() + hsel.r(), tmp16.r())
            A("dve", I("tensor_reduce", out=sink_col.ap, in_=tmp16.ap, axis=AX.X, op=ALU.add), tmp16.r(), sink_col.r())
            NW = 6
            wslots = [P.tile([4096], BF16, nreg=4) for _ in range(NW)]
            wctr = [0]

            wgate = []

            def wslot():
                t = wslots[wctr[0] % NW]
                wctr[0] += 1
                return t

            def wdma(dst, src, writes):
                op = P.dma("pool", dst, src, writes=writes)
                if wgate and wctr[0] <= NW:
                    op.deps.add(wgate[0])
                return op

            def wload(src, r0, nk, c0, ncols):
                assert nk * ncols <= 4096
                t = wslot()
                dst = t.ap[:, 0:nk * ncols].rearrange("p (k n) -> p k n", k=nk)
                s = src[r0:r0 + nk * 128, c0:c0 + ncols].rearrange("(k p) n -> p k n", p=128)
                P.dma("pool", dst, s, writes=t.r())
                return [(dst[:, k, :], t.r()) for k in range(nk)]

            bias_s = P.tile([137], F32)
            R0 = (P.arena.free[0][0] + 1023) // 1024 * 1024
            assert R0 + 144 * 1024 <= SB_BYTES, R0

            def KB(x):
                return R0 + int(x * 1024)

            xnT = P.tile([KC, EXT], BF16, nreg=10, at=KB(0))

            bias_p = P.tile([16, 257], F32, nreg=128, at=KB(91))
            LB = KB(40)
            tt_b = P.tile([383], BF16, lo=LB)
            tr_b = P.tile([383], BF16, lo=LB)
            rel_h = P.tile([16], BF16, lo=LB)
            rel_l = P.tile([16], BF16, lo=LB)
            rel_t = P.tile([16], F32, lo=LB)
            A("dve", I("tensor_copy", out=tt_b.ap[0:33], in_=tt.ap[0:33]), tt.r(), tt_b.r())
            A("dve", I("tensor_copy", out=tr_b.ap[0:33], in_=tr.ap[0:33]), tr.r(), tr_b.r())
            A("dve", I("tensor_copy", out=rel_h.ap[0:33], in_=rel.ap[0:33]), rel.r(), rel_h.r())
            A("dve", I("tensor_copy", out=rel_t.ap[0:33], in_=rel_h.ap[0:33]), rel_h.r(), rel_t.r())
            A("dve", I("tensor_tensor", out=rel_t.ap[0:33], in0=rel.ap[0:33], in1=rel_t.ap[0:33], op=ALU.subtract),
              rel.r() + rel_t.r(), rel_t.r())
            A("dve", I("tensor_copy", out=rel_l.ap[0:33], in_=rel_t.ap[0:33]), rel_t.r(), rel_l.r())
            lts = [P.tile([8, 128], BF16, lo=LB) for _ in range(2)]
            for lt_, rl_ in zip(lts, (rel_h, rel_l)):
                A("dve", I("memset", lt_.ap[0:33], 0.0), (), lt_.r())
                for t_ in range(8):
                    A("dve", I("tensor_copy", out=lt_.ap[0:33, t_, :].rearrange("p (h t) -> p h t", t=8)[:, :, t_],
                               in_=rl_.ap[0:33, :]), rl_.r() + lt_.r(), lt_.r())

            def bias_chunk(r):
                def f():
                    bk = banks[4 + r % 4]
                    for sl in range(32):
                        s = r * 32 + sl
                        for j, rl_ in enumerate((rel_h, rel_l)):
                            A("pe", I("matmul", bk.ap[:, sl * 16:(sl + 1) * 16], lhsT=tt_b.ap[0:33, 255 - s:255 - s + 128],
                                      rhs=rl_.ap[0:33, :], start=(j == 0), stop=(j == 1)), tt_b.r() + rl_.r(), bk.r())
                    A("dve", I("tensor_copy", out=bias_p.ap[:, :, r * 32:r * 32 + 32],
                               in_=bk.ap.rearrange("p (s h) -> p h s", h=16)), bk.r(), bias_p.r())
                return f

            def bias_sample():
                bk = banks[4]
                for (c0, n, t0) in ((0, 128, 127), (128, 8, 255)):
                    for t_ in range(8):
                        for j, lt_ in enumerate(lts):
                            A("pe", I("matmul", bk.ap[:, c0:c0 + n], lhsT=lt_.ap[0:33, t_, :], rhs=tr_b.ap[0:33, t0 - t_:t0 - t_ + n],
                                      start=(t_ == 0 and j == 0), stop=(t_ == 7 and j == 1)), lt_.r() + tr_b.r(), bk.r())
                A("dve", I("tensor_copy", out=bias_s.ap[:, 0:128], in_=bk.ap[:, 0:128]), bk.r(), bias_s.r())
                A("dve", I("tensor_copy", out=bias_s.ap[:, 129:137], in_=bk.ap[:, 128:136]), bk.r() + bias_s.r(), bias_s.r())
                A("dve", I("tensor_copy", out=bias_s.ap[:, 128:129], in_=sink_col.ap), sink_col.r() + bias_s.r(), bias_s.r())

            gsb = P.tile([383], F32, lo=LB)
            bkg = banks[7]
            for j, rl_ in enumerate((rel_h, rel_l)):
                A("pe", I("matmul", bkg.ap[0:16, 0:383], lhsT=rl_.ap[0:33, :], rhs=tr_b.ap[0:33, 0:383], start=(j == 0), stop=(j == 1)),
                  rl_.r() + tr_b.r(), bkg.r())
            A("dve", I("tensor_copy", out=gsb.ap[0:16], in_=bkg.ap[0:16, 0:383]), bkg.r(), gsb.r())
            gst = P.dma("sp", gscr, gsb.ap[0:16], reads=gsb.r())

            def bias_rows():
                for q in range(128):
                    src = bass.AP(gscr.tensor, 127 - q, [[383 * 16, 1], [383, 16], [1, 256]])
                    op = P.dma("sp", bias_p.ap[q:q + 1, :, 0:256], src, writes=bias_p.r(q))
                    op.deps.add(gst)

            bias_work = [bias_sample]

            def norm_transpose(src_fn, ntiles, dstT, stat0, lo, extra=()):
                xn_b = [P.tile([D], BF16, lo=lo) for _ in range(2)]
                junk = P.tile([D], BF16, lo=lo)
                srcs = {}

                def N0(i):
                    srcs[i] = src_fn(i)

                def N1a(i):
                    xap, xregs = srcs[i]
                    ss = stat.ap[:, stat0 + i:stat0 + i + 1]
                    ssr = stat.r(stat0 + i)
                    A("act", I("activation", out=junk.ap, in_=xap, func=AF.Square, accum_out=ss), xregs, junk.r() + ssr)
                    A("act", I("activation", out=ss, in_=ss, func=AF.Sqrt, scale=1.0 / D, bias=EPS), ssr, ssr)

                def N1b(i):
                    ss = stat.ap[:, stat0 + i:stat0 + i + 1]
                    ssr = stat.r(stat0 + i)
                    A("dve", I("reciprocal", out=ss, in_=ss), ssr, ssr)

                def N2(i):
                    xap, xregs = srcs[i]
                    ss = stat.ap[:, stat0 + i:stat0 + i + 1]
                    ssr = stat.r(stat0 + i)
                    xb = xn_b[i % 2]
                    A("dve", I("scalar_tensor_tensor", out=xb.ap, in0=xap, scalar=ss, in1=g_bc.ap, op0=ALU.mult, op1=ALU.mult),
                      xregs + ssr + g_bc.r(), xb.r())
                    pb = (banks[0], banks[1]) if i % 2 == 0 else (banks[2], banks[3])
                    for c in range(KC):
                        bk_ = pb[c // 8]
                        o = bk_.ap.bitcast(BF16)[:, (c % 8) * 128:(c % 8 + 1) * 128]
                        A("pe", I("transpose", out=o, in_=xb.ap[:, c * 128:(c + 1) * 128], identity=ident_b.ap),
                          xb.r() + ident_b.r(), bk_.r())

                def N3(i):
                    pb = (banks[0], banks[1]) if i % 2 == 0 else (banks[2], banks[3])
                    col = i * 128
                    for hf in range(2):
                        bk_ = pb[hf]
                        src = bk_.ap.bitcast(BF16).rearrange("p (c t) -> p c t", c=8)
                        dst = dstT.ap[:, hf * 8:(hf + 1) * 8, col:col + 128]
                        if hf == 0:
                            A("act", I("activation", out=dst, in_=src, func=AF.Identity), bk_.r(), dstT.r(i))
                        else:
                            A("dve", I("tensor_copy", out=dst, in_=src), bk_.r(), dstT.r(i))

                N0(0)
                if ntiles > 1:
                    N0(1)
                N1a(0)
                N1b(0)
                for i in range(ntiles + 1):
                    if i + 2 < ntiles:
                        N0(i + 2)
                    if i + 1 < ntiles:
                        N1a(i + 1)
                    if i < ntiles:
                        N2(i)
                    if i + 1 < ntiles:
                        N1b(i + 1)
                    if i < len(extra):
                        extra[i]()
                    if i >= 1:
                        N3(i - 1)
                for j in range(ntiles + 1, len(extra)):
                    extra[j]()
                P.free(junk)
                for t_ in xn_b:
                    P.free(t_)

            xload_ops = []

            def src1(i):
                t = xt[i % 3]
                if i >= 2:
                    xload_ops.append(P.dma("sp", t.ap, x_ext[i * 128:(i + 1) * 128, :], writes=t.r()))
                return t.ap, t.r()

            norm_transpose(src1, 10, xnT, 0, KB(40), extra=bias_work)
            for t_ in [tt_b, tr_b, rel_h, rel_l, rel_t, gsb] + lts:
                P.free(t_)
            dump("bias_p", bias_p, [128, 16, 257])
            dump("bias_s", bias_s, [128, 137])
            for t_ in xt:
                P.free(t_)
            dump("xnT", xnT, [128, KC, EXT], BF16)
            phase_end("p1")

            GA = (128, 512, [1, 2, 3, 4])
            GB = (640, 512, [5, 6, 7, 8])
            GS = (1152, 128, [9])
            GROUPS = [GA, GB, GS]
            RG = [(0, 512, [0, 1, 2, 3]), (512, 512, [4, 5, 6, 7]), (1024, 128, [8])]
            bank_rr = [0]

            def next_bank():
                b = banks[bank_rr[0] % 8]
                bank_rr[0] += 1
                return b

            def fm_mm(bk, col0, wk, m0, mw, acts, n):
                regs = bk.r(*[q for q in range(4) if q * 128 < col0 + n and (q + 1) * 128 > col0])
                nk = len(wk)
                for k in range(nk):
                    A("pe", I("matmul", bk.ap[0:mw, col0:col0 + n], lhsT=wk[k][0][:, m0:m0 + mw], rhs=acts[k][0],
                              start=(k == 0), stop=(k == nk - 1)), wk[k][1] + acts[k][1], regs)
                return regs

            def xn_acts(e0, n, xr):
                return [(xnT.ap[:, k, e0:e0 + n], xnT.r(*xr)) for k in range(KC)]

            for j in range(3):
                A("sp", I("dma_start", out=wcv.ap[:, j, :], in_=w_conv[j:j + 1, :].rearrange("o (c p) -> p (o c)", p=128),
                          allow_slow_non_contiguous=True), (), wcv.r(j), kind="dma")
            aT = P.tile([8, TR], BF16, nreg=24, at=KB(40))
            qT = P.tile([8, TR], BF16, nreg=24, at=KB(58))
            kT = P.tile([4, EXT], BF16, nreg=12, at=KB(76))
            v_tm = P.tile([10, 256], BF16, nreg=10, at=KB(86))
            L2 = KB(108)
            ubuf = P.tile([TP + 2], F32, nreg=3, lo=L2)
            us = P.tile([NSEQ, 10], F32, lo=L2)
            csb = [P.tile([512], F32, lo=L2) for _ in range(2)]
            t1b = [P.tile([512], F32, lo=L2) for _ in range(2)]
            t2b = [P.tile([512], F32, lo=L2) for _ in range(2)]
            uo_p = P.tile([8, 2], F32, lo=L2)
            uo_s = P.tile([8, 32], F32, lo=L2)
            scT = P.tile([8, 32], F32, lo=L2)
            sct = P.tile([1024], F32, lo=L2)
            P.dma("sp", sct.ap[0:32], sc, writes=sct.r())
            bias_rows()
            bk = next_bank()
            for c in range(8):
                A("pe", I("transpose", out=bk.ap[:, c * 32:(c + 1) * 32], in_=sct.ap[0:32, c * 128:(c + 1) * 128],
                          identity=ident_f.ap[0:32, 0:32]), sct.r() + ident_f.r(), bk.r())
            A("dve", I("tensor_copy", out=scT.ap, in_=bk.ap[:, 0:256].rearrange("p (c j) -> p c j", c=8)), bk.r(), scT.r())
            phase_end("p2a_sc")

            def v3(ap):
                return ap.rearrange("p (s t) -> p s t", t=8)

            it = 0
            for c in range(8):
                wk = []
                for kh in range(2):
                    t = wslot()
                    dst4 = t.ap[:, 0:8 * 384].rearrange("p (k j n) -> p k j n", k=8, j=3)
                    for j in range(3):
                        src = w_in[kh * 1024:(kh + 1) * 1024, j * 1024 + c * 128:j * 1024 + (c + 1) * 128].rearrange("(k p) n -> p k n", p=128)
                        wop = P.dma("pool", dst4[:, :, j, :], src, writes=t.r(j) if j < 2 else t.r(2, 3))
                        if c == 0 and kh == 0 and j == 0:
                            wop.deps.add(xload_ops[3])
                    dst3 = t.ap[:, 0:8 * 384].rearrange("p (k n) -> p k n", k=8)
                    wk += [(dst3[:, k, :], t.r()) for k in range(8)]
                if c == 0 and stop_after == "p2a_w0x":
                    A("dve", I("tensor_copy", out=t1b[0].ap[:, 0:128], in_=wk[0][0][:, 0:128]), wk[0][1] + wk[8][1], t1b[0].r())
                    phase_end("p2a_w0x")
                if c == 0:
                    phase_end("p2a_w0")
                bh = next_bank()
                ha = xn_acts(0, 128, [0])
                bh2 = next_bank()
                fm_mm(bh, 0, wk, 128, 128, ha, 128)
                fm_mm(bh2, 128, wk, 256, 128, ha, 128)
                if c == 0 and stop_after == "p2a_mm0w":
                    A("sp", None, bh.r(), ())
                    phase_end("p2a_mm0w")
                if c == 0:
                    phase_end("p2a_mm0")
                A("dve", I("tensor_copy", out=csb[0].ap[:, 0:128], in_=bh.ap[:, 0:128]), bh.r(0), csb[0].r())
                if c == 0:
                    phase_end("p2a_act0")
                A("dve", I("tensor_tensor", out=ubuf.ap[:, 0:2], in0=csb[0].ap[:, 126:128], in1=bh2.ap[:, 254:256], op=ALU.mult),
                  csb[0].r() + bh2.r(1), ubuf.r(0))
                if c == 0:
                    phase_end("p2a_c0h")
                for gi, (e0, n, xr) in enumerate((GA, GB)):
                    bB, bC, bH = next_bank(), next_bank(), next_bank()
                    acts = xn_acts(e0, n, xr)
                    fm_mm(bB, 0, wk, 0, 128, acts, n)
                    fm_mm(bC, 0, wk, 128, 128, acts, n)
                    fm_mm(bH, 0, wk, 256, 128, acts, n)
                    cs_, t1_, t2_ = csb[it % 2], t1b[it % 2], t2b[it % 2]
                    it += 1
                    off = gi * 512
                    A("act", I("activation", out=cs_.ap, in_=bC.ap, func=AF.Identity), bC.r(), cs_.r())
                    A("dve", I("tensor_tensor", out=ubuf.ap[:, 2 + off:2 + off + 512], in0=cs_.ap, in1=bH.ap, op=ALU.mult),
                      cs_.r() + bH.r(), ubuf.r(1 + gi))
                    ur = ubuf.r(0, 1) if gi == 0 else ubuf.r(1, 2)
                    A("act", I("activation", out=t1_.ap, in_=ubuf.ap[:, off:off + 512], func=AF.Identity, scale=wcv.ap[:, 0, c:c + 1]),
                      ur + wcv.r(), t1_.r())
                    A("dve", I("scalar_tensor_tensor", out=t2_.ap, in0=ubuf.ap[:, off + 1:off + 513], scalar=wcv.ap[:, 1, c:c + 1],
                               in1=t1_.ap, op0=ALU.mult, op1=ALU.add), ur + wcv.r() + t1_.r(), t2_.r())
                    A("dve", I("scalar_tensor_tensor", out=t1_.ap, in0=ubuf.ap[:, off + 2:off + 514], scalar=wcv.ap[:, 2, c:c + 1],
                               in1=t2_.ap, op0=ALU.mult, op1=ALU.add), ur + wcv.r() + t2_.r(), t1_.r())
                    A("dve", I("tensor_tensor", out=aT.ap[:, c, off:off + 512], in0=t1_.ap, in1=bB.ap, op=ALU.mult),
                      t1_.r() + bB.r(), aT.r(c * 3 + gi))
                A("dve", I("tensor_copy", out=uo_p.ap[:, c, :], in_=ubuf.ap[:, TP:TP + 2]), ubuf.r(2), uo_p.r())
                if c == 0:
                    phase_end("p2a_c0g")
                e0, n, xr = GS
                bS = next_bank()
                acts = xn_acts(e0, n, xr)
                fm_mm(bS, 0, wk, 0, 128, acts, n)
                fm_mm(bS, 128, wk, 128, 128, acts, n)
                fm_mm(bS, 256, wk, 256, 128, acts, n)
                cs_, t1_, t2_ = csb[it % 2], t1b[it % 2], t2b[it % 2]
                it += 1
                A("act", I("activation", out=cs_.ap[:, 0:128], in_=bS.ap[:, 128:256], func=AF.Identity), bS.r(1), cs_.r())
                A("dve", I("tensor_copy", out=us.ap[:, :, 0:2], in_=scT.ap[:, c, :].rearrange("p (s j) -> p s j", j=2)),
                  scT.r(), us.r())
                A("dve", I("tensor_tensor", out=us.ap[:, :, 2:10], in0=v3(cs_.ap[:, 0:128]), in1=v3(bS.ap[:, 256:384]), op=ALU.mult),
                  cs_.r() + bS.r(2), us.r())
                A("act", I("activation", out=v3(t1_.ap[:, 0:128]), in_=us.ap[:, :, 0:8], func=AF.Identity, scale=wcv.ap[:, 0, c:c + 1]),
                  us.r() + wcv.r(), t1_.r())
                A("dve", I("scalar_tensor_tensor", out=v3(t2_.ap[:, 0:128]), in0=us.ap[:, :, 1:9], scalar=wcv.ap[:, 1, c:c + 1],
                           in1=v3(t1_.ap[:, 0:128]), op0=ALU.mult, op1=ALU.add), us.r() + wcv.r() + t1_.r(), t2_.r())
                A("dve", I("scalar_tensor_tensor", out=v3(t1_.ap[:, 0:128]), in0=us.ap[:, :, 2:10], scalar=wcv.ap[:, 2, c:c + 1],
                           in1=v3(t2_.ap[:, 0:128]), op0=ALU.mult, op1=ALU.add), us.r() + wcv.r() + t2_.r(), t1_.r())
                A("dve", I("tensor_tensor", out=aT.ap[:, c, TP:TP + 128], in0=t1_.ap[:, 0:128], in1=bS.ap[:, 0:128], op=ALU.mult),
                  t1_.r() + bS.r(0), aT.r(c * 3 + 2))
                A("dve", I("tensor_copy", out=uo_s.ap[:, c, :].rearrange("p (s j) -> p s j", j=2), in_=us.ap[:, :, 8:10]),
                  us.r(), uo_s.r())
                if c == 0:
                    phase_end("p2a_c0")
            phase_end("p2a_conv")
            for (uo, npart, dst_out) in ((uo_p, 2, cnp_out), (uo_s, 32, cns_out)):
                cst = P.tile([1024], F32, lo=L2)
                bka, bkb = next_bank(), next_bank()
                for c in range(8):
                    bk_ = bka if c < 4 else bkb
                    A("pe", I("transpose", out=bk_.ap[0:npart, (c % 4) * 128:(c % 4 + 1) * 128], in_=uo.ap[:, c, :],
                              identity=ident_f.ap), uo.r() + ident_f.r(), bk_.r())
                A("dve", I("tensor_copy", out=cst.ap[0:npart, 0:512], in_=bka.ap[0:npart, :]), bka.r(), cst.r())
                A("dve", I("tensor_copy", out=cst.ap[0:npart, 512:1024], in_=bkb.ap[0:npart, :]), bkb.r(), cst.r())
                P.dma("sp", dst_out, cst.ap[0:npart, :], reads=cst.r(), out_final=True)
                P.free(cst)
            for t_ in csb + t1b + t2b + [ubuf, us, uo_p, uo_s, scT, sct]:
                P.free(t_)
            dump("aT", aT, [128, 8, TR], BF16)
            phase_end("p2a")

            for qp in range(4):
                wk = wload(w_in, 0, KC, 3072 + qp * 256, 256)
                for ml in range(2):
                    m = qp * 2 + ml
                    for gi, (e0, n, xr) in enumerate(GROUPS):
                        bk = next_bank()
                        regs = fm_mm(bk, 0, wk, ml * 128, 128, xn_acts(e0, n, xr), n)
                        A("act", I("activation", out=qT.ap[:, m, e0 - 128:e0 - 128 + n], in_=bk.ap[:, 0:n], func=AF.Identity, scale=0.125),
                          regs, qT.r(m * 3 + gi))
            EG = [(0, 512, [0, 1, 2, 3]), (512, 512, [4, 5, 6, 7]), (1024, 256, [8, 9])]
            for kp in range(2):
                t = wslot()
                w5 = t.ap.rearrange("p (k h u d) -> p k h u d", k=KC, h=2, u=2)
                for u in range(2):
                    for hh in range(2):
                        src = w_in[:, 4096 + kp * 128 + hh * 64:4096 + kp * 128 + (hh + 1) * 64].rearrange("(k p) n -> p k n", p=128)
                        P.dma("pool", w5[:, :, hh, u, :], src, writes=t.r(u * 2 + hh))
                w3 = t.ap.rearrange("p (k n) -> p k n", k=KC)
                wk = [(w3[:, k, :], t.r()) for k in range(KC)]
                for hl in range(2):
                    kvh = kp * 2 + hl
                    for gi, (e0, n, xr) in enumerate(EG):
                        bk = next_bank()
                        regs = fm_mm(bk, 0, wk, hl * 128, 128, xn_acts(e0, n, xr), n)
                        A("dve", I("tensor_copy", out=kT.ap[:, kvh, e0:e0 + n], in_=bk.ap[:, 0:n]), regs, kT.r(kvh * 3 + gi))
            phase_end("p2b")
            wk = wload(w_in, 0, 8, 4096, 512) + wload(w_in, 1024, 8, 4096, 512)
            kvo = [P.tile([512], F32, lo=L2) for _ in range(2)]
            for i in range(10):
                bk = next_bank()
                for kc in range(KC):
                    A("pe", I("matmul", bk.ap, lhsT=xnT.ap[:, kc, i * 128:(i + 1) * 128], rhs=wk[kc][0],
                              start=(kc == 0), stop=(kc == KC - 1)), wk[kc][1] + xnT.r(i), bk.r())
                if i == 0 and stop_after == "p2c_mm0":
                    A("sp", None, bk.r(), ())
                    phase_end("p2c_mm0")
                A("act", I("activation", out=v_tm.ap[:, i, :], in_=bk.ap[:, 256:512], func=AF.Identity), bk.r(2, 3), v_tm.r(i))
                if i == 0:
                    phase_end("p2c_i0")
                if i == 7:
                    phase_end("p2c_i7")
                if i == 8:
                    ko = kvo[0]
                    A("dve", I("tensor_copy", out=ko.ap, in_=bk.ap), bk.r(), ko.r())
                    P.dma("sp", knp_out, ko.ap[:, 0:256], reads=ko.r(), out_final=True)
                    P.dma("sp", vnp_out, ko.ap[:, 256:512], reads=ko.r(), out_final=True)
                if i == 9:
                    ko = kvo[1]
                    A("dve", I("tensor_copy", out=ko.ap, in_=bk.ap), bk.r(), ko.r())
                    import os
                    for s in range(NSEQ if not os.environ.get("SKIP_SMALL") else 0):
                        P.dma("sp", ks_out[s, 120:128, :], ko.ap[s * 8:(s + 1) * 8, 0:256], reads=ko.r(), out_final=True)
                        P.dma("sp", vs_out[s, 120:128, :], ko.ap[s * 8:(s + 1) * 8, 256:512], reads=ko.r(), out_final=True)
            phase_end("p2c0")
            P.dma("sp", ks_out[:, 0:120, :], ck[:, 8:128, :], out_final=True)
            P.dma("sp", vs_out[:, 0:120, :], cv[:, 8:128, :], out_final=True)
            for t_ in kvo:
                P.free(t_)
            dump("qT", qT, [128, 8, TR], BF16)
            dump("kT", kT, [128, 4, EXT], BF16)
            dump("v_tm", v_tm, [128, 10, 256], BF16)
            phase_end("p2")

            def load_pair(pair):
                wgc = wload(w_in, 0, KC, 4608 + pair * 256, 256)
                wga = wload(w_in, 0, KC, 6656 + pair * 256, 256)
                t = wslot()
                vy = t.ap.rearrange("p (k n) -> p k n", k=KC)
                P.dma("pool", vy[:, 0:8, :], w_co[:, pair * 256:(pair + 1) * 256].rearrange("(k p) n -> p k n", p=128), writes=t.r(0, 2))
                P.dma("pool", vy[:, 8:16, :], w_ao[:, pair * 256:(pair + 1) * 256].rearrange("(k p) n -> p k n", p=128), writes=t.r(1, 3))
                wyc = [(vy[:, k, :], t.r(0, 2)) for k in range(8)]
                wya = [(vy[:, 8 + k, :], t.r(1, 3)) for k in range(8)]
                return wgc, wga, wyc, wya

            pair_w = {0: load_pair(0), 1: load_pair(1)}

            attnT = P.tile([8, TR], BF16, nreg=72, at=KB(108))
            NSB = 4
            L3 = KB(126)
            sbs = [P.tile([257], F32, lo=L3) for _ in range(NSB)]
            pbs = [P.tile([257], BF16, lo=L3) for _ in range(NSB)]
            pts = [P.tile([256], BF16, lo=L3) for _ in range(NSB)]
            atm = [P.tile([8, 128], BF16, nreg=8, lo=L3) for _ in range(2)]
            tiles = [(2 * m + p, blk) for m in range(8) for p in range(2) for blk in range(1, 9)]
            NT = len(tiles)

            def tp_(k):
                h, blk = tiles[k]
                return h, blk, h // 4, h // 2, h % 2, k % NSB, k % 2

            def stA(k):
                h, blk, kvh, m, p, sl, pslot = tp_(k)
                sbank = banks[pslot]
                ps = slice(p * 64, (p + 1) * 64)
                kc0 = (blk - 1) * 128
                g0 = kc0 // 512 if kc0 < 1024 else 2
                g1 = (kc0 + 255) // 512 if kc0 + 255 < 1024 else 2
                kregs = kT.r(*sorted({kvh * 3 + g0, kvh * 3 + g1}))
                gq = 0 if blk <= 4 else 1
                A("pe", I("matmul", sbank.ap[:, 0:256], lhsT=qT.ap[ps, m, (blk - 1) * 128:blk * 128],
                          rhs=kT.ap[ps, kvh, kc0:kc0 + 256], start=True, stop=True), qT.r(m * 3 + gq) + kregs, sbank.r())

            def stB1(k):
                h, blk, kvh, m, p, sl, pslot = tp_(k)
                sbank = banks[pslot]
                sb_ = sbs[sl]
                A("dve", I("tensor_tensor", out=sb_.ap[:, 0:256], in0=sbank.ap[:, 0:256], in1=bias_p.ap[:, h, 0:256], op=ALU.add),
                  sbank.r() + bias_p.r(), sb_.r())
                if blk == 1:
                    A("dve", I("tensor_scalar", out=sb_.ap[:, 0:128], in0=sb_.ap[:, 0:128], scalar1=hmask.ap[:, 0:1], scalar2=None,
                               op0=ALU.add), sb_.r() + hmask.r(), sb_.r())
                if k % 8 < NSB:
                    A("dve", I("tensor_copy", out=sb_.ap[:, 256:257], in_=sink_bc.ap[:, h:h + 1]), sink_bc.r() + sb_.r(), sb_.r())

            def stB2(k):
                h, blk, kvh, m, p, sl, pslot = tp_(k)
                sb_, pb_ = sbs[sl], pbs[sl]
                mx, mxr = stat.ap[:, 32 + sl:33 + sl], stat.r(32 + sl)
                rs, rsr = stat.ap[:, 40 + sl:41 + sl], stat.r(40 + sl)
                A("dve", I("tensor_reduce", out=mx, in_=sb_.ap, axis=AX.X, op=ALU.max, negate=True), sb_.r(), mxr)
                A("act", I("activation", out=pb_.ap, in_=sb_.ap, func=AF.Exp, bias=mx, accum_out=rs), sb_.r() + mxr, pb_.r() + rsr)

            def stC(k):
                h, blk, kvh, m, p, sl, pslot = tp_(k)
                pb_ = pbs[sl]
                tb = banks[2 + pslot]
                tq = tb.ap.bitcast(BF16)[:, 0:256]
                for j in range(2):
                    A("pe", I("transpose", out=tq[:, j * 128:(j + 1) * 128], in_=pb_.ap[:, j * 128:(j + 1) * 128], identity=ident_b.ap),
                      pb_.r() + ident_b.r(), tb.r())

            def stD(k):
                h, blk, kvh, m, p, sl, pslot = tp_(k)
                tb = banks[2 + pslot]
                A("act", I("activation", out=pts[sl].ap, in_=tb.ap.bitcast(BF16)[:, 0:256], func=AF.Identity), tb.r(), pts[sl].r())

            def stE(k):
                h, blk, kvh, m, p, sl, pslot = tp_(k)
                pt_ = pts[sl]
                ob = banks[4 + pslot]
                oq = ob.ap[:, 0:64]
                A("pe", I("matmul", oq, lhsT=pt_.ap[:, 0:128], rhs=v_tm.ap[:, blk - 1, kvh * 64:(kvh + 1) * 64], start=True, stop=False),
                  pt_.r() + v_tm.r(blk - 1), ob.r())
                A("pe", I("matmul", oq, lhsT=pt_.ap[:, 128:256], rhs=v_tm.ap[:, blk, kvh * 64:(kvh + 1) * 64], start=False, stop=True),
                  pt_.r() + v_tm.r(blk), ob.r())

            def stF(k):
                h, blk, kvh, m, p, sl, pslot = tp_(k)
                rs, rsr = stat.ap[:, 40 + sl:41 + sl], stat.r(40 + sl)
                ob = banks[4 + pslot]
                at_ = atm[m % 2]
                A("dve", I("reciprocal", out=rs, in_=rs), rsr, rsr)
                A("act", I("activation", out=at_.ap[:, blk - 1, p * 64:(p + 1) * 64], in_=ob.ap[:, 0:64], func=AF.Identity, scale=rs),
                  ob.r() + rsr, at_.r(blk - 1))
                if k % 16 == 15:
                    tb = banks[6 + m % 2]
                    for b_ in range(8):
                        A("pe", I("transpose", out=tb.ap.bitcast(BF16)[:, b_ * 128:(b_ + 1) * 128], in_=at_.ap[:, b_, :],
                                  identity=ident_b.ap), at_.r(b_) + ident_b.r(), tb.r())
                    A("act", I("activation", out=attnT.ap[:, m, 0:TP], in_=tb.ap.bitcast(BF16), func=AF.Identity),
                      tb.r(), attnT.r(*[m * 9 + i for i in range(8)]))

            for i in range(NT + 4):
                if i < NT:
                    stA(i)
                if 0 <= i - 1 < NT:
                    stB1(i - 1)
                if 0 <= i - 2 < NT:
                    stB2(i - 2)
                if 0 <= i - 3 < NT:
                    stC(i - 3)
                    stD(i - 3)
                if 0 <= i - 4 < NT:
                    stE(i - 4)
                    stF(i - 4)
            for t_ in sbs + pbs + pts + atm + [bias_p]:
                P.free(t_)
            dump("attnT_p", attnT, [128, 8, TR], BF16)
            phase_end("p3a")

            L3b = KB(91)
            NB4 = 4
            ckb = [P.tile([4, 128], BF16, nreg=2, lo=L3b) for _ in range(NB4)]
            vsb = [P.tile([256], BF16, lo=L3b) for _ in range(NB4)]
            kts = [P.tile([4, 137], BF16, nreg=2, lo=L3b) for _ in range(NB4)]
            st1 = [P.tile([128], F32, nreg=2, lo=L3b) for _ in range(NB4)]
            st2 = [P.tile([128], F32, nreg=2, lo=L3b) for _ in range(NB4)]
            sbq = [P.tile([137], F32, lo=L3b) for _ in range(NB4)]
            pq = [P.tile([129], BF16, lo=L3b) for _ in range(NB4)]
            pn = P.tile([NSEQ, 128], BF16, nreg=NSEQ, lo=L3b)
            ptq = [P.tile([256], BF16, lo=L3b) for _ in range(NB4)]
            asd = [P.tile([128], BF16, nreg=4, lo=L3b) for _ in range(NB4)]
            osb = [P.tile([256], F32, lo=L3b) for _ in range(NB4)]
            sstat = P.tile([NB4, 4], F32, nreg=NB4 * 4, lo=L3b)
            A("dve", I("memset", pn.ap, 0.0), (), pn.r())
            tb = banks[2]
            b5, b6, b7 = banks[5], banks[6], banks[7]
            A("dve", I("memset", b7.ap[:, 128:129], 0.0), (), b7.r())

            def hv(ap):
                return ap.rearrange("p (m q t) -> p m q t", m=8, q=2)

            def sst(b4, j):
                return sstat.ap[:, b4, j:j + 1], sstat.r(b4 * 4 + j)

            tb3, obk, tb1 = banks[3], banks[4], banks[1]

            def X1a(s):
                b4 = s % NB4
                ckv = ckb[b4].ap.rearrange("p h (u d) -> p h u d", u=2)
                for u in range(2):
                    P.dma("pool", ckv[:, :, u, :], ck[s].rearrange("p (h d) -> p h d", h=4), writes=ckb[b4].r(u))
                P.dma("pool", vsb[b4].ap, cv[s], writes=vsb[b4].r())
                for kvh in range(4):
                    A("pe", I("transpose", out=tb.ap.bitcast(BF16)[:, kvh * 128:(kvh + 1) * 128], in_=ckb[b4].ap[:, kvh, :],
                              identity=ident_b.ap), ckb[b4].r() + ident_b.r(), tb.r())

            def X1b(s):
                b4 = s % NB4
                A("act", I("activation", out=kts[b4].ap[:, :, 0:128],
                           in_=tb.ap.bitcast(BF16)[:, 0:512].rearrange("p (h k) -> p h k", h=4), func=AF.Identity), tb.r(), kts[b4].r(0))
                A("dve", I("tensor_copy", out=kts[b4].ap[:, :, 129:137], in_=kT.ap[:, :, 1152 + s * 8:1152 + s * 8 + 8]),
                  kT.r(2, 5, 8, 11), kts[b4].r(1))

            def hv5(ap):
                return ap.rearrange("p (k g q t) -> p k g q t", k=4, g=2, q=2)

            def X2a(s):
                b4 = s % NB4
                for c0, kc in ((0, slice(0, 128)), (128, slice(129, 137))):
                    npart = 128 if c0 == 0 else 8
                    for kvh in range(4):
                        for p in range(2):
                            ps = slice(p * 64, (p + 1) * 64)
                            bq = b6 if p == 0 else b5
                            A("pe", I("matmul", hv5(bq.ap[0:npart, c0:c0 + 128])[:, kvh, :, p, :], lhsT=kts[b4].ap[ps, kvh, kc],
                                      rhs=qT.ap[ps, 2 * kvh:2 * kvh + 2, TP + s * 8:TP + s * 8 + 8], start=True, stop=True),
                              kts[b4].r() + qT.r((2 * kvh) * 3 + 2, (2 * kvh + 1) * 3 + 2), bq.r())

            def X2b(s):
                b4 = s % NB4
                A("act", I("activation", out=hv(st1[b4].ap)[:, :, 0, :], in_=hv(b6.ap[:, 0:128])[:, :, 0, :], func=AF.Identity),
                  b6.r(), st1[b4].r(0))
                A("dve", I("tensor_copy", out=hv(st1[b4].ap)[:, :, 1, :], in_=hv(b5.ap[:, 0:128])[:, :, 1, :]),
                  b5.r(), st1[b4].r(1))
                A("act", I("activation", out=hv(st2[b4].ap[0:8])[:, :, 0, :], in_=hv(b6.ap[0:8, 128:256])[:, :, 0, :], func=AF.Identity),
                  b6.r(), st2[b4].r(0))
                A("dve", I("tensor_copy", out=hv(st2[b4].ap[0:8])[:, :, 1, :], in_=hv(b5.ap[0:8, 128:256])[:, :, 1, :]),
                  b5.r(), st2[b4].r(1))

            def X3a(s):
                b4 = s % NB4
                A("pe", I("transpose", out=b7.ap[:, 0:128], in_=st1[b4].ap, identity=ident_f.ap), st1[b4].r() + ident_f.r(), b7.r())
                A("pe", I("transpose", out=b7.ap[:, 129:137], in_=st2[b4].ap[0:8, :], identity=ident_f.ap[0:8, 0:8]),
                  st2[b4].r() + ident_f.r(), b7.r())

            def X3b1(s):
                b4 = s % NB4
                (mx, mxr) = sst(b4, 0)
                A("dve", I("tensor_tensor", out=sbq[b4].ap, in0=b7.ap[:, 0:137], in1=bias_s.ap, op=ALU.add), b7.r() + bias_s.r(), sbq[b4].r())
                A("dve", I("tensor_reduce", out=mx, in_=sbq[b4].ap, axis=AX.X, op=ALU.max, negate=True), sbq[b4].r(), mxr)

            def X3b2(s):
                b4 = s % NB4
                (mx, mxr), (r1, r1r), (r2, r2r) = sst(b4, 0), sst(b4, 1), sst(b4, 2)
                A("act", I("activation", out=pq[b4].ap, in_=sbq[b4].ap[:, 0:129], func=AF.Exp, bias=mx, accum_out=r1),
                  sbq[b4].r() + mxr, pq[b4].r() + r1r)
                A("act", I("activation", out=pn.ap[:, s, s * 8:s * 8 + 8], in_=sbq[b4].ap[:, 129:137], func=AF.Exp, bias=mx, accum_out=r2),
                  sbq[b4].r() + mxr, pn.r(s) + r2r)

            def Y1a(s):
                b4 = s % NB4
                tq = tb3.ap.bitcast(BF16)[:, 0:256]
                A("pe", I("transpose", out=tq[:, 0:128], in_=pq[b4].ap[:, 0:128], identity=ident_b.ap), pq[b4].r() + ident_b.r(), tb3.r())
                A("pe", I("transpose", out=tq[:, 128:256], in_=pn.ap[:, s, :], identity=ident_b.ap), pn.r(s) + ident_b.r(), tb3.r())

            def Y1b1(s):
                b4 = s % NB4
                A("act", I("activation", out=ptq[b4].ap, in_=tb3.ap.bitcast(BF16)[:, 0:256], func=AF.Identity), tb3.r(), ptq[b4].r())

            def Y1b2(s):
                b4 = s % NB4
                A("pe", I("matmul", obk.ap[:, 0:256], lhsT=ptq[b4].ap[:, 0:128], rhs=vsb[b4].ap, start=True, stop=False),
                  ptq[b4].r() + vsb[b4].r(), obk.r())
                A("pe", I("matmul", obk.ap[:, 0:256], lhsT=ptq[b4].ap[:, 128:256], rhs=v_tm.ap[:, 9, :], start=False, stop=True),
                  ptq[b4].r() + v_tm.r(9), obk.r())

            def Y2a(s):
                b4 = s % NB4
                (r1, r1r), (r2, r2r) = sst(b4, 1), sst(b4, 2)
                A("act", I("activation", out=osb[b4].ap, in_=obk.ap[:, 0:256], func=AF.Identity), obk.r(), osb[b4].r())
                A("dve", I("tensor_tensor", out=r1, in0=r1, in1=r2, op=ALU.add), r1r + r2r, r1r)
                A("dve", I("reciprocal", out=r1, in_=r1), r1r, r1r)

            def Y2b1(s):
                b4 = s % NB4
                (r1, r1r) = sst(b4, 1)
                asv = asd[b4].ap.rearrange("p (u d) -> p u d", u=2)
                for kvh in range(4):
                    pr = slice(kvh * 32, (kvh + 1) * 32)
                    src = osb[b4].ap[pr, kvh * 64:(kvh + 1) * 64]
                    srcb = bass.AP(src.tensor, src.offset, [list(src.ap[0]), [0, 2], list(src.ap[1])])
                    if kvh % 2 == 0:
                        A("dve", I("tensor_scalar", out=asv[pr, :, :], in0=srcb, scalar1=r1[pr, :], scalar2=None, op0=ALU.mult),
                          osb[b4].r() + r1r, asd[b4].r(kvh))
                    else:
                        A("act", I("activation", out=asv[pr, :, :], in_=srcb, func=AF.Identity, scale=r1[pr, :]),
                          osb[b4].r() + r1r, asd[b4].r(kvh))

            def Y2b2(s):
                b4 = s % NB4
                A("pe", I("transpose", out=tb1.ap.bitcast(BF16)[:, 0:128], in_=asd[b4].ap, identity=ident_b.ap),
                  asd[b4].r() + ident_b.r(), tb1.r())

            def Y2b3(s):
                tq2 = tb1.ap.bitcast(BF16)[:, 0:128]
                for p in range(2):
                    pr = slice(p * 64, (p + 1) * 64)
                    src = tq2.rearrange("p (m q t) -> p m q t", m=8, q=2)[pr, :, p, :]
                    A("dve", I("tensor_copy", out=attnT.ap[pr, :, TP + s * 8:TP + s * 8 + 8], in_=src),
                      tb1.r() + attnT.r(*[m * 9 + 8 for m in range(8)]), attnT.r(*[m * 9 + 8 for m in range(8)]))

            sched = [(4, Y2a), (3, Y1a), (2, X3a), (1, X2a), (0, X1a),
                     (4, Y2b1), (3, Y1b1), (2, X3b1), (1, X2b), (0, X1b),
                     (4, Y2b2), (3, Y1b2), (2, X3b2),
                     (4, Y2b3)]
            for i in range(NSEQ + 4):
                for d_, fn_ in sched:
                    if 0 <= i - d_ < NSEQ:
                        fn_(i - d_)
            for t_ in ckb + vsb + kts + st1 + st2 + sbq + pq + [pn, sstat] + ptq + asd + osb:
                P.free(t_)
            P.free(qT)
            P.free(kT)
            P.free(v_tm)
            dump("attnT", attnT, [128, 8, TR], BF16)
            phase_end("p3")

            mergedT = P.tile([KC, TR], BF16, nreg=48, at=KB(72))
            sg = [P.tile([512], F32, lo=KB(126)) for _ in range(4)]
            mm = [P.tile([512], F32, lo=KB(126)) for _ in range(4)]
            it = 0
            for pair in range(8):
                wgc, wga, wyc, wya = pair_w[pair] if pair in pair_w else load_pair(pair)
                for ml in range(2):
                    c = pair * 2 + ml
                    for gi, (r0, n, tr_) in enumerate(RG):
                        e0 = r0 + 128
                        xr = [x + 1 for x in tr_]
                        bs = [banks[(it % 2) * 4 + j] for j in range(4)]
                        rg0 = fm_mm(bs[0], 0, wgc, ml * 128, 128, xn_acts(e0, n, xr), n)
                        rg1 = fm_mm(bs[1], 0, wga, ml * 128, 128, xn_acts(e0, n, xr), n)
                        rg2 = fm_mm(bs[2], 0, wyc, ml * 128, 128, [(aT.ap[:, k, r0:r0 + n], aT.r(k * 3 + gi)) for k in range(8)], n)
                        rg3 = fm_mm(bs[3], 0, wya, ml * 128, 128,
                                    [(attnT.ap[:, k, r0:r0 + n], attnT.r(*[k * 9 + x for x in tr_])) for k in range(8)], n)
                        s0, s1, m0, m1 = sg[(it % 2) * 2], sg[(it % 2) * 2 + 1], mm[(it % 2) * 2], mm[(it % 2) * 2 + 1]
                        it += 1
                        A("act", I("activation", out=s0.ap[:, 0:n], in_=bs[0].ap[:, 0:n], func=AF.Sigmoid), rg0, s0.r())
                        A("act", I("activation", out=s1.ap[:, 0:n], in_=bs[1].ap[:, 0:n], func=AF.Sigmoid), rg1, s1.r())
                        A("dve", I("tensor_tensor", out=m0.ap[:, 0:n], in0=s0.ap[:, 0:n], in1=bs[2].ap[:, 0:n], op=ALU.mult), s0.r() + rg2, m0.r())
                        A("dve", I("tensor_tensor", out=m1.ap[:, 0:n], in0=s1.ap[:, 0:n], in1=bs[3].ap[:, 0:n], op=ALU.mult), s1.r() + rg3, m1.r())
                        A("dve", I("tensor_tensor", out=mergedT.ap[:, c, r0:r0 + n], in0=m0.ap[:, 0:n], in1=m1.ap[:, 0:n], op=ALU.add),
                          m0.r() + m1.r(), mergedT.r(c * 3 + gi))
            for t_ in sg + mm:
                P.free(t_)
            P.free(aT)
            P.free(attnT)
            P.free(xnT)
            dump("mergedT", mergedT, [128, KC, TR], BF16)
            phase_end("p4a")

            h_acc = P.tile([9, D], F32, nreg=36, at=KB(0))
            for i in range(9):
                P.dma("sp", h_acc.ap[:, i, :], x_ext[128 + i * 128:128 + (i + 1) * 128, :], writes=h_acc.r(*[i * 4 + n for n in range(4)]))
            for nb in range(4):
                wk = wload(w_o, 0, 8, nb * 512, 512) + wload(w_o, 1024, 8, nb * 512, 512)
                for i in range(9):
                    bk = next_bank()
                    gi = 0 if i < 4 else (1 if i < 8 else 2)
                    for kc in range(KC):
                        A("pe", I("matmul", bk.ap, lhsT=mergedT.ap[:, kc, i * 128:(i + 1) * 128], rhs=wk[kc][0],
                                  start=(kc == 0), stop=(kc == KC - 1)), wk[kc][1] + mergedT.r(kc * 3 + gi), bk.r())
                    hs = h_acc.ap[:, i, nb * 512:(nb + 1) * 512]
                    A("dve", I("tensor_tensor", out=hs, in0=hs, in1=bk.ap, op=ALU.add), bk.r() + h_acc.r(i * 4 + nb), h_acc.r(i * 4 + nb))
            P.free(mergedT)
            dump("h1", h_acc, [128, 9, D])
            phase_end("p4b")

            P.dma("sp", g_bc.ap, bcast_rows(g_ffn, 128), writes=g_bc.r())
            hnT = P.tile([KC, TR], BF16, nreg=9, at=KB(72))

            def src2(i):
                return h_acc.ap[:, i, :], h_acc.r(*[i * 4 + n for n in range(4)])

            norm_transpose(src2, 9, hnT, 16, KB(108))
            dump("hnT", hnT, [128, KC, TR], BF16)
            guT = [P.tile([4, TR], BF16, nreg=12, lo=KB(108)) for _ in range(2)]
            sgl = [P.tile([512], F32, lo=KB(108)) for _ in range(2)]
            it = 0
            for j in range(DFF // 512):
                gu = guT[j % 2]
                for hp in range(2):
                    wg = wload(w_g, 0, KC, j * 512 + hp * 256, 256)
                    wu = wload(w_u, 0, KC, j * 512 + hp * 256, 256)
                    for ml in range(2):
                        mi = hp * 2 + ml
                        for gi, (r0, n, tr_) in enumerate(RG):
                            bg, bu = banks[(it % 3) * 2], banks[(it % 3) * 2 + 1]
                            acts = [(hnT.ap[:, k, r0:r0 + n], hnT.r(*tr_)) for k in range(KC)]
                            rg = fm_mm(bg, 0, wg, ml * 128, 128, acts, n)
                            ru = fm_mm(bu, 0, wu, ml * 128, 128, acts, n)
                            s_ = sgl[it % 2]
                            it += 1
                            A("act", I("activation", out=s_.ap[:, 0:n], in_=bg.ap[:, 0:n], func=AF.Silu), rg, s_.r())
                            A("dve", I("tensor_tensor", out=gu.ap[:, mi, r0:r0 + n], in0=s_.ap[:, 0:n], in1=bu.ap[:, 0:n], op=ALU.mult),
                              s_.r() + ru, gu.r(mi * 3 + gi))
                for nb in range(4):
                    wd = wload(w_d, j * 512, 4, nb * 512, 512)
                    for i in range(9):
                        bk = banks[6 + (i + nb) % 2]
                        gi = 0 if i < 4 else (1 if i < 8 else 2)
                        for kc in range(4):
                            A("pe", I("matmul", bk.ap, lhsT=gu.ap[:, kc, i * 128:(i + 1) * 128], rhs=wd[kc][0],
                                      start=(kc == 0), stop=(kc == 3)), wd[kc][1] + gu.r(kc * 3 + gi), bk.r())
                        hs = h_acc.ap[:, i, nb * 512:(nb + 1) * 512]
                        A("dve", I("tensor_tensor", out=hs, in0=hs, in1=bk.ap, op=ALU.add), bk.r() + h_acc.r(i * 4 + nb), h_acc.r(i * 4 + nb))
            for t_ in guT + sgl:
                P.free(t_)
            P.free(hnT)
            dump("h2", h_acc, [128, 9, D])
            phase_end("p5")

            P.dma("sp", g_bc.ap, bcast_rows(g_fin, 128), writes=g_bc.r())
            yt = [P.tile([D], F32, lo=KB(72)) for _ in range(2)]
            junk = P.tile([D], BF16, lo=KB(72))

            def F1(i):
                hap = h_acc.ap[:, i, :]
                hr = h_acc.r(*[i * 4 + n for n in range(4)])
                ss = stat.ap[:, 54 + i:55 + i]
                ssr = stat.r(54 + i)
                A("act", I("activation", out=junk.ap, in_=hap, func=AF.Square, accum_out=ss), hr, junk.r() + ssr)
                A("act", I("activation", out=ss, in_=ss, func=AF.Sqrt, scale=1.0 / D, bias=EPS), ssr, ssr)
                A("dve", I("reciprocal", out=ss, in_=ss), ssr, ssr)

            def F2(i):
                hap = h_acc.ap[:, i, :]
                hr = h_acc.r(*[i * 4 + n for n in range(4)])
                ss = stat.ap[:, 54 + i:55 + i]
                ssr = stat.r(54 + i)
                y_ = yt[i % 2]
                A("dve", I("scalar_tensor_tensor", out=y_.ap, in0=hap, scalar=ss, in1=g_bc.ap, op0=ALU.mult, op1=ALU.mult),
                  hr + ssr + g_bc.r(), y_.r())
                P.dma("sp", y_out[i * 128:(i + 1) * 128, :], y_.ap, reads=y_.r(), out_final=True)

            F1(0)
            for i in range(9):
                if i + 1 < 9:
                    F1(i + 1)
                F2(i)

        try:
            body()
        except Stop:
            pass
        P.finalize(stack)
        print("sbuf peak bytes/partition:", P.arena.peak, "ops:", len(P.ops), flush=True)
    return nc


_NC_CACHE = {}


def host_inputs(x_prompt, x_sample, cache_k, cache_v, state_conv, rel_bias, w_in, w_conv, w_conv_out,
                sinks, w_attn_out, w_o, g_mix, g_ffn, w_gate, w_up, w_down, g_final, cores=range(NCORES)):
    f = lambda a: np.ascontiguousarray(np.asarray(a, dtype=np.float32))
    T, Trev = _bucket_tables()
    rel_ext = np.concatenate([f(rel_bias), np.full((1, 16), NEG, np.float32)], axis=0)
    hsel = np.zeros((128, 16), np.float32)
    hsel[np.arange(128), np.arange(128) // 8] = 1.0
    shared = {
        "w_in": f(w_in[0]), "w_conv": f(w_conv[0]), "w_co": f(w_conv_out[0]), "w_ao": f(w_attn_out[0]), "w_o": f(w_o[0]),
        "w_g": f(w_gate[0]), "w_u": f(w_up[0]), "w_d": f(w_down[0]),
        "g_mix": f(g_mix[0]).reshape(1, D), "g_ffn": f(g_ffn[0]).reshape(1, D), "g_fin": f(g_final).reshape(1, D),
        "rel_ext": rel_ext, "sinks": f(sinks[0]).reshape(1, 16), "ttab": T, "trev": Trev, "hsel": hsel,
        "ident": np.eye(128, dtype=np.float32),
    }
    xp = np.asarray(x_prompt, dtype=np.float32)
    xs = np.asarray(x_sample, dtype=np.float32)
    maps = []
    for c in cores:
        b, half = c // 2, c % 2
        x_ext = np.zeros((EXT, D), np.float32)
        if half == 1:
            x_ext[0:128] = xp[b, TP - 128:TP]
        x_ext[128:128 + TP] = xp[b, half * TP:(half + 1) * TP]
        x_ext[128 + TP:] = xs[c * NSEQ:(c + 1) * NSEQ].reshape(TS, D)
        m = dict(shared)
        m["x_ext"] = x_ext
        m["ck"] = f(cache_k[0, c * NSEQ:(c + 1) * NSEQ]).reshape(NSEQ, 128, 256)
        m["cv"] = f(cache_v[0, c * NSEQ:(c + 1) * NSEQ]).reshape(NSEQ, 128, 256)
        m["sc"] = f(state_conv[0, c * NSEQ:(c + 1) * NSEQ]).reshape(NSEQ * 2, 1024)
        m["hmask"] = np.full((128, 1), 0.0 if half == 1 else NEG, np.float32)
        maps.append(m)
    return maps


def kernel(x_prompt, x_sample, cache_k, cache_v, state_conv, rel_bias, w_in, w_conv, w_conv_out,
           sinks, w_attn_out, w_o, g_mix, g_ffn, w_gate, w_up, w_down, g_final):
    if "nc" not in _NC_CACHE:
        _NC_CACHE["nc"] = build()
    nc = _NC_CACHE["nc"]
    maps = host_inputs(x_prompt, x_sample, cache_k, cache_v, state_conv, rel_bias, w_in, w_conv, w_conv_out,
                       sinks, w_attn_out, w_o, g_mix, g_ffn, w_gate, w_up, w_down, g_final)
    res = run_bass_kernel_spmd(nc, maps, core_ids=list(range(NCORES)))
    R = res.results
    B = 4
    y_prompt = np.zeros((B, 2048, D), np.float32)
    y_sample = np.zeros((128, 8, D), np.float32)
    nkp = np.zeros((1, B, 128, 4, 64), np.float32)
    nvp = np.zeros((1, B, 128, 4, 64), np.float32)
    ncp = np.zeros((1, B, 2, 1024), np.float32)
    nks = np.zeros((1, 128, 128, 4, 64), np.float32)
    nvs = np.zeros((1, 128, 128, 4, 64), np.float32)
    ncs = np.zeros((1, 128, 2, 1024), np.float32)
    for c in range(NCORES):
        b, half = c // 2, c % 2
        r = R[c]
        y = np.asarray(r["y"])
        y_prompt[b, half * TP:(half + 1) * TP] = y[0:TP]
        y_sample[c * NSEQ:(c + 1) * NSEQ] = y[TP:].reshape(NSEQ, 8, D)
        if half == 1:
            nkp[0, b] = np.asarray(r["knp"]).reshape(128, 4, 64)
            nvp[0, b] = np.asarray(r["vnp"]).reshape(128, 4, 64)
            ncp[0, b] = np.asarray(r["cnp"])
        nks[0, c * NSEQ:(c + 1) * NSEQ] = np.asarray(r["ks"]).reshape(NSEQ, 128, 4, 64)
        nvs[0, c * NSEQ:(c + 1) * NSEQ] = np.asarray(r["vs"]).reshape(NSEQ, 128, 4, 64)
        ncs[0, c * NSEQ:(c + 1) * NSEQ] = np.asarray(r["cns"]).reshape(NSEQ, 2, 1024)
    return (y_prompt, y_sample, nkp, nvp, ncp, nks, nvs, ncs)
```

```python
import math
from contextlib import ExitStack

import numpy as np
import concourse.bass as bass
import concourse.mybir as mybir
from concourse.bass_utils import run_bass_kernel_spmd

F32 = mybir.dt.float32
BF16 = mybir.dt.bfloat16
AF = mybir.ActivationFunctionType
ALU = mybir.AluOpType
AX = mybir.AxisListType

D = 2048
DFF = 5632
DIN = 8704
NCORES = 8
TP = 1024
TS = 128
TR = TP + TS
EXT = TR + 128
NSEQ = 16
EPS = 1e-6
NEG = -30000.0
KC = D // 128


class Reg:
    __slots__ = ("w", "rs", "excl")

    def __init__(self):
        self.w = None
        self.rs = []
        self.excl = False


class Tile:
    def __init__(self, ap, nreg, arena=None, off=0, nbytes=0):
        self.ap = ap
        self.regs = [Reg() for _ in range(nreg)]
        self.arena = arena
        self.off = off
        self.nbytes = nbytes

    def r(self, *idx):
        if not idx:
            return list(self.regs)
        return [self.regs[i] for i in idx]


class BankTile(Tile):
    def __init__(self, ap):
        Tile.__init__(self, ap, 1)
        self.regs[0].excl = True

    def r(self, *idx):
        return [self.regs[0]]


class Op:
    __slots__ = ("eng", "fn", "kind", "deps", "sig", "seq", "sem", "val", "prev", "waits", "clock", "idx")

    def __init__(self, eng, fn, kind):
        self.eng = eng
        self.fn = fn
        self.kind = kind
        self.deps = set()
        self.sig = False
        self.seq = 0
        self.sem = None
        self.val = 0
        self.prev = None
        self.waits = []
        self.clock = None
        self.idx = 0


class Arena:
    def __init__(self, size):
        self.size = size
        self.free = [(0, size)]
        self.pending = []
        self.peak = 0

    def alloc(self, n, top=False, at=None, lo=None):
        n = (n + 63) // 64 * 64
        order = list(enumerate(self.free))
        if top:
            order = order[::-1]
        for i, (s, e) in order:
            if at is not None:
                if not (s <= at and at + n <= e):
                    continue
                a = at
            elif lo is not None:
                a = max(s, lo)
                if a + n > e:
                    continue
            elif top:
                a = e - n
                if a < s:
                    continue
            else:
                a = s
                if a + n > e:
                    continue
            self.free.pop(i)
            if a > s:
                self.free.append((s, a))
            if a + n < e:
                self.free.append((a + n, e))
            self.free.sort()
            pend = []
            for (ps, pe, ops) in self.pending:
                if ps < a + n and pe > a:
                    pend.extend(ops)
            self.peak = max(self.peak, a + n)
            return a, n, pend
        raise RuntimeError(f"arena OOM: need {n} at={at} lo={lo}, free={self.free}")

    def release(self, off, n, ops):
        self.free.append((off, off + n))
        self.free.sort()
        merged = []
        for s, e in self.free:
            if merged and merged[-1][1] == s:
                merged[-1] = (merged[-1][0], e)
            else:
                merged.append((s, e))
        self.free = merged
        self.pending.append((off, off + n, ops))


class Prog:
    ENGS = ("pe", "act", "dve", "pool", "sp")
    NDSEM = {"sp": 16, "pool": 8}

    def __init__(self, nc, sb_arena_ap, sb_bytes):
        self.nc = nc
        self.ops = []
        self.out_ops = []
        self.arena = Arena(sb_bytes)
        self.sb = sb_arena_ap

    def tile(self, shape, dt, nreg=1, parts=128, top=False, at=None, lo=None):
        esz = 4 if dt == F32 else 2
        n = esz
        for s in shape:
            n *= s
        off, nb, pend = self.arena.alloc(n, top, at, lo)
        ap = self.sb[0:parts, off // 2:(off + n) // 2]
        if dt == F32:
            ap = ap.bitcast(F32)
        if len(shape) == 2:
            ap = ap.rearrange("p (a b) -> p a b", a=shape[0])
        elif len(shape) == 3:
            ap = ap.rearrange("p (a b c) -> p a b c", a=shape[0], b=shape[1])
        elif len(shape) == 4:
            ap = ap.rearrange("p (a b c d) -> p a b c d", a=shape[0], b=shape[1], c=shape[2])
        t = Tile(ap, nreg, self.arena, off, nb)
        if pend:
            for r in t.regs:
                r.rs = list(pend)
        return t

    def free(self, t):
        ops = []
        for r in t.regs:
            if r.w is not None:
                ops.append(r.w)
            ops.extend(r.rs)
        ops = list({id(o): o for o in ops}.values())
        self.arena.release(t.off, t.nbytes, ops)

    def add(self, eng, fn, reads=(), writes=(), kind="c", out=False):
        op = Op(eng, fn, kind)
        op.idx = len(self.ops)
        xr = [r for r in reads if r.excl]
        if xr:
            reads = [r for r in reads if not r.excl]
            writes = list(writes) + [r for r in xr if r not in writes]
        for r in reads:
            if r.w is not None:
                op.deps.add(r.w)
        for r in writes:
            if r.w is not None:
                op.deps.add(r.w)
            for o in r.rs:
                op.deps.add(o)
        for r in reads:
            r.rs.append(op)
        for r in writes:
            r.w = op
            r.rs = []
        op.deps.discard(op)
        self.ops.append(op)
        if out:
            self.out_ops.append(op)
        return op

    def dma(self, q, out, in_, reads=(), writes=(), out_final=False):
        return self.add(q, lambda e: e.dma_start(out=out, in_=in_), reads, writes, kind="dma", out=out_final)

    def finalize(self, stack):
        nc = self.nc
        fin = Op("sp", None, "c")
        fin.idx = len(self.ops)
        fin.deps = set(self.out_ops)
        self.ops.append(fin)
        for op in self.ops:
            for d in op.deps:
                if d.kind == "dma":
                    continue
                if d.eng == "pe" and op.eng == "pe":
                    continue
                d.sig = True
        dsem = {}
        for q in ("pool", "sp"):
            dsem[q] = [stack.enter_context(nc.semaphore(f"d_{q}{i}")) for i in range(self.NDSEM[q])]
        engsem = {e: stack.enter_context(nc.semaphore("s_" + e)) for e in ("pe", "act", "dve", "pool")}
        print("semaphores:", [x.num if hasattr(x, "num") else x for x in dsem["pool"][:2] + dsem["sp"][-2:] + list(engsem.values())])
        cnt = {e: 0 for e in self.ENGS}
        dcnt = {q: 0 for q in self.NDSEM}
        for op in self.ops:
            if op.kind == "dma":
                q = op.eng
                j = dcnt[q]
                K = self.NDSEM[q]
                op.sem = (q, j % K)
                op.val = 16 * (j // K + 1)
                if j >= K:
                    op.prev = (("d", q, j % K), 16 * (j // K))
                dcnt[q] += 1
            elif op.sig:
                cnt[op.eng] += 1
                op.seq = cnt[op.eng]
        known = {e: {} for e in self.ENGS}
        for op in self.ops:
            kn = known[op.eng]
            waits = []
            for d in sorted(op.deps, key=lambda o: -o.idx):
                if d.kind == "dma":
                    key = ("d",) + d.sem
                    val = d.val
                else:
                    if d.eng == "pe" and op.eng == "pe":
                        continue
                    key = d.eng
                    val = d.seq
                if kn.get(key, 0) >= val:
                    continue
                waits.append((key, val))
                if d.clock is not None:
                    for k, v in d.clock.items():
                        if kn.get(k, 0) < v:
                            kn[k] = v
                kn[key] = max(kn.get(key, 0), val)
            if op.prev is not None:
                key, val = op.prev
                if kn.get(key, 0) < val:
                    waits.append((key, val))
                    kn[key] = val
            op.waits = waits
            if op.kind == "dma" or op.sig:
                op.clock = dict(kn)
                if op.kind != "dma":
                    op.clock[op.eng] = op.seq

        def semof(key):
            if isinstance(key, tuple):
                return dsem[key[1]][key[2]]
            return engsem[key]

        per = {e: [o for o in self.ops if o.eng == e] for e in self.ENGS}

        def emit(ename, eng):
            for op in per[ename]:
                for key, val in op.waits:
                    eng.wait_ge(semof(key), val)
                if op.fn is None:
                    continue
                ins = op.fn(eng)
                if op.kind == "dma":
                    ins.then_inc(dsem[op.sem[0]][op.sem[1]], 16)
                elif op.sig:
                    ins.then_inc(engsem[ename], 1)

        block = stack.enter_context(nc.Block())

        @block.tensor
        def _(e):
            emit("pe", e)

        @block.scalar
        def _(e):
            emit("act", e)

        @block.vector
        def _(e):
            emit("dve", e)

        @block.gpsimd
        def _(e):
            emit("pool", e)

        @block.sync
        def _(e):
            emit("sp", e)


def _bucket_tables():
    dist = np.arange(0, 128)
    max_exact = 16
    ratio = np.log(np.maximum(dist, 1).astype(np.float32) / np.float32(max_exact)) / np.float32(math.log(128 / max_exact))
    large = max_exact + (ratio * np.float32(32 - max_exact)).astype(np.int32)
    large = np.minimum(large, 31)
    bucket = np.where(dist < max_exact, dist, large)
    T = np.zeros((33, 383), np.float32)
    for c in range(383):
        d = c - 127
        if 0 <= d < 128:
            T[bucket[d], c] = 1.0
        else:
            T[32, c] = 1.0
    Trev = np.ascontiguousarray(T[:, ::-1])
    return T, Trev


def I(method, *a, **kw):
    return lambda e: getattr(e, method)(*a, **kw)


def build(debug=(), stop_after=None):
    nc = bass.Bass("TRN2", target_bir_lowering=False)

    def din(name, shape):
        return nc.dram_tensor(name, list(shape), F32, kind="ExternalInput").ap()

    def dout(name, shape):
        return nc.dram_tensor(name, list(shape), F32, kind="ExternalOutput").ap()

    x_ext = din("x_ext", [EXT, D])
    ck = din("ck", [NSEQ, 128, 256])
    cv = din("cv", [NSEQ, 128, 256])
    sc = din("sc", [NSEQ * 2, 1024])
    w_in = din("w_in", [D, DIN])
    w_conv = din("w_conv", [3, 1024])
    w_co = din("w_co", [1024, D])
    w_ao = din("w_ao", [1024, D])
    w_o = din("w_o", [D, D])
    w_g = din("w_g", [D, DFF])
    w_u = din("w_u", [D, DFF])
    w_d = din("w_d", [DFF, D])
    g_mix = din("g_mix", [1, D])
    g_ffn = din("g_ffn", [1, D])
    g_fin = din("g_fin", [1, D])
    rel_ext = din("rel_ext", [33, 16])
    sinks = din("sinks", [1, 16])
    ttab = din("ttab", [33, 383])
    trev = din("trev", [33, 383])
    hsel_d = din("hsel", [128, 16])
    ident_d = din("ident", [128, 128])
    hmask_d = din("hmask", [128, 1])

    y_out = dout("y", [TR, D])
    knp_out = dout("knp", [128, 256])
    vnp_out = dout("vnp", [128, 256])
    cnp_out = dout("cnp", [2, 1024])
    ks_out = dout("ks", [NSEQ, 128, 256])
    vs_out = dout("vs", [NSEQ, 128, 256])
    cns_out = dout("cns", [NSEQ * 2, 1024])
    gscr = nc.dram_tensor("gscr", [16, 383], F32, kind="Internal").ap()

    SB_BYTES = 206 * 1024
    stack = ExitStack()
    with stack:
        sb_t = stack.enter_context(nc.sbuf_tensor("arena", [128, SB_BYTES // 2], BF16))
        ps_t = stack.enter_context(nc.psum_tensor("psum", [128, 8, 512], F32))
        P = Prog(nc, sb_t[:, :], SB_BYTES)
        banks = [BankTile(ps_t[:, b, :]) for b in range(8)]
        A = P.add

        def bcast_rows(ap2d, n):
            return bass.AP(ap2d.tensor, ap2d.offset, [[0, n]] + [list(x) for x in ap2d.ap[1:]])

        def dump(name, t, shape, dt=F32):
            if name not in debug:
                return
            o = nc.dram_tensor("dbg_" + name, list(shape), dt, kind="ExternalOutput").ap()
            P.dma("sp", o, t.ap, reads=t.r(), out_final=True)

        class Stop(Exception):
            pass

        def phase_end(name):
            if stop_after == name:
                raise Stop()

        def body():
            xt = [P.tile([D], F32, top=True) for _ in range(3)]
            for i in range(2):
                P.dma("sp", xt[i].ap, x_ext[i * 128:(i + 1) * 128, :], writes=xt[i].r())
            g_bc = P.tile([D], F32)
            P.dma("sp", g_bc.ap, bcast_rows(g_mix, 128), writes=g_bc.r())
            ident_f = P.tile([128], F32)
            ident_b = P.tile([128], BF16)
            P.dma("sp", ident_f.ap, ident_d, writes=ident_f.r())
            A("dve", I("tensor_copy", out=ident_b.ap, in_=ident_f.ap), ident_f.r(), ident_b.r())
            tt = P.tile([383], F32)
            tr = P.tile([383], F32)
            rel = P.tile([16], F32)
            P.dma("sp", tt.ap[0:33], ttab, writes=tt.r())
            P.dma("sp", tr.ap[0:33], trev, writes=tr.r())
            P.dma("sp", rel.ap[0:33], rel_ext, writes=rel.r())
            sink_bc = P.tile([16], F32)
            P.dma("sp", sink_bc.ap, bcast_rows(sinks, 128), writes=sink_bc.r())
            hsel = P.tile([16], F32)
            P.dma("sp", hsel.ap, hsel_d, writes=hsel.r())
            hmask = P.tile([1], F32)
            P.dma("sp", hmask.ap, hmask_d, writes=hmask.r())
            wcv = P.tile([3, 8], F32, nreg=3)
            stat = P.tile([64], F32, nreg=64)
            sink_col = P.tile([1], F32)
            tmp16 = P.tile([16], F32)
            A("dve", I("tensor_tensor", out=tmp16.ap, in0=sink_bc.ap, in1=hsel.ap, op=ALU.mult),
              sink_bc.r() + hsel.r(), tmp16.r())
            A("dve", I("tensor_reduce", out=sink_col.ap, in_=tmp16.ap, axis=AX.X, op=ALU.add), tmp16.r(), sink_col.r())
            NW = 6
            wslots = [P.tile([4096], BF16, nreg=4) for _ in range(NW)]
            wctr = [0]

            wgate = []

            def wslot():
                t = wslots[wctr[0] % NW]
                wctr[0] += 1
                return t

            def wdma(dst, src, writes):
                op = P.dma("pool", dst, src, writes=writes)
                if wgate and wctr[0] <= NW:
                    op.deps.add(wgate[0])
                return op

            def wload(src, r0, nk, c0, ncols):
                assert nk * ncols <= 4096
                t = wslot()
                dst = t.ap[:, 0:nk * ncols].rearrange("p (k n) -> p k n", k=nk)
                s = src[r0:r0 + nk * 128, c0:c0 + ncols].rearrange("(k p) n -> p k n", p=128)
                P.dma("pool", dst, s, writes=t.r())
                return [(dst[:, k, :], t.r()) for k in range(nk)]

            bias_s = P.tile([137], F32)
            R0 = (P.arena.free[0][0] + 1023) // 1024 * 1024
            assert R0 + 144 * 1024 <= SB_BYTES, R0

            def KB(x):
                return R0 + int(x * 1024)

            xnT = P.tile([KC, EXT], BF16, nreg=10, at=KB(0))

            bias_p = P.tile([16, 257], F32, nreg=128, at=KB(91))
            LB = KB(40)
            tt_b = P.tile([383], BF16, lo=LB)
            tr_b = P.tile([383], BF16, lo=LB)
            rel_h = P.tile([16], BF16, lo=LB)
            rel_l = P.tile([16], BF16, lo=LB)
            rel_t = P.tile([16], F32, lo=LB)
            A("dve", I("tensor_copy", out=tt_b.ap[0:33], in_=tt.ap[0:33]), tt.r(), tt_b.r())
            A("dve", I("tensor_copy", out=tr_b.ap[0:33], in_=tr.ap[0:33]), tr.r(), tr_b.r())
            A("dve", I("tensor_copy", out=rel_h.ap[0:33], in_=rel.ap[0:33]), rel.r(), rel_h.r())
            A("dve", I("tensor_copy", out=rel_t.ap[0:33], in_=rel_h.ap[0:33]), rel_h.r(), rel_t.r())
            A("dve", I("tensor_tensor", out=rel_t.ap[0:33], in0=rel.ap[0:33], in1=rel_t.ap[0:33], op=ALU.subtract),
              rel.r() + rel_t.r(), rel_t.r())
            A("dve", I("tensor_copy", out=rel_l.ap[0:33], in_=rel_t.ap[0:33]), rel_t.r(), rel_l.r())
            lts = [P.tile([8, 128], BF16, lo=LB) for _ in range(2)]
            for lt_, rl_ in zip(lts, (rel_h, rel_l)):
                A("dve", I("memset", lt_.ap[0:33], 0.0), (), lt_.r())
                for t_ in range(8):
                    A("dve", I("tensor_copy", out=lt_.ap[0:33, t_, :].rearrange("p (h t) -> p h t", t=8)[:, :, t_],
                               in_=rl_.ap[0:33, :]), rl_.r() + lt_.r(), lt_.r())

            def bias_chunk(r):
                def f():
                    bk = banks[4 + r % 4]
                    for sl in range(32):
                        s = r * 32 + sl
                        for j, rl_ in enumerate((rel_h, rel_l)):
                            A("pe", I("matmul", bk.ap[:, sl * 16:(sl + 1) * 16], lhsT=tt_b.ap[0:33, 255 - s:255 - s + 128],
                                      rhs=rl_.ap[0:33, :], start=(j == 0), stop=(j == 1)), tt_b.r() + rl_.r(), bk.r())
                    A("dve", I("tensor_copy", out=bias_p.ap[:, :, r * 32:r * 32 + 32],
                               in_=bk.ap.rearrange("p (s h) -> p h s", h=16)), bk.r(), bias_p.r())
                return f

            def bias_sample():
                bk = banks[4]
                for (c0, n, t0) in ((0, 128, 127), (128, 8, 255)):
                    for t_ in range(8):
                        for j, lt_ in enumerate(lts):
                            A("pe", I("matmul", bk.ap[:, c0:c0 + n], lhsT=lt_.ap[0:33, t_, :], rhs=tr_b.ap[0:33, t0 - t_:t0 - t_ + n],
                                      start=(t_ == 0 and j == 0), stop=(t_ == 7 and j == 1)), lt_.r() + tr_b.r(), bk.r())
                A("dve", I("tensor_copy", out=bias_s.ap[:, 0:128], in_=bk.ap[:, 0:128]), bk.r(), bias_s.r())
                A("dve", I("tensor_copy", out=bias_s.ap[:, 129:137], in_=bk.ap[:, 128:136]), bk.r() + bias_s.r(), bias_s.r())
                A("dve", I("tensor_copy", out=bias_s.ap[:, 128:129], in_=sink_col.ap), sink_col.r() + bias_s.r(), bias_s.r())

            gsb = P.tile([383], F32, lo=LB)
            bkg = banks[7]
            for j, rl_ in enumerate((rel_h, rel_l)):
                A("pe", I("matmul", bkg.ap[0:16, 0:383], lhsT=rl_.ap[0:33, :], rhs=tr_b.ap[0:33, 0:383], start=(j == 0), stop=(j == 1)),
                  rl_.r() + tr_b.r(), bkg.r())
            A("dve", I("tensor_copy", out=gsb.ap[0:16], in_=bkg.ap[0:16, 0:383]), bkg.r(), gsb.r())
            gst = P.dma("sp", gscr, gsb.ap[0:16], reads=gsb.r())

            def bias_rows():
                for q in range(128):
                    src = bass.AP(gscr.tensor, 127 - q, [[383 * 16, 1], [383, 16], [1, 256]])
                    op = P.dma("sp", bias_p.ap[q:q + 1, :, 0:256], src, writes=bias_p.r(q))
                    op.deps.add(gst)

            bias_work = [bias_sample]

            def norm_transpose(src_fn, ntiles, dstT, stat0, lo, extra=()):
                xn_b = [P.tile([D], BF16, lo=lo) for _ in range(2)]
                junk = P.tile([D], BF16, lo=lo)
                srcs = {}

                def N0(i):
                    srcs[i] = src_fn(i)

                def N1a(i):
                    xap, xregs = srcs[i]
                    ss = stat.ap[:, stat0 + i:stat0 + i + 1]
                    ssr = stat.r(stat0 + i)
                    A("act", I("activation", out=junk.ap, in_=xap, func=AF.Square, accum_out=ss), xregs, junk.r() + ssr)
                    A("act", I("activation", out=ss, in_=ss, func=AF.Sqrt, scale=1.0 / D, bias=EPS), ssr, ssr)

                def N1b(i):
                    ss = stat.ap[:, stat0 + i:stat0 + i + 1]
                    ssr = stat.r(stat0 + i)
                    A("dve", I("reciprocal", out=ss, in_=ss), ssr, ssr)

                def N2(i):
                    xap, xregs = srcs[i]
                    ss = stat.ap[:, stat0 + i:stat0 + i + 1]
                    ssr = stat.r(stat0 + i)
                    xb = xn_b[i % 2]
                    A("dve", I("scalar_tensor_tensor", out=xb.ap, in0=xap, scalar=ss, in1=g_bc.ap, op0=ALU.mult, op1=ALU.mult),
                      xregs + ssr + g_bc.r(), xb.r())
                    pb = (banks[0], banks[1]) if i % 2 == 0 else (banks[2], banks[3])
                    for c in range(KC):
                        bk_ = pb[c // 8]
                        o = bk_.ap.bitcast(BF16)[:, (c % 8) * 128:(c % 8 + 1) * 128]
                        A("pe", I("transpose", out=o, in_=xb.ap[:, c * 128:(c + 1) * 128], identity=ident_b.ap),
                          xb.r() + ident_b.r(), bk_.r())

                def N3(i):
                    pb = (banks[0], banks[1]) if i % 2 == 0 else (banks[2], banks[3])
                    col = i * 128
                    for hf in range(2):
                        bk_ = pb[hf]
                        src = bk_.ap.bitcast(BF16).rearrange("p (c t) -> p c t", c=8)
                        dst = dstT.ap[:, hf * 8:(hf + 1) * 8, col:col + 128]
                        if hf == 0:
                            A("act", I("activation", out=dst, in_=src, func=AF.Identity), bk_.r(), dstT.r(i))
                        else:
                            A("dve", I("tensor_copy", out=dst, in_=src), bk_.r(), dstT.r(i))

                N0(0)
                if ntiles > 1:
                    N0(1)
                N1a(0)
                N1b(0)
                for i in range(ntiles + 1):
                    if i + 2 < ntiles:
                        N0(i + 2)
                    if i + 1 < ntiles:
                        N1a(i + 1)
                    if i < ntiles:
                        N2(i)
                    if i + 1 < ntiles:
                        N1b(i + 1)
                    if i < len(extra):
                        extra[i]()
                    if i >= 1:
                        N3(i - 1)
                for j in range(ntiles + 1, len(extra)):
                    extra[j]()
                P.free(junk)
                for t_ in xn_b:
                    P.free(t_)

            xload_ops = []

            def src1(i):
                t = xt[i % 3]
                if i >= 2:
                    xload_ops.append(P.dma("sp", t.ap, x_ext[i * 128:(i + 1) * 128, :], writes=t.r()))
                return t.ap, t.r()

            norm_transpose(src1, 10, xnT, 0, KB(40), extra=bias_work)
            for t_ in [tt_b, tr_b, rel_h, rel_l, rel_t, gsb] + lts:
                P.free(t_)
            dump("bias_p", bias_p, [128, 16, 257])
            dump("bias_s", bias_s, [128, 137])
            for t_ in xt:
                P.free(t_)
            dump("xnT", xnT, [128, KC, EXT], BF16)
            phase_end("p1")

            GA = (128, 512, [1, 2, 3, 4])
            GB = (640, 512, [5, 6, 7, 8])
            GS = (1152, 128, [9])
            GROUPS = [GA, GB, GS]
            RG = [(0, 512, [0, 1, 2, 3]), (512, 512, [4, 5, 6, 7]), (1024, 128, [8])]
            bank_rr = [0]

            def next_bank():
                b = banks[bank_rr[0] % 8]
                bank_rr[0] += 1
                return b

            def fm_mm(bk, col0, wk, m0, mw, acts, n):
                regs = bk.r(*[q for q in range(4) if q * 128 < col0 + n and (q + 1) * 128 > col0])
                nk = len(wk)
                for k in range(nk):
                    A("pe", I("matmul", bk.ap[0:mw, col0:col0 + n], lhsT=wk[k][0][:, m0:m0 + mw], rhs=acts[k][0],
                              start=(k == 0), stop=(k == nk - 1)), wk[k][1] + acts[k][1], regs)
                return regs

            def xn_acts(e0, n, xr):
                return [(xnT.ap[:, k, e0:e0 + n], xnT.r(*xr)) for k in range(KC)]

            for j in range(3):
                A("sp", I("dma_start", out=wcv.ap[:, j, :], in_=w_conv[j:j + 1, :].rearrange("o (c p) -> p (o c)", p=128),
                          allow_slow_non_contiguous=True), (), wcv.r(j), kind="dma")
            aT = P.tile([8, TR], BF16, nreg=24, at=KB(40))
            qT = P.tile([8, TR], BF16, nreg=24, at=KB(58))
            kT = P.tile([4, EXT], BF16, nreg=12, at=KB(76))
            v_tm = P.tile([10, 256], BF16, nreg=10, at=KB(86))
            L2 = KB(108)
            ubuf = P.tile([TP + 2], F32, nreg=3, lo=L2)
            us = P.tile([NSEQ, 10], F32, lo=L2)
            csb = [P.tile([512], F32, lo=L2) for _ in range(2)]
            t1b = [P.tile([512], F32, lo=L2) for _ in range(2)]
            t2b = [P.tile([512], F32, lo=L2) for _ in range(2)]
            uo_p = P.tile([8, 2], F32, lo=L2)
            uo_s = P.tile([8, 32], F32, lo=L2)
            scT = P.tile([8, 32], F32, lo=L2)
            sct = P.tile([1024], F32, lo=L2)
            P.dma("sp", sct.ap[0:32], sc, writes=sct.r())
            bias_rows()
            bk = next_bank()
            for c in range(8):
                A("pe", I("transpose", out=bk.ap[:, c * 32:(c + 1) * 32], in_=sct.ap[0:32, c * 128:(c + 1) * 128],
                          identity=ident_f.ap[0:32, 0:32]), sct.r() + ident_f.r(), bk.r())
            A("dve", I("tensor_copy", out=scT.ap, in_=bk.ap[:, 0:256].rearrange("p (c j) -> p c j", c=8)), bk.r(), scT.r())
            phase_end("p2a_sc")

            def v3(ap):
                return ap.rearrange("p (s t) -> p s t", t=8)

            it = 0
            for c in range(8):
                wk = []
                for kh in range(2):
                    t = wslot()
                    dst4 = t.ap[:, 0:8 * 384].rearrange("p (k j n) -> p k j n", k=8, j=3)
                    for j in range(3):
                        src = w_in[kh * 1024:(kh + 1) * 1024, j * 1024 + c * 128:j * 1024 + (c + 1) * 128].rearrange("(k p) n -> p k n", p=128)
                        wop = P.dma("pool", dst4[:, :, j, :], src, writes=t.r(j) if j < 2 else t.r(2, 3))
                        if c == 0 and kh == 0 and j == 0:
                            wop.deps.add(xload_ops[3])
                    dst3 = t.ap[:, 0:8 * 384].rearrange("p (k n) -> p k n", k=8)
                    wk += [(dst3[:, k, :], t.r()) for k in range(8)]
                if c == 0 and stop_after == "p2a_w0x":
                    A("dve", I("tensor_copy", out=t1b[0].ap[:, 0:128], in_=wk[0][0][:, 0:128]), wk[0][1] + wk[8][1], t1b[0].r())
                    phase_end("p2a_w0x")
                if c == 0:
                    phase_end("p2a_w0")
                bh = next_bank()
                ha = xn_acts(126, 2, [0])
                bh2 = next_bank()
                fm_mm(bh, 126, wk, 128, 128, ha, 2)
                fm_mm(bh2, 254, wk, 256, 128, ha, 2)
                if c == 0 and stop_after == "p2a_mm0w":
                    A("sp", None, bh.r(), ())
                    phase_end("p2a_mm0w")
                if c == 0:
                    phase_end("p2a_mm0")
                A("dve", I("tensor_copy", out=csb[0].ap[:, 126:128], in_=bh.ap[:, 126:128]), bh.r(0), csb[0].r())
                if c == 0:
                    phase_end("p2a_act0")
                A("dve", I("tensor_tensor", out=ubuf.ap[:, 0:2], in0=csb[0].ap[:, 126:128], in1=bh2.ap[:, 254:256], op=ALU.mult),
                  csb[0].r() + bh2.r(1), ubuf.r(0))
                if c == 0:
                    phase_end("p2a_c0h")
                for gi, (e0, n, xr) in enumerate((GA, GB)):
                    bB, bC, bH = next_bank(), next_bank(), next_bank()
                    acts = xn_acts(e0, n, xr)
                    fm_mm(bB, 0, wk, 0, 128, acts, n)
                    fm_mm(bC, 0, wk, 128, 128, acts, n)
                    fm_mm(bH, 0, wk, 256, 128, acts, n)
                    cs_, t1_, t2_ = csb[it % 2], t1b[it % 2], t2b[it % 2]
                    it += 1
                    off = gi * 512
                    A("act", I("activation", out=cs_.ap, in_=bC.ap, func=AF.Identity), bC.r(), cs_.r())
                    A("dve", I("tensor_tensor", out=ubuf.ap[:, 2 + off:2 + off + 512], in0=cs_.ap, in1=bH.ap, op=ALU.mult),
                      cs_.r() + bH.r(), ubuf.r(1 + gi))
                    ur = ubuf.r(0, 1) if gi == 0 else ubuf.r(1, 2)
                    A("act", I("activation", out=t1_.ap, in_=ubuf.ap[:, off:off + 512], func=AF.Identity, scale=wcv.ap[:, 0, c:c + 1]),
                      ur + wcv.r(), t1_.r())
                    A("dve", I("scalar_tensor_tensor", out=t2_.ap, in0=ubuf.ap[:, off + 1:off + 513], scalar=wcv.ap[:, 1, c:c + 1],
                               in1=t1_.ap, op0=ALU.mult, op1=ALU.add), ur + wcv.r() + t1_.r(), t2_.r())
                    A("dve", I("scalar_tensor_tensor", out=t1_.ap, in0=ubuf.ap[:, off + 2:off + 514], scalar=wcv.ap[:, 2, c:c + 1],
                               in1=t2_.ap, op0=ALU.mult, op1=ALU.add), ur + wcv.r() + t2_.r(), t1_.r())
                    A("dve", I("tensor_tensor", out=aT.ap[:, c, off:off + 512], in0=t1_.ap, in1=bB.ap, op=ALU.mult),
                      t1_.r() + bB.r(), aT.r(c * 3 + gi))
                A("dve", I("tensor_copy", out=uo_p.ap[:, c, :], in_=ubuf.ap[:, TP:TP + 2]), ubuf.r(2), uo_p.r())
                if c == 0:
                    phase_end("p2a_c0g")
                e0, n, xr = GS
                bS = next_bank()
                acts = xn_acts(e0, n, xr)
                fm_mm(bS, 0, wk, 0, 128, acts, n)
                fm_mm(bS, 128, wk, 128, 128, acts, n)
                fm_mm(bS, 256, wk, 256, 128, acts, n)
                cs_, t1_, t2_ = csb[it % 2], t1b[it % 2], t2b[it % 2]
                it += 1
                A("act", I("activation", out=cs_.ap[:, 0:128], in_=bS.ap[:, 128:256], func=AF.Identity), bS.r(1), cs_.r())
                A("dve", I("tensor_copy", out=us.ap[:, :, 0:2], in_=scT.ap[:, c, :].rearrange("p (s j) -> p s j", j=2)),
                  scT.r(), us.r())
                A("dve", I("tensor_tensor", out=us.ap[:, :, 2:10], in0=v3(cs_.ap[:, 0:128]), in1=v3(bS.ap[:, 256:384]), op=ALU.mult),
                  cs_.r() + bS.r(2), us.r())
                A("act", I("activation", out=v3(t1_.ap[:, 0:128]), in_=us.ap[:, :, 0:8], func=AF.Identity, scale=wcv.ap[:, 0, c:c + 1]),
                  us.r() + wcv.r(), t1_.r())
                A("dve", I("scalar_tensor_tensor", out=v3(t2_.ap[:, 0:128]), in0=us.ap[:, :, 1:9], scalar=wcv.ap[:, 1, c:c + 1],
                           in1=v3(t1_.ap[:, 0:128]), op0=ALU.mult, op1=ALU.add), us.r() + wcv.r() + t1_.r(), t2_.r())
                A("dve", I("scalar_tensor_tensor", out=v3(t1_.ap[:, 0:128]), in0=us.ap[:, :, 2:10], scalar=wcv.ap[:, 2, c:c + 1],
                           in1=v3(t2_.ap[:, 0:128]), op0=ALU.mult, op1=ALU.add), us.r() + wcv.r() + t2_.r(), t1_.r())
                A("dve", I("tensor_tensor", out=aT.ap[:, c, TP:TP + 128], in0=t1_.ap[:, 0:128], in1=bS.ap[:, 0:128], op=ALU.mult),
                  t1_.r() + bS.r(0), aT.r(c * 3 + 2))
                A("dve", I("tensor_copy", out=uo_s.ap[:, c, :].rearrange("p (s j) -> p s j", j=2), in_=us.ap[:, :, 8:10]),
                  us.r(), uo_s.r())
                if c == 0:
                    phase_end("p2a_c0")
            phase_end("p2a_conv")
            for (uo, npart, dst_out) in ((uo_p, 2, cnp_out), (uo_s, 32, cns_out)):
                cst = P.tile([1024], F32, lo=L2)
                bka, bkb = next_bank(), next_bank()
                for c in range(8):
                    bk_ = bka if c < 4 else bkb
                    A("pe", I("transpose", out=bk_.ap[0:npart, (c % 4) * 128:(c % 4 + 1) * 128], in_=uo.ap[:, c, :],
                              identity=ident_f.ap), uo.r() + ident_f.r(), bk_.r())
                A("dve", I("tensor_copy", out=cst.ap[0:npart, 0:512], in_=bka.ap[0:npart, :]), bka.r(), cst.r())
                A("dve", I("tensor_copy", out=cst.ap[0:npart, 512:1024], in_=bkb.ap[0:npart, :]), bkb.r(), cst.r())
                P.dma("sp", dst_out, cst.ap[0:npart, :], reads=cst.r(), out_final=True)
                P.free(cst)
            for t_ in csb + t1b + t2b + [ubuf, us, uo_p, uo_s, scT, sct]:
                P.free(t_)
            dump("aT", aT, [128, 8, TR], BF16)
            phase_end("p2a")

            for qp in range(4):
                wk = wload(w_in, 0, KC, 3072 + qp * 256, 256)
                for ml in range(2):
                    m = qp * 2 + ml
                    for gi, (e0, n, xr) in enumerate(GROUPS):
                        bk = next_bank()
                        regs = fm_mm(bk, 0, wk, ml * 128, 128, xn_acts(e0, n, xr), n)
                        A("act", I("activation", out=qT.ap[:, m, e0 - 128:e0 - 128 + n], in_=bk.ap[:, 0:n], func=AF.Identity, scale=0.125),
                          regs, qT.r(m * 3 + gi))
            EG = [(0, 512, [0, 1, 2, 3]), (512, 512, [4, 5, 6, 7]), (1024, 256, [8, 9])]
            for kp in range(2):
                t = wslot()
                w5 = t.ap.rearrange("p (k h u d) -> p k h u d", k=KC, h=2, u=2)
                for u in range(2):
                    for hh in range(2):
                        src = w_in[:, 4096 + kp * 128 + hh * 64:4096 + kp * 128 + (hh + 1) * 64].rearrange("(k p) n -> p k n", p=128)
                        P.dma("pool", w5[:, :, hh, u, :], src, writes=t.r(u * 2 + hh))
                w3 = t.ap.rearrange("p (k n) -> p k n", k=KC)
                wk = [(w3[:, k, :], t.r()) for k in range(KC)]
                for hl in range(2):
                    kvh = kp * 2 + hl
                    for gi, (e0, n, xr) in enumerate(EG):
                        bk = next_bank()
                        regs = fm_mm(bk, 0, wk, hl * 128, 128, xn_acts(e0, n, xr), n)
                        A("dve", I("tensor_copy", out=kT.ap[:, kvh, e0:e0 + n], in_=bk.ap[:, 0:n]), regs, kT.r(kvh * 3 + gi))
            phase_end("p2b")
            wk = wload(w_in, 0, 8, 4096, 512) + wload(w_in, 1024, 8, 4096, 512)
            kvo = [P.tile([512], F32, lo=L2) for _ in range(2)]
            for i in range(10):
                bk = next_bank()
                for kc in range(KC):
                    A("pe", I("matmul", bk.ap, lhsT=xnT.ap[:, kc, i * 128:(i + 1) * 128], rhs=wk[kc][0],
                              start=(kc == 0), stop=(kc == KC - 1)), wk[kc][1] + xnT.r(i), bk.r())
                if i == 0 and stop_after == "p2c_mm0":
                    A("sp", None, bk.r(), ())
                    phase_end("p2c_mm0")
                A("act", I("activation", out=v_tm.ap[:, i, :], in_=bk.ap[:, 256:512], func=AF.Identity), bk.r(2, 3), v_tm.r(i))
                if i == 0:
                    phase_end("p2c_i0")
                if i == 7:
                    phase_end("p2c_i7")
                if i == 8:
                    ko = kvo[0]
                    A("dve", I("tensor_copy", out=ko.ap, in_=bk.ap), bk.r(), ko.r())
                    P.dma("sp", knp_out, ko.ap[:, 0:256], reads=ko.r(), out_final=True)
                    P.dma("sp", vnp_out, ko.ap[:, 256:512], reads=ko.r(), out_final=True)
                if i == 9:
                    ko = kvo[1]
                    A("dve", I("tensor_copy", out=ko.ap, in_=bk.ap), bk.r(), ko.r())
                    import os
                    for s in range(NSEQ if not os.environ.get("SKIP_SMALL") else 0):
                        P.dma("sp", ks_out[s, 120:128, :], ko.ap[s * 8:(s + 1) * 8, 0:256], reads=ko.r(), out_final=True)
                        P.dma("sp", vs_out[s, 120:128, :], ko.ap[s * 8:(s + 1) * 8, 256:512], reads=ko.r(), out_final=True)
            phase_end("p2c0")
            P.dma("sp", ks_out[:, 0:120, :], ck[:, 8:128, :], out_final=True)
            P.dma("sp", vs_out[:, 0:120, :], cv[:, 8:128, :], out_final=True)
            for t_ in kvo:
                P.free(t_)
            dump("qT", qT, [128, 8, TR], BF16)
            dump("kT", kT, [128, 4, EXT], BF16)
            dump("v_tm", v_tm, [128, 10, 256], BF16)
            phase_end("p2")

            def load_pair(pair):
                wgc = wload(w_in, 0, KC, 4608 + pair * 256, 256)
                wga = wload(w_in, 0, KC, 6656 + pair * 256, 256)
                t = wslot()
                vy = t.ap.rearrange("p (k n) -> p k n", k=KC)
                P.dma("pool", vy[:, 0:8, :], w_co[:, pair * 256:(pair + 1) * 256].rearrange("(k p) n -> p k n", p=128), writes=t.r(0, 2))
                P.dma("pool", vy[:, 8:16, :], w_ao[:, pair * 256:(pair + 1) * 256].rearrange("(k p) n -> p k n", p=128), writes=t.r(1, 3))
                wyc = [(vy[:, k, :], t.r(0, 2)) for k in range(8)]
                wya = [(vy[:, 8 + k, :], t.r(1, 3)) for k in range(8)]
                return wgc, wga, wyc, wya

            pair_w = {0: load_pair(0), 1: load_pair(1)}

            attnT = P.tile([8, TR], BF16, nreg=72, at=KB(108))
            NSB = 4
            L3 = KB(126)
            sbs = [P.tile([257], F32, lo=L3) for _ in range(NSB)]
            pbs = [P.tile([257], BF16, lo=L3) for _ in range(NSB)]
            pts = [P.tile([256], BF16, lo=L3) for _ in range(NSB)]
            atm = [P.tile([8, 128], BF16, nreg=8, lo=L3) for _ in range(2)]
            tiles = [(2 * m + p, blk) for m in range(8) for p in range(2) for blk in range(1, 9)]
            NT = len(tiles)

            def tp_(k):
                h, blk = tiles[k]
                return h, blk, h // 4, h // 2, h % 2, k % NSB, k % 2

            def stA(k):
                h, blk, kvh, m, p, sl, pslot = tp_(k)
                sbank = banks[pslot]
                ps = slice(p * 64, (p + 1) * 64)
                kc0 = (blk - 1) * 128
                g0 = kc0 // 512 if kc0 < 1024 else 2
                g1 = (kc0 + 255) // 512 if kc0 + 255 < 1024 else 2
                kregs = kT.r(*sorted({kvh * 3 + g0, kvh * 3 + g1}))
                gq = 0 if blk <= 4 else 1
                A("pe", I("matmul", sbank.ap[:, 0:256], lhsT=qT.ap[ps, m, (blk - 1) * 128:blk * 128],
                          rhs=kT.ap[ps, kvh, kc0:kc0 + 256], start=True, stop=True), qT.r(m * 3 + gq) + kregs, sbank.r())

            def stB1(k):
                h, blk, kvh, m, p, sl, pslot = tp_(k)
                sbank = banks[pslot]
                sb_ = sbs[sl]
                A("dve", I("tensor_tensor", out=sb_.ap[:, 0:256], in0=sbank.ap[:, 0:256], in1=bias_p.ap[:, h, 0:256], op=ALU.add),
                  sbank.r() + bias_p.r(), sb_.r())
                if blk == 1:
                    A("dve", I("tensor_scalar", out=sb_.ap[:, 0:128], in0=sb_.ap[:, 0:128], scalar1=hmask.ap[:, 0:1], scalar2=None,
                               op0=ALU.add), sb_.r() + hmask.r(), sb_.r())
                if k % 8 < NSB:
                    A("dve", I("tensor_copy", out=sb_.ap[:, 256:257], in_=sink_bc.ap[:, h:h + 1]), sink_bc.r() + sb_.r(), sb_.r())

            def stB2(k):
                h, blk, kvh, m, p, sl, pslot = tp_(k)
                sb_, pb_ = sbs[sl], pbs[sl]
                mx, mxr = stat.ap[:, 32 + sl:33 + sl], stat.r(32 + sl)
                rs, rsr = stat.ap[:, 40 + sl:41 + sl], stat.r(40 + sl)
                A("dve", I("tensor_reduce", out=mx, in_=sb_.ap, axis=AX.X, op=ALU.max, negate=True), sb_.r(), mxr)
                A("act", I("activation", out=pb_.ap, in_=sb_.ap, func=AF.Exp, bias=mx, accum_out=rs), sb_.r() + mxr, pb_.r() + rsr)

            def stC(k):
                h, blk, kvh, m, p, sl, pslot = tp_(k)
                pb_ = pbs[sl]
                tb = banks[2 + pslot]
                tq = tb.ap.bitcast(BF16)[:, 0:256]
                for j in range(2):
                    A("pe", I("transpose", out=tq[:, j * 128:(j + 1) * 128], in_=pb_.ap[:, j * 128:(j + 1) * 128], identity=ident_b.ap),
                      pb_.r() + ident_b.r(), tb.r())

            def stD(k):
                h, blk, kvh, m, p, sl, pslot = tp_(k)
                tb = banks[2 + pslot]
                A("act", I("activation", out=pts[sl].ap, in_=tb.ap.bitcast(BF16)[:, 0:256], func=AF.Identity), tb.r(), pts[sl].r())

            def stE(k):
                h, blk, kvh, m, p, sl, pslot = tp_(k)
                pt_ = pts[sl]
                ob = banks[4 + pslot]
                oq = ob.ap[:, 0:64]
                A("pe", I("matmul", oq, lhsT=pt_.ap[:, 0:128], rhs=v_tm.ap[:, blk - 1, kvh * 64:(kvh + 1) * 64], start=True, stop=False),
                  pt_.r() + v_tm.r(blk - 1), ob.r())
                A("pe", I("matmul", oq, lhsT=pt_.ap[:, 128:256], rhs=v_tm.ap[:, blk, kvh * 64:(kvh + 1) * 64], start=False, stop=True),
                  pt_.r() + v_tm.r(blk), ob.r())

            def stF(k):
                h, blk, kvh, m, p, sl, pslot = tp_(k)
                rs, rsr = stat.ap[:, 40 + sl:41 + sl], stat.r(40 + sl)
                ob = banks[4 + pslot]
                at_ = atm[m % 2]
                A("dve", I("reciprocal", out=rs, in_=rs), rsr, rsr)
                A("act", I("activation", out=at_.ap[:, blk - 1, p * 64:(p + 1) * 64], in_=ob.ap[:, 0:64], func=AF.Identity, scale=rs),
                  ob.r() + rsr, at_.r(blk - 1))
                if k % 16 == 15:
                    tb = banks[6 + m % 2]
                    for b_ in range(8):
                        A("pe", I("transpose", out=tb.ap.bitcast(BF16)[:, b_ * 128:(b_ + 1) * 128], in_=at_.ap[:, b_, :],
                                  identity=ident_b.ap), at_.r(b_) + ident_b.r(), tb.r())
                    A("act", I("activation", out=attnT.ap[:, m, 0:TP], in_=tb.ap.bitcast(BF16), func=AF.Identity),
                      tb.r(), attnT.r(*[m * 9 + i for i in range(8)]))

            for i in range(NT + 4):
                if i < NT:
                    stA(i)
                if 0 <= i - 1 < NT:
                    stB1(i - 1)
                if 0 <= i - 2 < NT:
                    stB2(i - 2)
                if 0 <= i - 3 < NT:
                    stC(i - 3)
                    stD(i - 3)
                if 0 <= i - 4 < NT:
                    stE(i - 4)
                    stF(i - 4)
            for t_ in sbs + pbs + pts + atm + [bias_p]:
                P.free(t_)
            dump("attnT_p", attnT, [128, 8, TR], BF16)
            phase_end("p3a")

            L3b = KB(91)
            NB4 = 4
            ckb = [P.tile([4, 128], BF16, nreg=2, lo=L3b) for _ in range(NB4)]
            vsb = [P.tile([256], BF16, lo=L3b) for _ in range(NB4)]
            kts = [P.tile([4, 137], BF16, nreg=2, lo=L3b) for _ in range(NB4)]
            st1 = [P.tile([128], F32, nreg=2, lo=L3b) for _ in range(NB4)]
            st2 = [P.tile([128], F32, nreg=2, lo=L3b) for _ in range(NB4)]
            sbq = [P.tile([137], F32, lo=L3b) for _ in range(NB4)]
            pq = [P.tile([129], BF16, lo=L3b) for _ in range(NB4)]
            pn = P.tile([NSEQ, 128], BF16, nreg=NSEQ, lo=L3b)
            ptq = [P.tile([256], BF16, lo=L3b) for _ in range(NB4)]
            asd = [P.tile([128], BF16, nreg=4, lo=L3b) for _ in range(NB4)]
            osb = [P.tile([256], F32, lo=L3b) for _ in range(NB4)]
            sstat = P.tile([NB4, 4], F32, nreg=NB4 * 4, lo=L3b)
            A("dve", I("memset", pn.ap, 0.0), (), pn.r())
            tb = banks[2]
            b5, b6, b7 = banks[5], banks[6], banks[7]
            A("dve", I("memset", b7.ap[:, 128:129], 0.0), (), b7.r())

            def hv(ap):
                return ap.rearrange("p (m q t) -> p m q t", m=8, q=2)

            def sst(b4, j):
                return sstat.ap[:, b4, j:j + 1], sstat.r(b4 * 4 + j)

            tb3, obk, tb1 = banks[3], banks[4], banks[1]

            def X1a(s):
                b4 = s % NB4
                ckv = ckb[b4].ap.rearrange("p h (u d) -> p h u d", u=2)
                for u in range(2):
                    P.dma("pool", ckv[:, :, u, :], ck[s].rearrange("p (h d) -> p h d", h=4), writes=ckb[b4].r(u))
                P.dma("pool", vsb[b4].ap, cv[s], writes=vsb[b4].r())
                for kvh in range(4):
                    A("pe", I("transpose", out=tb.ap.bitcast(BF16)[:, kvh * 128:(kvh + 1) * 128], in_=ckb[b4].ap[:, kvh, :],
                              identity=ident_b.ap), ckb[b4].r() + ident_b.r(), tb.r())

            def X1b(s):
                b4 = s % NB4
                A("act", I("activation", out=kts[b4].ap[:, :, 0:128],
                           in_=tb.ap.bitcast(BF16)[:, 0:512].rearrange("p (h k) -> p h k", h=4), func=AF.Identity), tb.r(), kts[b4].r(0))
                A("dve", I("tensor_copy", out=kts[b4].ap[:, :, 129:137], in_=kT.ap[:, :, 1152 + s * 8:1152 + s * 8 + 8]),
                  kT.r(2, 5, 8, 11), kts[b4].r(1))

            def hv5(ap):
                return ap.rearrange("p (k g q t) -> p k g q t", k=4, g=2, q=2)

            def X2a(s):
                b4 = s % NB4
                for c0, kc in ((0, slice(0, 128)), (128, slice(129, 137))):
                    npart = 128 if c0 == 0 else 8
                    for kvh in range(4):
                        for p in range(2):
                            ps = slice(p * 64, (p + 1) * 64)
                            bq = b6 if p == 0 else b5
                            A("pe", I("matmul", hv5(bq.ap[0:npart, c0:c0 + 128])[:, kvh, :, p, :], lhsT=kts[b4].ap[ps, kvh, kc],
                                      rhs=qT.ap[ps, 2 * kvh:2 * kvh + 2, TP + s * 8:TP + s * 8 + 8], start=True, stop=True),
                              kts[b4].r() + qT.r((2 * kvh) * 3 + 2, (2 * kvh + 1) * 3 + 2), bq.r())

            def X2b(s):
                b4 = s % NB4
                A("act", I("activation", out=hv(st1[b4].ap)[:, :, 0, :], in_=hv(b6.ap[:, 0:128])[:, :, 0, :], func=AF.Identity),
                  b6.r(), st1[b4].r(0))
                A("dve", I("tensor_copy", out=hv(st1[b4].ap)[:, :, 1, :], in_=hv(b5.ap[:, 0:128])[:, :, 1, :]),
                  b5.r(), st1[b4].r(1))
                A("act", I("activation", out=hv(st2[b4].ap[0:8])[:, :, 0, :], in_=hv(b6.ap[0:8, 128:256])[:, :, 0, :], func=AF.Identity),
                  b6.r(), st2[b4].r(0))
                A("dve", I("tensor_copy", out=hv(st2[b4].ap[0:8])[:, :, 1, :], in_=hv(b5.ap[0:8, 128:256])[:, :, 1, :]),
                  b5.r(), st2[b4].r(1))

            def X3a(s):
                b4 = s % NB4
                A("pe", I("transpose", out=b7.ap[:, 0:128], in_=st1[b4].ap, identity=ident_f.ap), st1[b4].r() + ident_f.r(), b7.r())
                A("pe", I("transpose", out=b7.ap[:, 129:137], in_=st2[b4].ap[0:8, :], identity=ident_f.ap[0:8, 0:8]),
                  st2[b4].r() + ident_f.r(), b7.r())

            def X3b1(s):
                b4 = s % NB4
                (mx, mxr) = sst(b4, 0)
                A("dve", I("tensor_tensor", out=sbq[b4].ap, in0=b7.ap[:, 0:137], in1=bias_s.ap, op=ALU.add), b7.r() + bias_s.r(), sbq[b4].r())
                A("dve", I("tensor_reduce", out=mx, in_=sbq[b4].ap, axis=AX.X, op=ALU.max, negate=True), sbq[b4].r(), mxr)

            def X3b2(s):
                b4 = s % NB4
                (mx, mxr), (r1, r1r), (r2, r2r) = sst(b4, 0), sst(b4, 1), sst(b4, 2)
                A("act", I("activation", out=pq[b4].ap, in_=sbq[b4].ap[:, 0:129], func=AF.Exp, bias=mx, accum_out=r1),
                  sbq[b4].r() + mxr, pq[b4].r() + r1r)
                A("act", I("activation", out=pn.ap[:, s, s * 8:s * 8 + 8], in_=sbq[b4].ap[:, 129:137], func=AF.Exp, bias=mx, accum_out=r2),
                  sbq[b4].r() + mxr, pn.r(s) + r2r)

            def Y1a(s):
                b4 = s % NB4
                tq = tb3.ap.bitcast(BF16)[:, 0:256]
                A("pe", I("transpose", out=tq[:, 0:128], in_=pq[b4].ap[:, 0:128], identity=ident_b.ap), pq[b4].r() + ident_b.r(), tb3.r())
                A("pe", I("transpose", out=tq[:, 128:256], in_=pn.ap[:, s, :], identity=ident_b.ap), pn.r(s) + ident_b.r(), tb3.r())

            def Y1b1(s):
                b4 = s % NB4
                A("act", I("activation", out=ptq[b4].ap, in_=tb3.ap.bitcast(BF16)[:, 0:256], func=AF.Identity), tb3.r(), ptq[b4].r())

            def Y1b2(s):
                b4 = s % NB4
                A("pe", I("matmul", obk.ap[:, 0:256], lhsT=ptq[b4].ap[:, 0:128], rhs=vsb[b4].ap, start=True, stop=False),
                  ptq[b4].r() + vsb[b4].r(), obk.r())
                A("pe", I("matmul", obk.ap[:, 0:256], lhsT=ptq[b4].ap[:, 128:256], rhs=v_tm.ap[:, 9, :], start=False, stop=True),
                  ptq[b4].r() + v_tm.r(9), obk.r())

            def Y2a(s):
                b4 = s % NB4
                (r1, r1r), (r2, r2r) = sst(b4, 1), sst(b4, 2)
                A("act", I("activation", out=osb[b4].ap, in_=obk.ap[:, 0:256], func=AF.Identity), obk.r(), osb[b4].r())
                A("dve", I("tensor_tensor", out=r1, in0=r1, in1=r2, op=ALU.add), r1r + r2r, r1r)
                A("dve", I("reciprocal", out=r1, in_=r1), r1r, r1r)

            def Y2b1(s):
                b4 = s % NB4
                (r1, r1r) = sst(b4, 1)
                asv = asd[b4].ap.rearrange("p (u d) -> p u d", u=2)
                for kvh in range(4):
                    pr = slice(kvh * 32, (kvh + 1) * 32)
                    src = osb[b4].ap[pr, kvh * 64:(kvh + 1) * 64]
                    srcb = bass.AP(src.tensor, src.offset, [list(src.ap[0]), [0, 2], list(src.ap[1])])
                    if kvh % 2 == 0:
                        A("dve", I("tensor_scalar", out=asv[pr, :, :], in0=srcb, scalar1=r1[pr, :], scalar2=None, op0=ALU.mult),
                          osb[b4].r() + r1r, asd[b4].r(kvh))
                    else:
                        A("act", I("activation", out=asv[pr, :, :], in_=srcb, func=AF.Identity, scale=r1[pr, :]),
                          osb[b4].r() + r1r, asd[b4].r(kvh))

            def Y2b2(s):
                b4 = s % NB4
                A("pe", I("transpose", out=tb1.ap.bitcast(BF16)[:, 0:128], in_=asd[b4].ap, identity=ident_b.ap),
                  asd[b4].r() + ident_b.r(), tb1.r())

            def Y2b3(s):
                tq2 = tb1.ap.bitcast(BF16)[:, 0:128]
                for p in range(2):
                    pr = slice(p * 64, (p + 1) * 64)
                    src = tq2.rearrange("p (m q t) -> p m q t", m=8, q=2)[pr, :, p, :]
                    A("dve", I("tensor_copy", out=attnT.ap[pr, :, TP + s * 8:TP + s * 8 + 8], in_=src),
                      tb1.r() + attnT.r(*[m * 9 + 8 for m in range(8)]), attnT.r(*[m * 9 + 8 for m in range(8)]))

            sched = [(4, Y2a), (3, Y1a), (2, X3a), (1, X2a), (0, X1a),
                     (4, Y2b1), (3, Y1b1), (2, X3b1), (1, X2b), (0, X1b),
                     (4, Y2b2), (3, Y1b2), (2, X3b2),
                     (4, Y2b3)]
            for i in range(NSEQ + 4):
                for d_, fn_ in sched:
                    if 0 <= i - d_ < NSEQ:
                        fn_(i - d_)
            for t_ in ckb + vsb + kts + st1 + st2 + sbq + pq + [pn, sstat] + ptq + asd + osb:
                P.free(t_)
            P.free(qT)
            P.free(kT)
            P.free(v_tm)
            dump("attnT", attnT, [128, 8, TR], BF16)
            phase_end("p3")

            mergedT = P.tile([KC, TR], BF16, nreg=48, at=KB(72))
            sg = [P.tile([512], F32, lo=KB(126)) for _ in range(4)]
            mm = [P.tile([512], F32, lo=KB(126)) for _ in range(4)]
            it = 0
            for pair in range(8):
                wgc, wga, wyc, wya = pair_w[pair] if pair in pair_w else load_pair(pair)
                for ml in range(2):
                    c = pair * 2 + ml
                    for gi, (r0, n, tr_) in enumerate(RG):
                        e0 = r0 + 128
                        xr = [x + 1 for x in tr_]
                        bs = [banks[(it % 2) * 4 + j] for j in range(4)]
                        rg0 = fm_mm(bs[0], 0, wgc, ml * 128, 128, xn_acts(e0, n, xr), n)
                        rg1 = fm_mm(bs[1], 0, wga, ml * 128, 128, xn_acts(e0, n, xr), n)
                        rg2 = fm_mm(bs[2], 0, wyc, ml * 128, 128, [(aT.ap[:, k, r0:r0 + n], aT.r(k * 3 + gi)) for k in range(8)], n)
                        rg3 = fm_mm(bs[3], 0, wya, ml * 128, 128,
                                    [(attnT.ap[:, k, r0:r0 + n], attnT.r(*[k * 9 + x for x in tr_])) for k in range(8)], n)
                        s0, s1, m0, m1 = sg[(it % 2) * 2], sg[(it % 2) * 2 + 1], mm[(it % 2) * 2], mm[(it % 2) * 2 + 1]
                        it += 1
                        A("act", I("activation", out=s0.ap[:, 0:n], in_=bs[0].ap[:, 0:n], func=AF.Sigmoid), rg0, s0.r())
                        A("act", I("activation", out=s1.ap[:, 0:n], in_=bs[1].ap[:, 0:n], func=AF.Sigmoid), rg1, s1.r())
                        A("dve", I("tensor_tensor", out=m0.ap[:, 0:n], in0=s0.ap[:, 0:n], in1=bs[2].ap[:, 0:n], op=ALU.mult), s0.r() + rg2, m0.r())
                        A("dve", I("tensor_tensor", out=m1.ap[:, 0:n], in0=s1.ap[:, 0:n], in1=bs[3].ap[:, 0:n], op=ALU.mult), s1.r() + rg3, m1.r())
                        A("dve", I("tensor_tensor", out=mergedT.ap[:, c, r0:r0 + n], in0=m0.ap[:, 0:n], in1=m1.ap[:, 0:n], op=ALU.add),
                          m0.r() + m1.r(), mergedT.r(c * 3 + gi))
            for t_ in sg + mm:
                P.free(t_)
            P.free(aT)
            P.free(attnT)
            P.free(xnT)
            dump("mergedT", mergedT, [128, KC, TR], BF16)
            phase_end("p4a")

            h_acc = P.tile([9, D], F32, nreg=36, at=KB(0))
            for i in range(9):
                P.dma("sp", h_acc.ap[:, i, :], x_ext[128 + i * 128:128 + (i + 1) * 128, :], writes=h_acc.r(*[i * 4 + n for n in range(4)]))
            for nb in range(4):
                wk = wload(w_o, 0, 8, nb * 512, 512) + wload(w_o, 1024, 8, nb * 512, 512)
                for i in range(9):
                    bk = next_bank()
                    gi = 0 if i < 4 else (1 if i < 8 else 2)
                    for kc in range(KC):
                        A("pe", I("matmul", bk.ap, lhsT=mergedT.ap[:, kc, i * 128:(i + 1) * 128], rhs=wk[kc][0],
                                  start=(kc == 0), stop=(kc == KC - 1)), wk[kc][1] + mergedT.r(kc * 3 + gi), bk.r())
                    hs = h_acc.ap[:, i, nb * 512:(nb + 1) * 512]
                    A("dve", I("tensor_tensor", out=hs, in0=hs, in1=bk.ap, op=ALU.add), bk.r() + h_acc.r(i * 4 + nb), h_acc.r(i * 4 + nb))
            P.free(mergedT)
            dump("h1", h_acc, [128, 9, D])
            phase_end("p4b")

            P.dma("sp", g_bc.ap, bcast_rows(g_ffn, 128), writes=g_bc.r())
            hnT = P.tile([KC, TR], BF16, nreg=9, at=KB(72))

            def src2(i):
                return h_acc.ap[:, i, :], h_acc.r(*[i * 4 + n for n in range(4)])

            norm_transpose(src2, 9, hnT, 16, KB(108))
            dump("hnT", hnT, [128, KC, TR], BF16)
            guT = [P.tile([4, TR], BF16, nreg=12, lo=KB(108)) for _ in range(2)]
            sgl = [P.tile([512], F32, lo=KB(108)) for _ in range(2)]
            it = 0
            for j in range(DFF // 512):
                gu = guT[j % 2]
                for hp in range(2):
                    wg = wload(w_g, 0, KC, j * 512 + hp * 256, 256)
                    wu = wload(w_u, 0, KC, j * 512 + hp * 256, 256)
                    for ml in range(2):
                        mi = hp * 2 + ml
                        for gi, (r0, n, tr_) in enumerate(RG):
                            bg, bu = banks[(it % 3) * 2], banks[(it % 3) * 2 + 1]
                            acts = [(hnT.ap[:, k, r0:r0 + n], hnT.r(*tr_)) for k in range(KC)]
                            rg = fm_mm(bg, 0, wg, ml * 128, 128, acts, n)
                            ru = fm_mm(bu, 0, wu, ml * 128, 128, acts, n)
                            s_ = sgl[it % 2]
                            it += 1
                            A("act", I("activation", out=s_.ap[:, 0:n], in_=bg.ap[:, 0:n], func=AF.Silu), rg, s_.r())
                            A("dve", I("tensor_tensor", out=gu.ap[:, mi, r0:r0 + n], in0=s_.ap[:, 0:n], in1=bu.ap[:, 0:n], op=ALU.mult),
                              s_.r() + ru, gu.r(mi * 3 + gi))
                for nb in range(4):
                    wd = wload(w_d, j * 512, 4, nb * 512, 512)
                    for i in range(9):
                        bk = banks[6 + (i + nb) % 2]
                        gi = 0 if i < 4 else (1 if i < 8 else 2)
                        for kc in range(4):
                            A("pe", I("matmul", bk.ap, lhsT=gu.ap[:, kc, i * 128:(i + 1) * 128], rhs=wd[kc][0],
                                      start=(kc == 0), stop=(kc == 3)), wd[kc][1] + gu.r(kc * 3 + gi), bk.r())
                        hs = h_acc.ap[:, i, nb * 512:(nb + 1) * 512]
                        A("dve", I("tensor_tensor", out=hs, in0=hs, in1=bk.ap, op=ALU.add), bk.r() + h_acc.r(i * 4 + nb), h_acc.r(i * 4 + nb))
            for t_ in guT + sgl:
                P.free(t_)
            P.free(hnT)
            dump("h2", h_acc, [128, 9, D])
            phase_end("p5")

            P.dma("sp", g_bc.ap, bcast_rows(g_fin, 128), writes=g_bc.r())
            yt = [P.tile([D], F32, lo=KB(72)) for _ in range(2)]
            junk = P.tile([D], BF16, lo=KB(72))

            def F1(i):
                hap = h_acc.ap[:, i, :]
                hr = h_acc.r(*[i * 4 + n for n in range(4)])
                ss = stat.ap[:, 54 + i:55 + i]
                ssr = stat.r(54 + i)
                A("act", I("activation", out=junk.ap, in_=hap, func=AF.Square, accum_out=ss), hr, junk.r() + ssr)
                A("act", I("activation", out=ss, in_=ss, func=AF.Sqrt, scale=1.0 / D, bias=EPS), ssr, ssr)
                A("dve", I("reciprocal", out=ss, in_=ss), ssr, ssr)

            def F2(i):
                hap = h_acc.ap[:, i, :]
                hr = h_acc.r(*[i * 4 + n for n in range(4)])
                ss = stat.ap[:, 54 + i:55 + i]
                ssr = stat.r(54 + i)
                y_ = yt[i % 2]
                A("dve", I("scalar_tensor_tensor", out=y_.ap, in0=hap, scalar=ss, in1=g_bc.ap, op0=ALU.mult, op1=ALU.mult),
                  hr + ssr + g_bc.r(), y_.r())
                P.dma("sp", y_out[i * 128:(i + 1) * 128, :], y_.ap, reads=y_.r(), out_final=True)

            F1(0)
            for i in range(9):
                if i + 1 < 9:
                    F1(i + 1)
                F2(i)

        try:
            body()
        except Stop:
            pass
        P.finalize(stack)
        print("sbuf peak bytes/partition:", P.arena.peak, "ops:", len(P.ops), flush=True)
    return nc


_NC_CACHE = {}


def host_inputs(x_prompt, x_sample, cache_k, cache_v, state_conv, rel_bias, w_in, w_conv, w_conv_out,
                sinks, w_attn_out, w_o, g_mix, g_ffn, w_gate, w_up, w_down, g_final, cores=range(NCORES)):
    f = lambda a: np.ascontiguousarray(np.asarray(a, dtype=np.float32))
    T, Trev = _bucket_tables()
    rel_ext = np.concatenate([f(rel_bias), np.full((1, 16), NEG, np.float32)], axis=0)
    hsel = np.zeros((128, 16), np.float32)
    hsel[np.arange(128), np.arange(128) // 8] = 1.0
    shared = {
        "w_in": f(w_in[0]), "w_conv": f(w_conv[0]), "w_co": f(w_conv_out[0]), "w_ao": f(w_attn_out[0]), "w_o": f(w_o[0]),
        "w_g": f(w_gate[0]), "w_u": f(w_up[0]), "w_d": f(w_down[0]),
        "g_mix": f(g_mix[0]).reshape(1, D), "g_ffn": f(g_ffn[0]).reshape(1, D), "g_fin": f(g_final).reshape(1, D),
        "rel_ext": rel_ext, "sinks": f(sinks[0]).reshape(1, 16), "ttab": T, "trev": Trev, "hsel": hsel,
        "ident": np.eye(128, dtype=np.float32),
    }
    xp = np.asarray(x_prompt, dtype=np.float32)
    xs = np.asarray(x_sample, dtype=np.float32)
    maps = []
    for c in cores:
        b, half = c // 2, c % 2
        x_ext = np.zeros((EXT, D), np.float32)
        if half == 1:
            x_ext[0:128] = xp[b, TP - 128:TP]
        x_ext[128:128 + TP] = xp[b, half * TP:(half + 1) * TP]
        x_ext[128 + TP:] = xs[c * NSEQ:(c + 1) * NSEQ].reshape(TS, D)
        m = dict(shared)
        m["x_ext"] = x_ext
        m["ck"] = f(cache_k[0, c * NSEQ:(c + 1) * NSEQ]).reshape(NSEQ, 128, 256)
        m["cv"] = f(cache_v[0, c * NSEQ:(c + 1) * NSEQ]).reshape(NSEQ, 128, 256)
        m["sc"] = f(state_conv[0, c * NSEQ:(c + 1) * NSEQ]).reshape(NSEQ * 2, 1024)
        m["hmask"] = np.full((128, 1), 0.0 if half == 1 else NEG, np.float32)
        maps.append(m)
    return maps


def kernel(x_prompt, x_sample, cache_k, cache_v, state_conv, rel_bias, w_in, w_conv, w_conv_out,
           sinks, w_attn_out, w_o, g_mix, g_ffn, w_gate, w_up, w_down, g_final):
    if "nc" not in _NC_CACHE:
        _NC_CACHE["nc"] = build()
    nc = _NC_CACHE["nc"]
    maps = host_inputs(x_prompt, x_sample, cache_k, cache_v, state_conv, rel_bias, w_in, w_conv, w_conv_out,
                       sinks, w_attn_out, w_o, g_mix, g_ffn, w_gate, w_up, w_down, g_final)
    res = run_bass_kernel_spmd(nc, maps, core_ids=list(range(NCORES)))
    R = res.results
    B = 4
    y_prompt = np.zeros((B, 2048, D), np.float32)
    y_sample = np.zeros((128, 8, D), np.float32)
    nkp = np.zeros((1, B, 128, 4, 64), np.float32)
    nvp = np.zeros((1, B, 128, 4, 64), np.float32)
    ncp = np.zeros((1, B, 2, 1024), np.float32)
    nks = np.zeros((1, 128, 128, 4, 64), np.float32)
    nvs = np.zeros((1, 128, 128, 4, 64), np.float32)
    ncs = np.zeros((1, 128, 2, 1024), np.float32)
    for c in range(NCORES):
        b, half = c // 2, c % 2
        r = R[c]
        y = np.asarray(r["y"])
        y_prompt[b, half * TP:(half + 1) * TP] = y[0:TP]
        y_sample[c * NSEQ:(c + 1) * NSEQ] = y[TP:].reshape(NSEQ, 8, D)
        if half == 1:
            nkp[0, b] = np.asarray(r["knp"]).reshape(128, 4, 64)
            nvp[0, b] = np.asarray(r["vnp"]).reshape(128, 4, 64)
            ncp[0, b] = np.asarray(r["cnp"])
        nks[0, c * NSEQ:(c + 1) * NSEQ] = np.asarray(r["ks"]).reshape(NSEQ, 128, 4, 64)
        nvs[0, c * NSEQ:(c + 1) * NSEQ] = np.asarray(r["vs"]).reshape(NSEQ, 128, 4, 64)
        ncs[0, c * NSEQ:(c + 1) * NSEQ] = np.asarray(r["cns"]).reshape(NSEQ, 2, 1024)
    return (y_prompt, y_sample, nkp, nvp, ncp, nks, nvs, ncs)
```

```python
import math
from contextlib import ExitStack

import numpy as np
import concourse.bass as bass
import concourse.mybir as mybir
from concourse.bass_utils import run_bass_kernel_spmd

F32 = mybir.dt.float32
BF16 = mybir.dt.bfloat16
AF = mybir.ActivationFunctionType
ALU = mybir.AluOpType
AX = mybir.AxisListType

D = 2048
DFF = 5632
DIN = 8704
NCORES = 8
TP = 1024
TS = 128
TR = TP + TS
EXT = TR + 128
NSEQ = 16
EPS = 1e-6
NEG = -30000.0
KC = D // 128


class Reg:
    __slots__ = ("w", "rs", "excl")

    def __init__(self):
        self.w = None
        self.rs = []
        self.excl = False


class Tile:
    def __init__(self, ap, nreg, arena=None, off=0, nbytes=0):
        self.ap = ap
        self.regs = [Reg() for _ in range(nreg)]
        self.arena = arena
        self.off = off
        self.nbytes = nbytes

    def r(self, *idx):
        if not idx:
            return list(self.regs)
        return [self.regs[i] for i in idx]


class BankTile(Tile):
    def __init__(self, ap):
        Tile.__init__(self, ap, 1)
        self.regs[0].excl = True

    def r(self, *idx):
        return [self.regs[0]]


class Op:
    __slots__ = ("eng", "fn", "kind", "deps", "sig", "seq", "sem", "val", "prev", "waits", "clock", "idx")

    def __init__(self, eng, fn, kind):
        self.eng = eng
        self.fn = fn
        self.kind = kind
        self.deps = set()
        self.sig = False
        self.seq = 0
        self.sem = None
        self.val = 0
        self.prev = None
        self.waits = []
        self.clock = None
        self.idx = 0


class Arena:
    def __init__(self, size):
        self.size = size
        self.free = [(0, size)]
        self.pending = []
        self.peak = 0

    def alloc(self, n, top=False, at=None, lo=None):
        n = (n + 63) // 64 * 64
        order = list(enumerate(self.free))
        if top:
            order = order[::-1]
        for i, (s, e) in order:
            if at is not None:
                if not (s <= at and at + n <= e):
                    continue
                a = at
            elif lo is not None:
                a = max(s, lo)
                if a + n > e:
                    continue
            elif top:
                a = e - n
                if a < s:
                    continue
            else:
                a = s
                if a + n > e:
                    continue
            self.free.pop(i)
            if a > s:
                self.free.append((s, a))
            if a + n < e:
                self.free.append((a + n, e))
            self.free.sort()
            pend = []
            for (ps, pe, ops) in self.pending:
                if ps < a + n and pe > a:
                    pend.extend(ops)
            self.peak = max(self.peak, a + n)
            return a, n, pend
        raise RuntimeError(f"arena OOM: need {n} at={at} lo={lo}, free={self.free}")

    def release(self, off, n, ops):
        self.free.append((off, off + n))
        self.free.sort()
        merged = []
        for s, e in self.free:
            if merged and merged[-1][1] == s:
                merged[-1] = (merged[-1][0], e)
            else:
                merged.append((s, e))
        self.free = merged
        self.pending.append((off, off + n, ops))


class Prog:
    ENGS = ("pe", "act", "dve", "pool", "sp")
    NDSEM = {"sp": 16, "pool": 8}

    def __init__(self, nc, sb_arena_ap, sb_bytes):
        self.nc = nc
        self.ops = []
        self.out_ops = []
        self.arena = Arena(sb_bytes)
        self.sb = sb_arena_ap

    def tile(self, shape, dt, nreg=1, parts=128, top=False, at=None, lo=None):
        esz = 4 if dt == F32 else 2
        n = esz
        for s in shape:
            n *= s
        off, nb, pend = self.arena.alloc(n, top, at, lo)
        ap = self.sb[0:parts, off // 2:(off + n) // 2]
        if dt == F32:
            ap = ap.bitcast(F32)
        if len(shape) == 2:
            ap = ap.rearrange("p (a b) -> p a b", a=shape[0])
        elif len(shape) == 3:
            ap = ap.rearrange("p (a b c) -> p a b c", a=shape[0], b=shape[1])
        elif len(shape) == 4:
            ap = ap.rearrange("p (a b c d) -> p a b c d", a=shape[0], b=shape[1], c=shape[2])
        t = Tile(ap, nreg, self.arena, off, nb)
        if pend:
            for r in t.regs:
                r.rs = list(pend)
        return t

    def free(self, t):
        ops = []
        for r in t.regs:
            if r.w is not None:
                ops.append(r.w)
            ops.extend(r.rs)
        ops = list({id(o): o for o in ops}.values())
        self.arena.release(t.off, t.nbytes, ops)

    def add(self, eng, fn, reads=(), writes=(), kind="c", out=False):
        op = Op(eng, fn, kind)
        op.idx = len(self.ops)
        xr = [r for r in reads if r.excl]
        if xr:
            reads = [r for r in reads if not r.excl]
            writes = list(writes) + [r for r in xr if r not in writes]
        for r in reads:
            if r.w is not None:
                op.deps.add(r.w)
        for r in writes:
            if r.w is not None:
                op.deps.add(r.w)
            for o in r.rs:
                op.deps.add(o)
        for r in reads:
            r.rs.append(op)
        for r in writes:
            r.w = op
            r.rs = []
        op.deps.discard(op)
        self.ops.append(op)
        if out:
            self.out_ops.append(op)
        return op

    def dma(self, q, out, in_, reads=(), writes=(), out_final=False):
        return self.add(q, lambda e: e.dma_start(out=out, in_=in_), reads, writes, kind="dma", out=out_final)

    def finalize(self, stack):
        nc = self.nc
        fin = Op("sp", None, "c")
        fin.idx = len(self.ops)
        fin.deps = set(self.out_ops)
        self.ops.append(fin)
        for op in self.ops:
            for d in op.deps:
                if d.kind == "dma":
                    continue
                if d.eng == "pe" and op.eng == "pe":
                    continue
                d.sig = True
        dsem = {}
        for q in ("pool", "sp"):
            dsem[q] = [stack.enter_context(nc.semaphore(f"d_{q}{i}")) for i in range(self.NDSEM[q])]
        engsem = {e: stack.enter_context(nc.semaphore("s_" + e)) for e in ("pe", "act", "dve", "pool")}
        print("semaphores:", [x.num if hasattr(x, "num") else x for x in dsem["pool"][:2] + dsem["sp"][-2:] + list(engsem.values())])
        cnt = {e: 0 for e in self.ENGS}
        dcnt = {q: 0 for q in self.NDSEM}
        for op in self.ops:
            if op.kind == "dma":
                q = op.eng
                j = dcnt[q]
                K = self.NDSEM[q]
                op.sem = (q, j % K)
                op.val = 16 * (j // K + 1)
                if j >= K:
                    op.prev = (("d", q, j % K), 16 * (j // K))
                dcnt[q] += 1
            elif op.sig:
                cnt[op.eng] += 1
                op.seq = cnt[op.eng]
        known = {e: {} for e in self.ENGS}
        for op in self.ops:
            kn = known[op.eng]
            waits = []
            for d in sorted(op.deps, key=lambda o: -o.idx):
                if d.kind == "dma":
                    key = ("d",) + d.sem
                    val = d.val
                else:
                    if d.eng == "pe" and op.eng == "pe":
                        continue
                    key = d.eng
                    val = d.seq
                if kn.get(key, 0) >= val:
                    continue
                waits.append((key, val))
                if d.clock is not None:
                    for k, v in d.clock.items():
                        if kn.get(k, 0) < v:
                            kn[k] = v
                kn[key] = max(kn.get(key, 0), val)
            if op.prev is not None:
                key, val = op.prev
                if kn.get(key, 0) < val:
                    waits.append((key, val))
                    kn[key] = val
            op.waits = waits
            if op.kind == "dma" or op.sig:
                op.clock = dict(kn)
                if op.kind != "dma":
                    op.clock[op.eng] = op.seq

        def semof(key):
            if isinstance(key, tuple):
                return dsem[key[1]][key[2]]
            return engsem[key]

        per = {e: [o for o in self.ops if o.eng == e] for e in self.ENGS}

        def emit(ename, eng):
            for op in per[ename]:
                for key, val in op.waits:
                    eng.wait_ge(semof(key), val)
                if op.fn is None:
                    continue
                ins = op.fn(eng)
                if op.kind == "dma":
                    ins.then_inc(dsem[op.sem[0]][op.sem[1]], 16)
                elif op.sig:
                    ins.then_inc(engsem[ename], 1)

        block = stack.enter_context(nc.Block())

        @block.tensor
        def _(e):
            emit("pe", e)

        @block.scalar
        def _(e):
            emit("act", e)

        @block.vector
        def _(e):
            emit("dve", e)

        @block.gpsimd
        def _(e):
            emit("pool", e)

        @block.sync
        def _(e):
            emit("sp", e)


def _bucket_tables():
    dist = np.arange(0, 128)
    max_exact = 16
    ratio = np.log(np.maximum(dist, 1).astype(np.float32) / np.float32(max_exact)) / np.float32(math.log(128 / max_exact))
    large = max_exact + (ratio * np.float32(32 - max_exact)).astype(np.int32)
    large = np.minimum(large, 31)
    bucket = np.where(dist < max_exact, dist, large)
    T = np.zeros((33, 383), np.float32)
    for c in range(383):
        d = c - 127
        if 0 <= d < 128:
            T[bucket[d], c] = 1.0
        else:
            T[32, c] = 1.0
    Trev = np.ascontiguousarray(T[:, ::-1])
    return T, Trev


def I(method, *a, **kw):
    return lambda e: getattr(e, method)(*a, **kw)


def build(debug=(), stop_after=None):
    nc = bass.Bass("TRN2", target_bir_lowering=False)

    def din(name, shape):
        return nc.dram_tensor(name, list(shape), F32, kind="ExternalInput").ap()

    def dout(name, shape):
        return nc.dram_tensor(name, list(shape), F32, kind="ExternalOutput").ap()

    x_ext = din("x_ext", [EXT, D])
    ck = din("ck", [NSEQ, 128, 256])
    cv = din("cv", [NSEQ, 128, 256])
    sc = din("sc", [NSEQ * 2, 1024])
    w_in = din("w_in", [D, DIN])
    w_conv = din("w_conv", [3, 1024])
    w_co = din("w_co", [1024, D])
    w_ao = din("w_ao", [1024, D])
    w_o = din("w_o", [D, D])
    w_g = din("w_g", [D, DFF])
    w_u = din("w_u", [D, DFF])
    w_d = din("w_d", [DFF, D])
    g_mix = din("g_mix", [1, D])
    g_ffn = din("g_ffn", [1, D])
    g_fin = din("g_fin", [1, D])
    rel_ext = din("rel_ext", [33, 16])
    sinks = din("sinks", [1, 16])
    ttab = din("ttab", [33, 383])
    trev = din("trev", [33, 383])
    hsel_d = din("hsel", [128, 16])
    ident_d = din("ident", [128, 128])
    hmask_d = din("hmask", [128, 1])

    y_out = dout("y", [TR, D])
    knp_out = dout("knp", [128, 256])
    vnp_out = dout("vnp", [128, 256])
    cnp_out = dout("cnp", [2, 1024])
    ks_out = dout("ks", [NSEQ, 128, 256])
    vs_out = dout("vs", [NSEQ, 128, 256])
    cns_out = dout("cns", [NSEQ * 2, 1024])
    gscr = nc.dram_tensor("gscr", [16, 383], F32, kind="Internal").ap()

    SB_BYTES = 206 * 1024
    stack = ExitStack()
    with stack:
        sb_t = stack.enter_context(nc.sbuf_tensor("arena", [128, SB_BYTES // 2], BF16))
        ps_t = stack.enter_context(nc.psum_tensor("psum", [128, 8, 512], F32))
        P = Prog(nc, sb_t[:, :], SB_BYTES)
        banks = [BankTile(ps_t[:, b, :]) for b in range(8)]
        A = P.add

        def bcast_rows(ap2d, n):
            return bass.AP(ap2d.tensor, ap2d.offset, [[0, n]] + [list(x) for x in ap2d.ap[1:]])

        def dump(name, t, shape, dt=F32):
            if name not in debug:
                return
            o = nc.dram_tensor("dbg_" + name, list(shape), dt, kind="ExternalOutput").ap()
            P.dma("sp", o, t.ap, reads=t.r(), out_final=True)

        class Stop(Exception):
            pass

        def phase_end(name):
            if stop_after == name:
                raise Stop()

        def body():
            xt = [P.tile([D], F32, top=True) for _ in range(3)]
            for i in range(2):
                P.dma("sp", xt[i].ap, x_ext[i * 128:(i + 1) * 128, :], writes=xt[i].r())
            g_bc = P.tile([D], F32)
            P.dma("sp", g_bc.ap, bcast_rows(g_mix, 128), writes=g_bc.r())
            ident_f = P.tile([128], F32)
            ident_b = P.tile([128], BF16)
            P.dma("sp", ident_f.ap, ident_d, writes=ident_f.r())
            A("dve", I("tensor_copy", out=ident_b.ap, in_=ident_f.ap), ident_f.r(), ident_b.r())
            tt = P.tile([383], F32)
            tr = P.tile([383], F32)
            rel = P.tile([16], F32)
            P.dma("sp", tt.ap[0:33], ttab, writes=tt.r())
            P.dma("sp", tr.ap[0:33], trev, writes=tr.r())
            P.dma("sp", rel.ap[0:33], rel_ext, writes=rel.r())
            sink_bc = P.tile([16], F32)
            P.dma("sp", sink_bc.ap, bcast_rows(sinks, 128), writes=sink_bc.r())
            hsel = P.tile([16], F32)
            P.dma("sp", hsel.ap, hsel_d, writes=hsel.r())
            hmask = P.tile([1], F32)
            P.dma("sp", hmask.ap, hmask_d, writes=hmask.r())
            wcv = P.tile([3, 8], F32, nreg=3)
            stat = P.tile([64], F32, nreg=64)
            sink_col = P.tile([1], F32)
            tmp16 = P.tile([16], F32)
            A("dve", I("tensor_tensor", out=tmp16.ap, in0=sink_bc.ap, in1=hsel.ap, op=ALU.mult),
              sink_bc.r() + hsel.r(), tmp16.r())
            A("dve", I("tensor_reduce", out=sink_col.ap, in_=tmp16.ap, axis=AX.X, op=ALU.add), tmp16.r(), sink_col.r())
            NW = 6
            wslots = [P.tile([4096], BF16, nreg=4) for _ in range(NW)]
            wctr = [0]

            wgate = []

            def wslot():
                t = wslots[wctr[0] % NW]
                wctr[0] += 1
                return t

            def wdma(dst, src, writes):
                op = P.dma("pool", dst, src, writes=writes)
                if wgate and wctr[0] <= NW:
                    op.deps.add(wgate[0])
                return op

            def wload(src, r0, nk, c0, ncols):
                assert nk * ncols <= 4096
                t = wslot()
                dst = t.ap[:, 0:nk * ncols].rearrange("p (k n) -> p k n", k=nk)
                s = src[r0:r0 + nk * 128, c0:c0 + ncols].rearrange("(k p) n -> p k n", p=128)
                P.dma("pool", dst, s, writes=t.r())
                return [(dst[:, k, :], t.r()) for k in range(nk)]

            bias_s = P.tile([137], F32)
            R0 = (P.arena.free[0][0] + 1023) // 1024 * 1024
            assert R0 + 144 * 1024 <= SB_BYTES, R0

            def KB(x):
                return R0 + int(x * 1024)

            xnT = P.tile([KC, EXT], BF16, nreg=10, at=KB(0))

            bias_p = P.tile([16, 257], F32, nreg=128, at=KB(91))
            LB = KB(40)
            tt_b = P.tile([383], BF16, lo=LB)
            tr_b = P.tile([383], BF16, lo=LB)
            rel_h = P.tile([16], BF16, lo=LB)
            rel_l = P.tile([16], BF16, lo=LB)
            rel_t = P.tile([16], F32, lo=LB)
            A("dve", I("tensor_copy", out=tt_b.ap[0:33], in_=tt.ap[0:33]), tt.r(), tt_b.r())
            A("dve", I("tensor_copy", out=tr_b.ap[0:33], in_=tr.ap[0:33]), tr.r(), tr_b.r())
            A("dve", I("tensor_copy", out=rel_h.ap[0:33], in_=rel.ap[0:33]), rel.r(), rel_h.r())
            A("dve", I("tensor_copy", out=rel_t.ap[0:33], in_=rel_h.ap[0:33]), rel_h.r(), rel_t.r())
            A("dve", I("tensor_tensor", out=rel_t.ap[0:33], in0=rel.ap[0:33], in1=rel_t.ap[0:33], op=ALU.subtract),
              rel.r() + rel_t.r(), rel_t.r())
            A("dve", I("tensor_copy", out=rel_l.ap[0:33], in_=rel_t.ap[0:33]), rel_t.r(), rel_l.r())
            lts = [P.tile([8, 128], BF16, lo=LB) for _ in range(2)]
            for lt_, rl_ in zip(lts, (rel_h, rel_l)):
                A("dve", I("memset", lt_.ap[0:33], 0.0), (), lt_.r())
                for t_ in range(8):
                    A("dve", I("tensor_copy", out=lt_.ap[0:33, t_, :].rearrange("p (h t) -> p h t", t=8)[:, :, t_],
                               in_=rl_.ap[0:33, :]), rl_.r() + lt_.r(), lt_.r())

            def bias_chunk(r):
                def f():
                    bk = banks[4 + r % 4]
                    for sl in range(32):
                        s = r * 32 + sl
                        for j, rl_ in enumerate((rel_h, rel_l)):
                            A("pe", I("matmul", bk.ap[:, sl * 16:(sl + 1) * 16], lhsT=tt_b.ap[0:33, 255 - s:255 - s + 128],
                                      rhs=rl_.ap[0:33, :], start=(j == 0), stop=(j == 1)), tt_b.r() + rl_.r(), bk.r())
                    A("dve", I("tensor_copy", out=bias_p.ap[:, :, r * 32:r * 32 + 32],
                               in_=bk.ap.rearrange("p (s h) -> p h s", h=16)), bk.r(), bias_p.r())
                return f

            def bias_sample():
                bk = banks[4]
                for (c0, n, t0) in ((0, 128, 127), (128, 8, 255)):
                    for t_ in range(8):
                        for j, lt_ in enumerate(lts):
                            A("pe", I("matmul", bk.ap[:, c0:c0 + n], lhsT=lt_.ap[0:33, t_, :], rhs=tr_b.ap[0:33, t0 - t_:t0 - t_ + n],
                                      start=(t_ == 0 and j == 0), stop=(t_ == 7 and j == 1)), lt_.r() + tr_b.r(), bk.r())
                A("dve", I("tensor_copy", out=bias_s.ap[:, 0:128], in_=bk.ap[:, 0:128]), bk.r(), bias_s.r())
                A("dve", I("tensor_copy", out=bias_s.ap[:, 129:137], in_=bk.ap[:, 128:136]), bk.r() + bias_s.r(), bias_s.r())
                A("dve", I("tensor_copy", out=bias_s.ap[:, 128:129], in_=sink_col.ap), sink_col.r() + bias_s.r(), bias_s.r())

            gsb = P.tile([383], F32, lo=LB)
            bkg = banks[7]
            for j, rl_ in enumerate((rel_h, rel_l)):
                A("pe", I("matmul", bkg.ap[0:16, 0:383], lhsT=rl_.ap[0:33, :], rhs=tr_b.ap[0:33, 0:383], start=(j == 0), stop=(j == 1)),
                  rl_.r() + tr_b.r(), bkg.r())
            A("dve", I("tensor_copy", out=gsb.ap[0:16], in_=bkg.ap[0:16, 0:383]), bkg.r(), gsb.r())
            gst = P.dma("sp", gscr, gsb.ap[0:16], reads=gsb.r())

            def bias_rows():
                for q in range(128):
                    src = bass.AP(gscr.tensor, 127 - q, [[383 * 16, 1], [383, 16], [1, 256]])
                    op = P.dma("sp", bias_p.ap[q:q + 1, :, 0:256], src, writes=bias_p.r(q))
                    op.deps.add(gst)

            bias_work = [bias_sample]

            def norm_transpose(src_fn, ntiles, dstT, stat0, lo, extra=()):
                xn_b = [P.tile([D], BF16, lo=lo) for _ in range(2)]
                junk = P.tile([D], BF16, lo=lo)
                srcs = {}

                def N0(i):
                    srcs[i] = src_fn(i)

                def N1a(i):
                    xap, xregs = srcs[i]
                    ss = stat.ap[:, stat0 + i:stat0 + i + 1]
                    ssr = stat.r(stat0 + i)
                    A("act", I("activation", out=junk.ap, in_=xap, func=AF.Square, accum_out=ss), xregs, junk.r() + ssr)
                    A("act", I("activation", out=ss, in_=ss, func=AF.Sqrt, scale=1.0 / D, bias=EPS), ssr, ssr)

                def N1b(i):
                    ss = stat.ap[:, stat0 + i:stat0 + i + 1]
                    ssr = stat.r(stat0 + i)
                    A("dve", I("reciprocal", out=ss, in_=ss), ssr, ssr)

                def N2(i):
                    xap, xregs = srcs[i]
                    ss = stat.ap[:, stat0 + i:stat0 + i + 1]
                    ssr = stat.r(stat0 + i)
                    xb = xn_b[i % 2]
                    A("dve", I("scalar_tensor_tensor", out=xb.ap, in0=xap, scalar=ss, in1=g_bc.ap, op0=ALU.mult, op1=ALU.mult),
                      xregs + ssr + g_bc.r(), xb.r())
                    pb = (banks[0], banks[1]) if i % 2 == 0 else (banks[2], banks[3])
                    for c in range(KC):
                        bk_ = pb[c // 8]
                        o = bk_.ap.bitcast(BF16)[:, (c % 8) * 128:(c % 8 + 1) * 128]
                        A("pe", I("transpose", out=o, in_=xb.ap[:, c * 128:(c + 1) * 128], identity=ident_b.ap),
                          xb.r() + ident_b.r(), bk_.r())

                def N3(i):
                    pb = (banks[0], banks[1]) if i % 2 == 0 else (banks[2], banks[3])
                    col = i * 128
                    for hf in range(2):
                        bk_ = pb[hf]
                        src = bk_.ap.bitcast(BF16).rearrange("p (c t) -> p c t", c=8)
                        dst = dstT.ap[:, hf * 8:(hf + 1) * 8, col:col + 128]
                        if hf == 0:
                            A("act", I("activation", out=dst, in_=src, func=AF.Identity), bk_.r(), dstT.r(i))
                        else:
                            A("dve", I("tensor_copy", out=dst, in_=src), bk_.r(), dstT.r(i))

                N0(0)
                if ntiles > 1:
                    N0(1)
                N1a(0)
                N1b(0)
                for i in range(ntiles + 1):
                    if i + 2 < ntiles:
                        N0(i + 2)
                    if i + 1 < ntiles:
                        N1a(i + 1)
                    if i < ntiles:
                        N2(i)
                    if i + 1 < ntiles:
                        N1b(i + 1)
                    if i < len(extra):
                        extra[i]()
                    if i >= 1:
                        N3(i - 1)
                for j in range(ntiles + 1, len(extra)):
                    extra[j]()
                P.free(junk)
                for t_ in xn_b:
                    P.free(t_)

            xload_ops = []

            def src1(i):
                t = xt[i % 3]
                if i >= 2:
                    xload_ops.append(P.dma("sp", t.ap, x_ext[i * 128:(i + 1) * 128, :], writes=t.r()))
                return t.ap, t.r()

            norm_transpose(src1, 10, xnT, 0, KB(40), extra=bias_work)
            for t_ in [tt_b, tr_b, rel_h, rel_l, rel_t, gsb] + lts:
                P.free(t_)
            dump("bias_p", bias_p, [128, 16, 257])
            dump("bias_s", bias_s, [128, 137])
            for t_ in xt:
                P.free(t_)
            dump("xnT", xnT, [128, KC, EXT], BF16)
            phase_end("p1")

            GA = (128, 512, [1, 2, 3, 4])
            GB = (640, 512, [5, 6, 7, 8])
            GS = (1152, 128, [9])
            GROUPS = [GA, GB, GS]
            RG = [(0, 512, [0, 1, 2, 3]), (512, 512, [4, 5, 6, 7]), (1024, 128, [8])]
            bank_rr = [0]

            def next_bank():
                b = banks[bank_rr[0] % 8]
                bank_rr[0] += 1
                return b

            def fm_mm(bk, col0, wk, m0, mw, acts, n):
                regs = bk.r(*[q for q in range(4) if q * 128 < col0 + n and (q + 1) * 128 > col0])
                nk = len(wk)
                for k in range(nk):
                    A("pe", I("matmul", bk.ap[0:mw, col0:col0 + n], lhsT=wk[k][0][:, m0:m0 + mw], rhs=acts[k][0],
                              start=(k == 0), stop=(k == nk - 1)), wk[k][1] + acts[k][1], regs)
                return regs

            def xn_acts(e0, n, xr):
                return [(xnT.ap[:, k, e0:e0 + n], xnT.r(*xr)) for k in range(KC)]

            for j in range(3):
                A("sp", I("dma_start", out=wcv.ap[:, j, :], in_=w_conv[j:j + 1, :].rearrange("o (c p) -> p (o c)", p=128),
                          allow_slow_non_contiguous=True), (), wcv.r(j), kind="dma")
            aT = P.tile([8, TR], BF16, nreg=24, at=KB(40))
            qT = P.tile([8, TR], BF16, nreg=24, at=KB(58))
            kT = P.tile([4, EXT], BF16, nreg=12, at=KB(76))
            v_tm = P.tile([10, 256], BF16, nreg=10, at=KB(86))
            L2 = KB(108)
            ubuf = P.tile([TP + 2], F32, nreg=3, lo=L2)
            us = P.tile([NSEQ, 10], F32, lo=L2)
            csb = [P.tile([512], F32, lo=L2) for _ in range(2)]
            t1b = [P.tile([512], F32, lo=L2) for _ in range(2)]
            t2b = [P.tile([512], F32, lo=L2) for _ in range(2)]
            uo_p = P.tile([8, 2], F32, lo=L2)
            uo_s = P.tile([8, 32], F32, lo=L2)
            scT = P.tile([8, 32], F32, lo=L2)
            sct = P.tile([1024], F32, lo=L2)
            P.dma("sp", sct.ap[0:32], sc, writes=sct.r())
            bias_rows()
            bk = next_bank()
            for c in range(8):
                A("pe", I("transpose", out=bk.ap[:, c * 32:(c + 1) * 32], in_=sct.ap[0:32, c * 128:(c + 1) * 128],
                          identity=ident_f.ap[0:32, 0:32]), sct.r() + ident_f.r(), bk.r())
            A("dve", I("tensor_copy", out=scT.ap, in_=bk.ap[:, 0:256].rearrange("p (c j) -> p c j", c=8)), bk.r(), scT.r())
            phase_end("p2a_sc")

            def v3(ap):
                return ap.rearrange("p (s t) -> p s t", t=8)

            it = 0
            for c in range(8):
                wk = []
                for kh in range(2):
                    t = wslot()
                    dst4 = t.ap[:, 0:8 * 384].rearrange("p (k j n) -> p k j n", k=8, j=3)
                    for j in range(3):
                        src = w_in[kh * 1024:(kh + 1) * 1024, j * 1024 + c * 128:j * 1024 + (c + 1) * 128].rearrange("(k p) n -> p k n", p=128)
                        wop = P.dma("pool", dst4[:, :, j, :], src, writes=t.r(j) if j < 2 else t.r(2, 3))
                        if c == 0 and kh == 0 and j == 0:
                            wop.deps.add(xload_ops[3])
                    dst3 = t.ap[:, 0:8 * 384].rearrange("p (k n) -> p k n", k=8)
                    wk += [(dst3[:, k, :], t.r()) for k in range(8)]
                if c == 0 and stop_after == "p2a_w0x":
                    A("dve", I("tensor_copy", out=t1b[0].ap[:, 0:128], in_=wk[0][0][:, 0:128]), wk[0][1] + wk[8][1], t1b[0].r())
                    phase_end("p2a_w0x")
                if c == 0:
                    phase_end("p2a_w0")
                bh = next_bank()
                ha = xn_acts(126, 2, [0])
                bh2 = next_bank()
                fm_mm(bh, 126, wk, 128, 128, ha, 2)
                fm_mm(bh2, 254, wk, 256, 128, ha, 2)
                if c == 0 and stop_after == "p2a_mm0w":
                    A("sp", None, bh.r(), ())
                    phase_end("p2a_mm0w")
                if c == 0:
                    phase_end("p2a_mm0")
                A("dve", I("tensor_copy", out=csb[0].ap[:, 126:128], in_=bh.ap[:, 126:128]), bh.r(0), csb[0].r())
                if c == 0:
                    phase_end("p2a_act0")
                A("dve", I("tensor_tensor", out=ubuf.ap[:, 0:2], in0=csb[0].ap[:, 126:128], in1=bh2.ap[:, 254:256], op=ALU.mult),
                  csb[0].r() + bh2.r(1), ubuf.r(0))
                if c == 0:
                    phase_end("p2a_c0h")
                for gi, (e0, n, xr) in enumerate((GA, GB)):
                    bB, bC, bH = next_bank(), next_bank(), next_bank()
                    acts = xn_acts(e0, n, xr)
                    fm_mm(bB, 0, wk, 0, 128, acts, n)
                    fm_mm(bC, 0, wk, 128, 128, acts, n)
                    fm_mm(bH, 0, wk, 256, 128, acts, n)
                    cs_, t1_, t2_ = csb[it % 2], t1b[it % 2], t2b[it % 2]
                    it += 1
                    off = gi * 512
                    A("act", I("activation", out=cs_.ap, in_=bC.ap, func=AF.Identity), bC.r(), cs_.r())
                    A("dve", I("tensor_tensor", out=ubuf.ap[:, 2 + off:2 + off + 512], in0=cs_.ap, in1=bH.ap, op=ALU.mult),
                      cs_.r() + bH.r(), ubuf.r(1 + gi))
                    ur = ubuf.r(0, 1) if gi == 0 else ubuf.r(1, 2)
                    A("act", I("activation", out=t1_.ap, in_=ubuf.ap[:, off:off + 512], func=AF.Identity, scale=wcv.ap[:, 0, c:c + 1]),
                      ur + wcv.r(), t1_.r())
                    A("dve", I("scalar_tensor_tensor", out=t2_.ap, in0=ubuf.ap[:, off + 1:off + 513], scalar=wcv.ap[:, 1, c:c + 1],
                               in1=t1_.ap, op0=ALU.mult, op1=ALU.add), ur + wcv.r() + t1_.r(), t2_.r())
                    A("dve", I("scalar_tensor_tensor", out=t1_.ap, in0=ubuf.ap[:, off + 2:off + 514], scalar=wcv.ap[:, 2, c:c + 1],
                               in1=t2_.ap, op0=ALU.mult, op1=ALU.add), ur + wcv.r() + t2_.r(), t1_.r())
                    A("dve", I("tensor_tensor", out=aT.ap[:, c, off:off + 512], in0=t1_.ap, in1=bB.ap, op=ALU.mult),
                      t1_.r() + bB.r(), aT.r(c * 3 + gi))
                A("dve", I("tensor_copy", out=uo_p.ap[:, c, :], in_=ubuf.ap[:, TP:TP + 2]), ubuf.r(2), uo_p.r())
                if c == 0:
                    phase_end("p2a_c0g")
                e0, n, xr = GS
                bS = next_bank()
                acts = xn_acts(e0, n, xr)
                fm_mm(bS, 0, wk, 0, 128, acts, n)
                fm_mm(bS, 128, wk, 128, 128, acts, n)
                fm_mm(bS, 256, wk, 256, 128, acts, n)
                cs_, t1_, t2_ = csb[it % 2], t1b[it % 2], t2b[it % 2]
                it += 1
                A("act", I("activation", out=cs_.ap[:, 0:128], in_=bS.ap[:, 128:256], func=AF.Identity), bS.r(1), cs_.r())
                A("dve", I("tensor_copy", out=us.ap[:, :, 0:2], in_=scT.ap[:, c, :].rearrange("p (s j) -> p s j", j=2)),
                  scT.r(), us.r())
                A("dve", I("tensor_tensor", out=us.ap[:, :, 2:10], in0=v3(cs_.ap[:, 0:128]), in1=v3(bS.ap[:, 256:384]), op=ALU.mult),
                  cs_.r() + bS.r(2), us.r())
                A("act", I("activation", out=v3(t1_.ap[:, 0:128]), in_=us.ap[:, :, 0:8], func=AF.Identity, scale=wcv.ap[:, 0, c:c + 1]),
                  us.r() + wcv.r(), t1_.r())
                A("dve", I("scalar_tensor_tensor", out=v3(t2_.ap[:, 0:128]), in0=us.ap[:, :, 1:9], scalar=wcv.ap[:, 1, c:c + 1],
                           in1=v3(t1_.ap[:, 0:128]), op0=ALU.mult, op1=ALU.add), us.r() + wcv.r() + t1_.r(), t2_.r())
                A("dve", I("scalar_tensor_tensor", out=v3(t1_.ap[:, 0:128]), in0=us.ap[:, :, 2:10], scalar=wcv.ap[:, 2, c:c + 1],
                           in1=v3(t2_.ap[:, 0:128]), op0=ALU.mult, op1=ALU.add), us.r() + wcv.r() + t2_.r(), t1_.r())
                A("dve", I("tensor_tensor", out=aT.ap[:, c, TP:TP + 128], in0=t1_.ap[:, 0:128], in1=bS.ap[:, 0:128], op=ALU.mult),
                  t1_.r() + bS.r(0), aT.r(c * 3 + 2))
                A("dve", I("tensor_copy", out=uo_s.ap[:, c, :].rearrange("p (s j) -> p s j", j=2), in_=us.ap[:, :, 8:10]),
                  us.r(), uo_s.r())
                if c == 0:
                    phase_end("p2a_c0")
            phase_end("p2a_conv")
            for (uo, npart, dst_out) in ((uo_p, 2, cnp_out), (uo_s, 32, cns_out)):
                cst = P.tile([1024], F32, lo=L2)
                bka, bkb = next_bank(), next_bank()
                for c in range(8):
                    bk_ = bka if c < 4 else bkb
                    A("pe", I("transpose", out=bk_.ap[0:npart, (c % 4) * 128:(c % 4 + 1) * 128], in_=uo.ap[:, c, :],
                              identity=ident_f.ap), uo.r() + ident_f.r(), bk_.r())
                A("dve", I("tensor_copy", out=cst.ap[0:npart, 0:512], in_=bka.ap[0:npart, :]), bka.r(), cst.r())
                A("dve", I("tensor_copy", out=cst.ap[0:npart, 512:1024], in_=bkb.ap[0:npart, :]), bkb.r(), cst.r())
                P.dma("sp", dst_out, cst.ap[0:npart, :], reads=cst.r(), out_final=True)
                P.free(cst)
            for t_ in csb + t1b + t2b + [ubuf, us, uo_p, uo_s, scT, sct]:
                P.free(t_)
            dump("aT", aT, [128, 8, TR], BF16)
            phase_end("p2a")

            for qp in range(4):
                wk = wload(w_in, 0, KC, 3072 + qp * 256, 256)
                for ml in range(2):
                    m = qp * 2 + ml
                    for gi, (e0, n, xr) in enumerate(GROUPS):
                        bk = next_bank()
                        regs = fm_mm(bk, 0, wk, ml * 128, 128, xn_acts(e0, n, xr), n)
                        A("act", I("activation", out=qT.ap[:, m, e0 - 128:e0 - 128 + n], in_=bk.ap[:, 0:n], func=AF.Identity, scale=0.125),
                          regs, qT.r(m * 3 + gi))
            EG = [(0, 512, [0, 1, 2, 3]), (512, 512, [4, 5, 6, 7]), (1024, 256, [8, 9])]
            wk = wload(w_in, 0, KC, 4096, 256)
            for c in range(2):
                for gi, (e0, n, xr) in enumerate(EG):
                    bk = next_bank()
                    regs = fm_mm(bk, 0, wk, c * 128, 128, xn_acts(e0, n, xr), n)
                    ra, rb = kT.r((2 * c) * 3 + gi), kT.r((2 * c + 1) * 3 + gi)
                    A("dve", I("tensor_copy", out=kT.ap[0:64, 2 * c, e0:e0 + n], in_=bk.ap[0:64, 0:n]), regs, ra)
                    A("act", I("activation", out=kT.ap[64:128, 2 * c + 1, e0:e0 + n], in_=bk.ap[64:128, 0:n], func=AF.Identity), regs, rb)
                    P.dma("sp", kT.ap[64:128, 2 * c, e0:e0 + n], kT.ap[0:64, 2 * c, e0:e0 + n], reads=ra, writes=ra)
                    P.dma("sp", kT.ap[0:64, 2 * c + 1, e0:e0 + n], kT.ap[64:128, 2 * c + 1, e0:e0 + n], reads=rb, writes=rb)
            phase_end("p2b")
            wk = wload(w_in, 0, 8, 4096, 512) + wload(w_in, 1024, 8, 4096, 512)
            kvo = [P.tile([512], F32, lo=L2) for _ in range(2)]
            for i in range(10):
                bk = next_bank()
                c_lo = 0 if i >= 8 else 256
                for kc in range(KC):
                    A("pe", I("matmul", bk.ap[:, c_lo:512], lhsT=xnT.ap[:, kc, i * 128:(i + 1) * 128], rhs=wk[kc][0][:, c_lo:512],
                              start=(kc == 0), stop=(kc == KC - 1)), wk[kc][1] + xnT.r(i), bk.r())
                if i == 0 and stop_after == "p2c_mm0":
                    A("sp", None, bk.r(), ())
                    phase_end("p2c_mm0")
                A("act", I("activation", out=v_tm.ap[:, i, :], in_=bk.ap[:, 256:512], func=AF.Identity), bk.r(2, 3), v_tm.r(i))
                if i == 0:
                    phase_end("p2c_i0")
                if i == 7:
                    phase_end("p2c_i7")
                if i == 8:
                    ko = kvo[0]
                    A("dve", I("tensor_copy", out=ko.ap, in_=bk.ap), bk.r(), ko.r())
                    P.dma("sp", knp_out, ko.ap[:, 0:256], reads=ko.r(), out_final=True)
                    P.dma("sp", vnp_out, ko.ap[:, 256:512], reads=ko.r(), out_final=True)
                if i == 9:
                    ko = kvo[1]
                    A("dve", I("tensor_copy", out=ko.ap, in_=bk.ap), bk.r(), ko.r())
                    import os
                    for s in range(NSEQ if not os.environ.get("SKIP_SMALL") else 0):
                        P.dma("sp", ks_out[s, 120:128, :], ko.ap[s * 8:(s + 1) * 8, 0:256], reads=ko.r(), out_final=True)
                        P.dma("sp", vs_out[s, 120:128, :], ko.ap[s * 8:(s + 1) * 8, 256:512], reads=ko.r(), out_final=True)
            phase_end("p2c0")
            P.dma("sp", ks_out[:, 0:120, :], ck[:, 8:128, :], out_final=True)
            P.dma("sp", vs_out[:, 0:120, :], cv[:, 8:128, :], out_final=True)
            for t_ in kvo:
                P.free(t_)
            dump("qT", qT, [128, 8, TR], BF16)
            dump("kT", kT, [128, 4, EXT], BF16)
            dump("v_tm", v_tm, [128, 10, 256], BF16)
            phase_end("p2")

            def load_pair(pair):
                wgc = wload(w_in, 0, KC, 4608 + pair * 256, 256)
                wga = wload(w_in, 0, KC, 6656 + pair * 256, 256)
                t = wslot()
                vy = t.ap.rearrange("p (k n) -> p k n", k=KC)
                P.dma("pool", vy[:, 0:8, :], w_co[:, pair * 256:(pair + 1) * 256].rearrange("(k p) n -> p k n", p=128), writes=t.r(0, 2))
                P.dma("pool", vy[:, 8:16, :], w_ao[:, pair * 256:(pair + 1) * 256].rearrange("(k p) n -> p k n", p=128), writes=t.r(1, 3))
                wyc = [(vy[:, k, :], t.r(0, 2)) for k in range(8)]
                wya = [(vy[:, 8 + k, :], t.r(1, 3)) for k in range(8)]
                return wgc, wga, wyc, wya

            pair_w = {0: load_pair(0), 1: load_pair(1)}

            attnT = P.tile([8, TR], BF16, nreg=72, at=KB(108))
            NSB = 4
            L3 = KB(126)
            sbs = [P.tile([257], F32, lo=L3) for _ in range(NSB)]
            pbs = [P.tile([257], BF16, lo=L3) for _ in range(NSB)]
            pts = [P.tile([256], BF16, lo=L3) for _ in range(NSB)]
            atm = [P.tile([8, 128], BF16, nreg=8, lo=L3) for _ in range(2)]
            tiles = [(2 * m + p, blk) for m in range(8) for p in range(2) for blk in range(1, 9)]
            NT = len(tiles)

            def tp_(k):
                h, blk = tiles[k]
                return h, blk, h // 4, h // 2, h % 2, k % NSB, k % 2

            def stA(k):
                h, blk, kvh, m, p, sl, pslot = tp_(k)
                sbank = banks[pslot]
                ps = slice(p * 64, (p + 1) * 64)
                kc0 = (blk - 1) * 128
                g0 = kc0 // 512 if kc0 < 1024 else 2
                g1 = (kc0 + 255) // 512 if kc0 + 255 < 1024 else 2
                kregs = kT.r(*sorted({kvh * 3 + g0, kvh * 3 + g1}))
                gq = 0 if blk <= 4 else 1
                A("pe", I("matmul", sbank.ap[:, 0:256], lhsT=qT.ap[ps, m, (blk - 1) * 128:blk * 128],
                          rhs=kT.ap[ps, kvh, kc0:kc0 + 256], start=True, stop=True), qT.r(m * 3 + gq) + kregs, sbank.r())

            def stB1(k):
                h, blk, kvh, m, p, sl, pslot = tp_(k)
                sbank = banks[pslot]
                sb_ = sbs[sl]
                A("dve", I("tensor_tensor", out=sb_.ap[:, 0:256], in0=sbank.ap[:, 0:256], in1=bias_p.ap[:, h, 0:256], op=ALU.add),
                  sbank.r() + bias_p.r(), sb_.r())
                if blk == 1:
                    A("dve", I("tensor_scalar", out=sb_.ap[:, 0:128], in0=sb_.ap[:, 0:128], scalar1=hmask.ap[:, 0:1], scalar2=None,
                               op0=ALU.add), sb_.r() + hmask.r(), sb_.r())
                if k % 8 < NSB:
                    A("dve", I("tensor_copy", out=sb_.ap[:, 256:257], in_=sink_bc.ap[:, h:h + 1]), sink_bc.r() + sb_.r(), sb_.r())

            def stB2(k):
                h, blk, kvh, m, p, sl, pslot = tp_(k)
                sb_, pb_ = sbs[sl], pbs[sl]
                mx, mxr = stat.ap[:, 32 + sl:33 + sl], stat.r(32 + sl)
                rs, rsr = stat.ap[:, 40 + sl:41 + sl], stat.r(40 + sl)
                A("dve", I("tensor_reduce", out=mx, in_=sb_.ap, axis=AX.X, op=ALU.max, negate=True), sb_.r(), mxr)
                A("act", I("activation", out=pb_.ap, in_=sb_.ap, func=AF.Exp, bias=mx, accum_out=rs), sb_.r() + mxr, pb_.r() + rsr)

            def stC(k):
                h, blk, kvh, m, p, sl, pslot = tp_(k)
                pb_ = pbs[sl]
                tb = banks[2 + pslot]
                tq = tb.ap.bitcast(BF16)[:, 0:256]
                for j in range(2):
                    A("pe", I("transpose", out=tq[:, j * 128:(j + 1) * 128], in_=pb_.ap[:, j * 128:(j + 1) * 128], identity=ident_b.ap),
                      pb_.r() + ident_b.r(), tb.r())

            def stD(k):
                h, blk, kvh, m, p, sl, pslot = tp_(k)
                tb = banks[2 + pslot]
                A("act", I("activation", out=pts[sl].ap, in_=tb.ap.bitcast(BF16)[:, 0:256], func=AF.Identity), tb.r(), pts[sl].r())

            def stE(k):
                h, blk, kvh, m, p, sl, pslot = tp_(k)
                pt_ = pts[sl]
                ob = banks[4 + pslot]
                oq = ob.ap[:, 0:64]
                A("pe", I("matmul", oq, lhsT=pt_.ap[:, 0:128], rhs=v_tm.ap[:, blk - 1, kvh * 64:(kvh + 1) * 64], start=True, stop=False),
                  pt_.r() + v_tm.r(blk - 1), ob.r())
                A("pe", I("matmul", oq, lhsT=pt_.ap[:, 128:256], rhs=v_tm.ap[:, blk, kvh * 64:(kvh + 1) * 64], start=False, stop=True),
                  pt_.r() + v_tm.r(blk), ob.r())

            def stF(k):
                h, blk, kvh, m, p, sl, pslot = tp_(k)
                rs, rsr = stat.ap[:, 40 + sl:41 + sl], stat.r(40 + sl)
                ob = banks[4 + pslot]
                at_ = atm[m % 2]
                A("dve", I("reciprocal", out=rs, in_=rs), rsr, rsr)
                A("act", I("activation", out=at_.ap[:, blk - 1, p * 64:(p + 1) * 64], in_=ob.ap[:, 0:64], func=AF.Identity, scale=rs),
                  ob.r() + rsr, at_.r(blk - 1))
                if k % 16 == 15:
                    tb = banks[6 + m % 2]
                    for b_ in range(8):
                        A("pe", I("transpose", out=tb.ap.bitcast(BF16)[:, b_ * 128:(b_ + 1) * 128], in_=at_.ap[:, b_, :],
                                  identity=ident_b.ap), at_.r(b_) + ident_b.r(), tb.r())
                    A("act", I("activation", out=attnT.ap[:, m, 0:TP], in_=tb.ap.bitcast(BF16), func=AF.Identity),
                      tb.r(), attnT.r(*[m * 9 + i for i in range(8)]))

            for i in range(NT + 4):
                if i < NT:
                    stA(i)
                if 0 <= i - 1 < NT:
                    stB1(i - 1)
                if 0 <= i - 2 < NT:
                    stB2(i - 2)
                if 0 <= i - 3 < NT:
                    stC(i - 3)
                    stD(i - 3)
                if 0 <= i - 4 < NT:
                    stE(i - 4)
                    stF(i - 4)
            for t_ in sbs + pbs + pts + atm + [bias_p]:
                P.free(t_)
            dump("attnT_p", attnT, [128, 8, TR], BF16)
            phase_end("p3a")

            L3b = KB(91)
            NB4 = 4
            ckb = [P.tile([4, 128], BF16, nreg=2, lo=L3b) for _ in range(NB4)]
            vsb = [P.tile([256], BF16, lo=L3b) for _ in range(NB4)]
            kts = [P.tile([4, 137], BF16, nreg=2, lo=L3b) for _ in range(NB4)]
            st1 = [P.tile([128], F32, nreg=2, lo=L3b) for _ in range(NB4)]
            st2 = [P.tile([128], F32, nreg=2, lo=L3b) for _ in range(NB4)]
            sbq = [P.tile([137], F32, lo=L3b) for _ in range(NB4)]
            pq = [P.tile([129], BF16, lo=L3b) for _ in range(NB4)]
            pn = P.tile([NSEQ, 128], BF16, nreg=NSEQ, lo=L3b)
            ptq = [P.tile([256], BF16, lo=L3b) for _ in range(NB4)]
            asd = [P.tile([128], BF16, nreg=4, lo=L3b) for _ in range(NB4)]
            osb = [P.tile([256], F32, lo=L3b) for _ in range(NB4)]
            sstat = P.tile([NB4, 4], F32, nreg=NB4 * 4, lo=L3b)
            A("dve", I("memset", pn.ap, 0.0), (), pn.r())
            tb = banks[2]
            b5, b6, b7 = banks[5], banks[6], banks[7]
            A("dve", I("memset", b7.ap[:, 128:129], 0.0), (), b7.r())

            def hv(ap):
                return ap.rearrange("p (m q t) -> p m q t", m=8, q=2)

            def sst(b4, j):
                return sstat.ap[:, b4, j:j + 1], sstat.r(b4 * 4 + j)

            tb3, obk, tb1 = banks[3], banks[4], banks[1]

            def X1a(s):
                b4 = s % NB4
                ckv = ckb[b4].ap.rearrange("p h (u d) -> p h u d", u=2)
                for u in range(2):
                    P.dma("pool", ckv[:, :, u, :], ck[s].rearrange("p (h d) -> p h d", h=4), writes=ckb[b4].r(u))
                P.dma("pool", vsb[b4].ap, cv[s], writes=vsb[b4].r())
                for kvh in range(4):
                    A("pe", I("transpose", out=tb.ap.bitcast(BF16)[:, kvh * 128:(kvh + 1) * 128], in_=ckb[b4].ap[:, kvh, :],
                              identity=ident_b.ap), ckb[b4].r() + ident_b.r(), tb.r())

            def X1b(s):
                b4 = s % NB4
                A("act", I("activation", out=kts[b4].ap[:, :, 0:128],
                           in_=tb.ap.bitcast(BF16)[:, 0:512].rearrange("p (h k) -> p h k", h=4), func=AF.Identity), tb.r(), kts[b4].r(0))
                A("dve", I("tensor_copy", out=kts[b4].ap[:, :, 129:137], in_=kT.ap[:, :, 1152 + s * 8:1152 + s * 8 + 8]),
                  kT.r(2, 5, 8, 11), kts[b4].r(1))

            def hv5(ap):
                return ap.rearrange("p (k g q t) -> p k g q t", k=4, g=2, q=2)

            def X2a(s):
                b4 = s % NB4
                for c0, kc in ((0, slice(0, 128)), (128, slice(129, 137))):
                    npart = 128 if c0 == 0 else 8
                    for kvh in range(4):
                        for p in range(2):
                            ps = slice(p * 64, (p + 1) * 64)
                            bq = b6 if p == 0 else b5
                            A("pe", I("matmul", hv5(bq.ap[0:npart, c0:c0 + 128])[:, kvh, :, p, :], lhsT=kts[b4].ap[ps, kvh, kc],
                                      rhs=qT.ap[ps, 2 * kvh:2 * kvh + 2, TP + s * 8:TP + s * 8 + 8], start=True, stop=True),
                              kts[b4].r() + qT.r((2 * kvh) * 3 + 2, (2 * kvh + 1) * 3 + 2), bq.r())

            def X2b(s):
                b4 = s % NB4
                A("act", I("activation", out=hv(st1[b4].ap)[:, :, 0, :], in_=hv(b6.ap[:, 0:128])[:, :, 0, :], func=AF.Identity),
                  b6.r(), st1[b4].r(0))
                A("dve", I("tensor_copy", out=hv(st1[b4].ap)[:, :, 1, :], in_=hv(b5.ap[:, 0:128])[:, :, 1, :]),
                  b5.r(), st1[b4].r(1))
                A("act", I("activation", out=hv(st2[b4].ap[0:8])[:, :, 0, :], in_=hv(b6.ap[0:8, 128:256])[:, :, 0, :], func=AF.Identity),
                  b6.r(), st2[b4].r(0))
                A("dve", I("tensor_copy", out=hv(st2[b4].ap[0:8])[:, :, 1, :], in_=hv(b5.ap[0:8, 128:256])[:, :, 1, :]),
                  b5.r(), st2[b4].r(1))

            def X3a(s):
                b4 = s % NB4
                A("pe", I("transpose", out=b7.ap[:, 0:128], in_=st1[b4].ap, identity=ident_f.ap), st1[b4].r() + ident_f.r(), b7.r())
                A("pe", I("transpose", out=b7.ap[:, 129:137], in_=st2[b4].ap[0:8, :], identity=ident_f.ap[0:8, 0:8]),
                  st2[b4].r() + ident_f.r(), b7.r())

            def X3b1(s):
                b4 = s % NB4
                (mx, mxr) = sst(b4, 0)
                A("dve", I("tensor_tensor", out=sbq[b4].ap, in0=b7.ap[:, 0:137], in1=bias_s.ap, op=ALU.add), b7.r() + bias_s.r(), sbq[b4].r())
                A("dve", I("tensor_reduce", out=mx, in_=sbq[b4].ap, axis=AX.X, op=ALU.max, negate=True), sbq[b4].r(), mxr)

            def X3b2(s):
                b4 = s % NB4
                (mx, mxr), (r1, r1r), (r2, r2r) = sst(b4, 0), sst(b4, 1), sst(b4, 2)
                A("act", I("activation", out=pq[b4].ap, in_=sbq[b4].ap[:, 0:129], func=AF.Exp, bias=mx, accum_out=r1),
                  sbq[b4].r() + mxr, pq[b4].r() + r1r)
                A("act", I("activation", out=pn.ap[:, s, s * 8:s * 8 + 8], in_=sbq[b4].ap[:, 129:137], func=AF.Exp, bias=mx, accum_out=r2),
                  sbq[b4].r() + mxr, pn.r(s) + r2r)

            def Y1a(s):
                b4 = s % NB4
                tq = tb3.ap.bitcast(BF16)[:, 0:256]
                A("pe", I("transpose", out=tq[:, 0:128], in_=pq[b4].ap[:, 0:128], identity=ident_b.ap), pq[b4].r() + ident_b.r(), tb3.r())
                A("pe", I("transpose", out=tq[:, 128:256], in_=pn.ap[:, s, :], identity=ident_b.ap), pn.r(s) + ident_b.r(), tb3.r())

            def Y1b1(s):
                b4 = s % NB4
                A("act", I("activation", out=ptq[b4].ap, in_=tb3.ap.bitcast(BF16)[:, 0:256], func=AF.Identity), tb3.r(), ptq[b4].r())

            def Y1b2(s):
                b4 = s % NB4
                A("pe", I("matmul", obk.ap[:, 0:256], lhsT=ptq[b4].ap[:, 0:128], rhs=vsb[b4].ap, start=True, stop=False),
                  ptq[b4].r() + vsb[b4].r(), obk.r())
                A("pe", I("matmul", obk.ap[:, 0:256], lhsT=ptq[b4].ap[:, 128:256], rhs=v_tm.ap[:, 9, :], start=False, stop=True),
                  ptq[b4].r() + v_tm.r(9), obk.r())

            def Y2a(s):
                b4 = s % NB4
                (r1, r1r), (r2, r2r) = sst(b4, 1), sst(b4, 2)
                A("act", I("activation", out=osb[b4].ap, in_=obk.ap[:, 0:256], func=AF.Identity), obk.r(), osb[b4].r())
                A("dve", I("tensor_tensor", out=r1, in0=r1, in1=r2, op=ALU.add), r1r + r2r, r1r)
                A("dve", I("reciprocal", out=r1, in_=r1), r1r, r1r)

            def Y2b1(s):
                b4 = s % NB4
                (r1, r1r) = sst(b4, 1)
                asv = asd[b4].ap.rearrange("p (u d) -> p u d", u=2)
                for kvh in range(4):
                    pr = slice(kvh * 32, (kvh + 1) * 32)
                    src = osb[b4].ap[pr, kvh * 64:(kvh + 1) * 64]
                    srcb = bass.AP(src.tensor, src.offset, [list(src.ap[0]), [0, 2], list(src.ap[1])])
                    if kvh % 2 == 0:
                        A("dve", I("tensor_scalar", out=asv[pr, :, :], in0=srcb, scalar1=r1[pr, :], scalar2=None, op0=ALU.mult),
                          osb[b4].r() + r1r, asd[b4].r(kvh))
                    else:
                        A("act", I("activation", out=asv[pr, :, :], in_=srcb, func=AF.Identity, scale=r1[pr, :]),
                          osb[b4].r() + r1r, asd[b4].r(kvh))

            def Y2b2(s):
                b4 = s % NB4
                A("pe", I("transpose", out=tb1.ap.bitcast(BF16)[:, 0:128], in_=asd[b4].ap, identity=ident_b.ap),
                  asd[b4].r() + ident_b.r(), tb1.r())

            def Y2b3(s):
                tq2 = tb1.ap.bitcast(BF16)[:, 0:128]
                for p in range(2):
                    pr = slice(p * 64, (p + 1) * 64)
                    src = tq2.rearrange("p (m q t) -> p m q t", m=8, q=2)[pr, :, p, :]
                    A("dve", I("tensor_copy", out=attnT.ap[pr, :, TP + s * 8:TP + s * 8 + 8], in_=src),
                      tb1.r() + attnT.r(*[m * 9 + 8 for m in range(8)]), attnT.r(*[m * 9 + 8 for m in range(8)]))

            sched = [(4, Y2a), (3, Y1a), (2, X3a), (1, X2a), (0, X1a),
                     (4, Y2b1), (3, Y1b1), (2, X3b1), (1, X2b), (0, X1b),
                     (4, Y2b2), (3, Y1b2), (2, X3b2),
                     (4, Y2b3)]
            for i in range(NSEQ + 4):
                for d_, fn_ in sched:
                    if 0 <= i - d_ < NSEQ:
                        fn_(i - d_)
            for t_ in ckb + vsb + kts + st1 + st2 + sbq + pq + [pn, sstat] + ptq + asd + osb:
                P.free(t_)
            P.free(qT)
            P.free(kT)
            P.free(v_tm)
            dump("attnT", attnT, [128, 8, TR], BF16)
            phase_end("p3")

            mergedT = P.tile([KC, TR], BF16, nreg=48, at=KB(72))
            sg = [P.tile([512], F32, lo=KB(126)) for _ in range(4)]
            mm = [P.tile([512], F32, lo=KB(126)) for _ in range(4)]
            it = 0
            for pair in range(8):
                wgc, wga, wyc, wya = pair_w[pair] if pair in pair_w else load_pair(pair)
                for ml in range(2):
                    c = pair * 2 + ml
                    for gi, (r0, n, tr_) in enumerate(RG):
                        e0 = r0 + 128
                        xr = [x + 1 for x in tr_]
                        bs = [banks[(it % 2) * 4 + j] for j in range(4)]
                        rg0 = fm_mm(bs[0], 0, wgc, ml * 128, 128, xn_acts(e0, n, xr), n)
                        rg1 = fm_mm(bs[1], 0, wga, ml * 128, 128, xn_acts(e0, n, xr), n)
                        rg2 = fm_mm(bs[2], 0, wyc, ml * 128, 128, [(aT.ap[:, k, r0:r0 + n], aT.r(k * 3 + gi)) for k in range(8)], n)
                        rg3 = fm_mm(bs[3], 0, wya, ml * 128, 128,
                                    [(attnT.ap[:, k, r0:r0 + n], attnT.r(*[k * 9 + x for x in tr_])) for k in range(8)], n)
                        s0, s1, m0, m1 = sg[(it % 2) * 2], sg[(it % 2) * 2 + 1], mm[(it % 2) * 2], mm[(it % 2) * 2 + 1]
                        it += 1
                        A("act", I("activation", out=s0.ap[:, 0:n], in_=bs[0].ap[:, 0:n], func=AF.Sigmoid), rg0, s0.r())
                        A("act", I("activation", out=s1.ap[:, 0:n], in_=bs[1].ap[:, 0:n], func=AF.Sigmoid), rg1, s1.r())
                        A("dve", I("tensor_tensor", out=m0.ap[:, 0:n], in0=s0.ap[:, 0:n], in1=bs[2].ap[:, 0:n], op=ALU.mult), s0.r() + rg2, m0.r())
                        A("dve", I("tensor_tensor", out=m1.ap[:, 0:n], in0=s1.ap[:, 0:n], in1=bs[3].ap[:, 0:n], op=ALU.mult), s1.r() + rg3, m1.r())
                        A("dve", I("tensor_tensor", out=mergedT.ap[:, c, r0:r0 + n], in0=m0.ap[:, 0:n], in1=m1.ap[:, 0:n], op=ALU.add),
                          m0.r() + m1.r(), mergedT.r(c * 3 + gi))
            for t_ in sg + mm:
                P.free(t_)
            P.free(aT)
            P.free(attnT)
            P.free(xnT)
            dump("mergedT", mergedT, [128, KC, TR], BF16)
            phase_end("p4a")

            h_acc = P.tile([9, D], F32, nreg=36, at=KB(0))
            for i in range(9):
                P.dma("sp", h_acc.ap[:, i, :], x_ext[128 + i * 128:128 + (i + 1) * 128, :], writes=h_acc.r(*[i * 4 + n for n in range(4)]))
            for nb in range(4):
                wk = wload(w_o, 0, 8, nb * 512, 512) + wload(w_o, 1024, 8, nb * 512, 512)
                for i in range(9):
                    bk = next_bank()
                    gi = 0 if i < 4 else (1 if i < 8 else 2)
                    for kc in range(KC):
                        A("pe", I("matmul", bk.ap, lhsT=mergedT.ap[:, kc, i * 128:(i + 1) * 128], rhs=wk[kc][0],
                                  start=(kc == 0), stop=(kc == KC - 1)), wk[kc][1] + mergedT.r(kc * 3 + gi), bk.r())
                    hs = h_acc.ap[:, i, nb * 512:(nb + 1) * 512]
                    A("dve", I("tensor_tensor", out=hs, in0=hs, in1=bk.ap, op=ALU.add), bk.r() + h_acc.r(i * 4 + nb), h_acc.r(i * 4 + nb))
            P.free(mergedT)
            dump("h1", h_acc, [128, 9, D])
            phase_end("p4b")

            P.dma("sp", g_bc.ap, bcast_rows(g_ffn, 128), writes=g_bc.r())
            hnT = P.tile([KC, TR], BF16, nreg=9, at=KB(72))

            def src2(i):
                return h_acc.ap[:, i, :], h_acc.r(*[i * 4 + n for n in range(4)])

            norm_transpose(src2, 9, hnT, 16, KB(108))
            dump("hnT", hnT, [128, KC, TR], BF16)
            guT = [P.tile([4, TR], BF16, nreg=12, lo=KB(108)) for _ in range(2)]
            sgl = [P.tile([512], F32, lo=KB(108)) for _ in range(2)]
            it = 0
            for j in range(DFF // 512):
                gu = guT[j % 2]
                for hp in range(2):
                    wg = wload(w_g, 0, KC, j * 512 + hp * 256, 256)
                    wu = wload(w_u, 0, KC, j * 512 + hp * 256, 256)
                    for ml in range(2):
                        mi = hp * 2 + ml
                        for gi, (r0, n, tr_) in enumerate(RG):
                            bg, bu = banks[(it % 3) * 2], banks[(it % 3) * 2 + 1]
                            acts = [(hnT.ap[:, k, r0:r0 + n], hnT.r(*tr_)) for k in range(KC)]
                            rg = fm_mm(bg, 0, wg, ml * 128, 128, acts, n)
                            ru = fm_mm(bu, 0, wu, ml * 128, 128, acts, n)
                            s_ = sgl[it % 2]
                            it += 1
                            A("act", I("activation", out=s_.ap[:, 0:n], in_=bg.ap[:, 0:n], func=AF.Silu), rg, s_.r())
                            A("dve", I("tensor_tensor", out=gu.ap[:, mi, r0:r0 + n], in0=s_.ap[:, 0:n], in1=bu.ap[:, 0:n], op=ALU.mult),
                              s_.r() + ru, gu.r(mi * 3 + gi))
                for nb in range(4):
                    wd = wload(w_d, j * 512, 4, nb * 512, 512)
                    for i in range(9):
                        bk = banks[6 + (i + nb) % 2]
                        gi = 0 if i < 4 else (1 if i < 8 else 2)
                        for kc in range(4):
                            A("pe", I("matmul", bk.ap, lhsT=gu.ap[:, kc, i * 128:(i + 1) * 128], rhs=wd[kc][0],
                                      start=(kc == 0), stop=(kc == 3)), wd[kc][1] + gu.r(kc * 3 + gi), bk.r())
                        hs = h_acc.ap[:, i, nb * 512:(nb + 1) * 512]
                        A("dve", I("tensor_tensor", out=hs, in0=hs, in1=bk.ap, op=ALU.add), bk.r() + h_acc.r(i * 4 + nb), h_acc.r(i * 4 + nb))
            for t_ in guT + sgl:
                P.free(t_)
            P.free(hnT)
            dump("h2", h_acc, [128, 9, D])
            phase_end("p5")

            P.dma("sp", g_bc.ap, bcast_rows(g_fin, 128), writes=g_bc.r())
            yt = [P.tile([D], F32, lo=KB(72)) for _ in range(2)]
            junk = P.tile([D], BF16, lo=KB(72))

            def F1(i):
                hap = h_acc.ap[:, i, :]
                hr = h_acc.r(*[i * 4 + n for n in range(4)])
                ss = stat.ap[:, 54 + i:55 + i]
                ssr = stat.r(54 + i)
                A("act", I("activation", out=junk.ap, in_=hap, func=AF.Square, accum_out=ss), hr, junk.r() + ssr)
                A("act", I("activation", out=ss, in_=ss, func=AF.Sqrt, scale=1.0 / D, bias=EPS), ssr, ssr)
                A("dve", I("reciprocal", out=ss, in_=ss), ssr, ssr)

            def F2(i):
                hap = h_acc.ap[:, i, :]
                hr = h_acc.r(*[i * 4 + n for n in range(4)])
                ss = stat.ap[:, 54 + i:55 + i]
                ssr = stat.r(54 + i)
                y_ = yt[i % 2]
                A("dve", I("scalar_tensor_tensor", out=y_.ap, in0=hap, scalar=ss, in1=g_bc.ap, op0=ALU.mult, op1=ALU.mult),
                  hr + ssr + g_bc.r(), y_.r())
                P.dma("sp", y_out[i * 128:(i + 1) * 128, :], y_.ap, reads=y_.r(), out_final=True)

            F1(0)
            for i in range(9):
                if i + 1 < 9:
                    F1(i + 1)
                F2(i)

        try:
            body()
        except Stop:
            pass
        P.finalize(stack)
        print("sbuf peak bytes/partition:", P.arena.peak, "ops:", len(P.ops), flush=True)
    return nc


_NC_CACHE = {}


def host_inputs(x_prompt, x_sample, cache_k, cache_v, state_conv, rel_bias, w_in, w_conv, w_conv_out,
                sinks, w_attn_out, w_o, g_mix, g_ffn, w_gate, w_up, w_down, g_final, cores=range(NCORES)):
    f = lambda a: np.ascontiguousarray(np.asarray(a, dtype=np.float32))
    T, Trev = _bucket_tables()
    rel_ext = np.concatenate([f(rel_bias), np.full((1, 16), NEG, np.float32)], axis=0)
    hsel = np.zeros((128, 16), np.float32)
    hsel[np.arange(128), np.arange(128) // 8] = 1.0
    shared = {
        "w_in": f(w_in[0]), "w_conv": f(w_conv[0]), "w_co": f(w_conv_out[0]), "w_ao": f(w_attn_out[0]), "w_o": f(w_o[0]),
        "w_g": f(w_gate[0]), "w_u": f(w_up[0]), "w_d": f(w_down[0]),
        "g_mix": f(g_mix[0]).reshape(1, D), "g_ffn": f(g_ffn[0]).reshape(1, D), "g_fin": f(g_final).reshape(1, D),
        "rel_ext": rel_ext, "sinks": f(sinks[0]).reshape(1, 16), "ttab": T, "trev": Trev, "hsel": hsel,
        "ident": np.eye(128, dtype=np.float32),
    }
    xp = np.asarray(x_prompt, dtype=np.float32)
    xs = np.asarray(x_sample, dtype=np.float32)
    maps = []
    for c in cores:
        b, half = c // 2, c % 2
        x_ext = np.zeros((EXT, D), np.float32)
        if half == 1:
            x_ext[0:128] = xp[b, TP - 128:TP]
        x_ext[128:128 + TP] = xp[b, half * TP:(half + 1) * TP]
        x_ext[128 + TP:] = xs[c * NSEQ:(c + 1) * NSEQ].reshape(TS, D)
        m = dict(shared)
        m["x_ext"] = x_ext
        m["ck"] = f(cache_k[0, c * NSEQ:(c + 1) * NSEQ]).reshape(NSEQ, 128, 256)
        m["cv"] = f(cache_v[0, c * NSEQ:(c + 1) * NSEQ]).reshape(NSEQ, 128, 256)
        m["sc"] = f(state_conv[0, c * NSEQ:(c + 1) * NSEQ]).reshape(NSEQ * 2, 1024)
        m["hmask"] = np.full((128, 1), 0.0 if half == 1 else NEG, np.float32)
        maps.append(m)
    return maps


def kernel(x_prompt, x_sample, cache_k, cache_v, state_conv, rel_bias, w_in, w_conv, w_conv_out,
           sinks, w_attn_out, w_o, g_mix, g_ffn, w_gate, w_up, w_down, g_final):
    if "nc" not in _NC_CACHE:
        _NC_CACHE["nc"] = build()
    nc = _NC_CACHE["nc"]
    maps = host_inputs(x_prompt, x_sample, cache_k, cache_v, state_conv, rel_bias, w_in, w_conv, w_conv_out,
                       sinks, w_attn_out, w_o, g_mix, g_ffn, w_gate, w_up, w_down, g_final)
    res = run_bass_kernel_spmd(nc, maps, core_ids=list(range(NCORES)))
    R = res.results
    B = 4
    y_prompt = np.zeros((B, 2048, D), np.float32)
    y_sample = np.zeros((128, 8, D), np.float32)
    nkp = np.zeros((1, B, 128, 4, 64), np.float32)
    nvp = np.zeros((1, B, 128, 4, 64), np.float32)
    ncp = np.zeros((1, B, 2, 1024), np.float32)
    nks = np.zeros((1, 128, 128, 4, 64), np.float32)
    nvs = np.zeros((1, 128, 128, 4, 64), np.float32)
    ncs = np.zeros((1, 128, 2, 1024), np.float32)
    for c in range(NCORES):
        b, half = c // 2, c % 2
        r = R[c]
        y = np.asarray(r["y"])
        y_prompt[b, half * TP:(half + 1) * TP] = y[0:TP]
        y_sample[c * NSEQ:(c + 1) * NSEQ] = y[TP:].reshape(NSEQ, 8, D)
        if half == 1:
            nkp[0, b] = np.asarray(r["knp"]).reshape(128, 4, 64)
            nvp[0, b] = np.asarray(r["vnp"]).reshape(128, 4, 64)
            ncp[0, b] = np.asarray(r["cnp"])
        nks[0, c * NSEQ:(c + 1) * NSEQ] = np.asarray(r["ks"]).reshape(NSEQ, 128, 4, 64)
        nvs[0, c * NSEQ:(c + 1) * NSEQ] = np.asarray(r["vs"]).reshape(NSEQ, 128, 4, 64)
        ncs[0, c * NSEQ:(c + 1) * NSEQ] = np.asarray(r["cns"]).reshape(NSEQ, 2, 1024)
    return (y_prompt, y_sample, nkp, nvp, ncp, nks, nvs, ncs)
```
